# Optimizing a Trainium2 kernel written in Bass

```python
import jax, jax.numpy as jnp
from jax import lax
import numpy as np

D_MODEL = 1024
BATCH = 8
SEQ = 8192
DEPTH = 2

GRID_W = 64
NA_HEADS = 8
NA_HEAD_DIM = 64
NA_WIN_R = 8
NA_WIN_C = 16
NA_W = NA_HEADS * NA_HEAD_DIM
MLA_HEADS = 4
MLA_Q_RANK = 384
MLA_KV_RANK = 256
MLA_NOPE = 128
MLA_ROPE = 64
MLA_V = 128
MLA_QK = MLA_NOPE + MLA_ROPE
MLA_W = MLA_HEADS * MLA_V
GQA_HEADS = 8
GQA_KV_HEADS = 2
GQA_HEAD_DIM = 64
GQA_WINDOW = 128
GQA_Q_W = GQA_HEADS * GQA_HEAD_DIM
GQA_KV_W = GQA_KV_HEADS * GQA_HEAD_DIM
BLOCK = 128
N_BRANCH = 3
D_FF = 2816
CONV_W = 3
ROPE_THETA = 10000.0
EPS = 1e-6
NEG_INF = -1e30
IN_SPLITS = (3 * NA_W, MLA_Q_RANK, MLA_KV_RANK, MLA_ROPE, GQA_Q_W, 2 * GQA_KV_W, N_BRANCH * D_MODEL)
IN_W = 3 * NA_W + MLA_Q_RANK + MLA_KV_RANK + MLA_ROPE + GQA_Q_W + 2 * GQA_KV_W + N_BRANCH * D_MODEL

kernel_name = "hybrid_gated_na_mla_swa_convffn"


def rmsnorm(x, g):
    xf = x.astype(jnp.float32)
    y = xf * lax.rsqrt(jnp.mean(xf * xf, axis=-1, keepdims=True) + EPS)
    return (y * g.astype(jnp.float32)).astype(x.dtype)


def rope_tables(seq, dim):
    inv = 1.0 / (ROPE_THETA ** (jnp.arange(0, dim, 2, dtype=jnp.float32) / dim))
    ang = jnp.arange(seq, dtype=jnp.float32)[:, None] * inv[None, :]
    return jnp.cos(ang), jnp.sin(ang)


def apply_rope(x, cos, sin):
    x1, x2 = jnp.split(x, 2, axis=-1)
    c = cos[None, :, None, :].astype(x.dtype)
    s = sin[None, :, None, :].astype(x.dtype)
    return jnp.concatenate([x1 * c - x2 * s, x1 * s + x2 * c], axis=-1)


def neighbourhood_attention(q, k, v, rpb):
    B, S, H, hd = q.shape
    rows = S // GRID_W
    win_r = min(NA_WIN_R, rows)
    qg = q.reshape(B, rows, GRID_W, H, hd)
    kg = k.reshape(B, rows, GRID_W, H, hd)
    vg = v.reshape(B, rows, GRID_W, H, hd)
    col_q = jnp.arange(GRID_W)
    col_start = jnp.clip(col_q - NA_WIN_C // 2, 0, GRID_W - NA_WIN_C)
    col_idx = col_start[:, None] + jnp.arange(NA_WIN_C)[None, :]
    col_bias_idx = col_idx - col_q[:, None] + (NA_WIN_C - 1)
    scale = hd ** -0.5

    def one_row(r):
        r_start = jnp.clip(r - NA_WIN_R // 2, 0, rows - win_r)
        q_r = lax.dynamic_index_in_dim(qg, r, axis=1, keepdims=False)
        k_rows = lax.dynamic_slice_in_dim(kg, r_start, win_r, axis=1)
        v_rows = lax.dynamic_slice_in_dim(vg, r_start, win_r, axis=1)
        k_win = k_rows[:, :, col_idx]
        v_win = v_rows[:, :, col_idx]
        s = jnp.einsum('bqhd,brqchd->bhqrc', q_r, k_win,
                       preferred_element_type=jnp.float32) * scale
        row_bias_idx = r_start + jnp.arange(win_r) - r + (NA_WIN_R - 1)
        bias = rpb[:, row_bias_idx[None, :, None], col_bias_idx[:, None, :]]
        s = s + bias[None].astype(jnp.float32)
        p = jax.nn.softmax(s.reshape(B, H, GRID_W, win_r * NA_WIN_C), axis=-1)
        p = p.reshape(B, H, GRID_W, win_r, NA_WIN_C).astype(v.dtype)
        return jnp.einsum('bhqrc,brqchd->bqhd', p, v_win)

    out = lax.map(one_row, jnp.arange(rows))
    return out.transpose(1, 0, 2, 3, 4).reshape(B, S, H * hd)


def mla_attention(c_q, c_kv, k_rope, g_qa, g_kva, w_uq, w_ukv, cos, sin):
    B, S, _ = c_q.shape
    H = MLA_HEADS
    q = (rmsnorm(c_q, g_qa) @ w_uq).reshape(B, S, H, MLA_QK)
    q_nope = q[..., :MLA_NOPE]
    q_pe = apply_rope(q[..., MLA_NOPE:], cos, sin)
    kv = (rmsnorm(c_kv, g_kva) @ w_ukv).reshape(B, S, H, MLA_NOPE + MLA_V)
    k_nope = kv[..., :MLA_NOPE]
    v = kv[..., MLA_NOPE:]
    k_pe = apply_rope(k_rope[:, :, None, :], cos, sin)[:, :, 0, :]
    scale = MLA_QK ** -0.5
    nb = S // BLOCK
    qn_b = q_nope.reshape(B, nb, BLOCK, H, MLA_NOPE).transpose(1, 0, 2, 3, 4)
    qp_b = q_pe.reshape(B, nb, BLOCK, H, MLA_ROPE).transpose(1, 0, 2, 3, 4)

    def one_block(args):
        qn, qp = args
        s = (jnp.einsum('bqhd,bkhd->bhqk', qn, k_nope, preferred_element_type=jnp.float32)
             + jnp.einsum('bqhr,bkr->bhqk', qp, k_pe, preferred_element_type=jnp.float32)) * scale
        p = jax.nn.softmax(s, axis=-1).astype(v.dtype)
        return jnp.einsum('bhqk,bkhd->bqhd', p, v)

    o = lax.map(one_block, (qn_b, qp_b))
    return o.transpose(1, 0, 2, 3, 4).reshape(B, S, H * MLA_V)


def window_gqa_sink(q, k, v, sink, cos, sin):
    B, S, H, hd = q.shape
    KVH = k.shape[2]
    G = H // KVH
    q = apply_rope(q, cos, sin)
    k = apply_rope(k, cos, sin)
    nb = S // BLOCK
    qb = q.reshape(B, nb, BLOCK, KVH, G, hd)
    pad = ((0, 0), (BLOCK, BLOCK), (0, 0), (0, 0))
    kp = jnp.pad(k, pad).reshape(B, nb + 2, BLOCK, KVH, hd)
    vp = jnp.pad(v, pad).reshape(B, nb + 2, BLOCK, KVH, hd)
    kw = jnp.concatenate([kp[:, :-2], kp[:, 1:-1], kp[:, 2:]], axis=2)
    vw = jnp.concatenate([vp[:, :-2], vp[:, 1:-1], vp[:, 2:]], axis=2)
    scale = hd ** -0.5
    s = jnp.einsum('bnqkgd,bnjkd->bnkgqj', qb, kw,
                   preferred_element_type=jnp.float32) * scale
    blk = jnp.arange(nb)
    qpos = blk[:, None] * BLOCK + jnp.arange(BLOCK)[None, :]
    kpos = (blk[:, None] - 1) * BLOCK + jnp.arange(3 * BLOCK)[None, :]
    valid = ((kpos[:, None, :] >= 0) & (kpos[:, None, :] < S)
             & (jnp.abs(qpos[:, :, None] - kpos[:, None, :]) <= GQA_WINDOW))
    s = jnp.where(valid[None, :, None, None], s, NEG_INF)
    sink_b = jnp.broadcast_to(sink.astype(jnp.float32).reshape(KVH, G)[None, None, :, :, None, None],
                              s.shape[:-1] + (1,))
    p = jax.nn.softmax(jnp.concatenate([s, sink_b], axis=-1), axis=-1)[..., :-1]
    o = jnp.einsum('bnkgqj,bnjkd->bnqkgd', p.astype(v.dtype), vw)
    return o.reshape(B, S, H * hd)


def gated_parallel_mixer(h, w_in, b_gate, na_rpb, mla_qa_g, mla_kva_g, mla_w_uq, mla_w_ukv,
                         gqa_sink, w_br_na, w_br_mla, w_br_gqa, w_out, cos_mla, sin_mla, cos_gqa, sin_gqa):
    B, S, _ = h.shape
    z = h @ w_in
    offs = np.cumsum(IN_SPLITS)[:-1].tolist()
    z_na, c_q, c_kv, k_rope, z_gq, z_gkv, z_gate = jnp.split(z, offs, axis=-1)
    na_qkv = z_na.reshape(B, S, 3, NA_HEADS, NA_HEAD_DIM)
    y_na = neighbourhood_attention(na_qkv[:, :, 0], na_qkv[:, :, 1], na_qkv[:, :, 2], na_rpb)
    y_mla = mla_attention(c_q, c_kv, k_rope, mla_qa_g, mla_kva_g, mla_w_uq, mla_w_ukv, cos_mla, sin_mla)
    gkv = z_gkv.reshape(B, S, 2, GQA_KV_HEADS, GQA_HEAD_DIM)
    y_gqa = window_gqa_sink(z_gq.reshape(B, S, GQA_HEADS, GQA_HEAD_DIM), gkv[:, :, 0], gkv[:, :, 1],
                            gqa_sink, cos_gqa, sin_gqa)
    gates = jax.nn.sigmoid(z_gate + b_gate).reshape(B, S, N_BRANCH, D_MODEL)
    merged = (gates[:, :, 0] * (y_na @ w_br_na)
              + gates[:, :, 1] * (y_mla @ w_br_mla)
              + gates[:, :, 2] * (y_gqa @ w_br_gqa))
    return merged @ w_out


def conv_ffn(h, w_up, conv_w, conv_b, w_down):
    u = h @ w_up
    C = u.shape[-1]
    u = lax.conv_general_dilated(u, conv_w[:, None, :].astype(u.dtype), window_strides=(1,),
                                 padding=((CONV_W // 2, CONV_W // 2),),
                                 dimension_numbers=('NWC', 'WIO', 'NWC'),
                                 feature_group_count=C) + conv_b
    a, g = jnp.split(u, 2, axis=-1)
    return (jax.nn.gelu(g) * a) @ w_down


def setup_inputs(seed: int = 0) -> dict:
    key = jax.random.key(seed)
    ks = jax.random.split(key, 24)
    f32 = jnp.float32

    def nrm(k, shape, scale):
        return jax.random.normal(k, shape, f32) * scale

    L = DEPTH
    return {
        "x": nrm(ks[0], (BATCH, SEQ, D_MODEL), 1.0),
        "norm1_g": 1.0 + nrm(ks[1], (L, D_MODEL), 0.05),
        "w_in": nrm(ks[2], (L, D_MODEL, IN_W), D_MODEL ** -0.5),
        "b_gate": nrm(ks[3], (L, N_BRANCH * D_MODEL), 0.02),
        "na_rpb": nrm(ks[4], (L, NA_HEADS, 2 * NA_WIN_R - 1, 2 * NA_WIN_C - 1), 0.1),
        "mla_qa_g": 1.0 + nrm(ks[5], (L, MLA_Q_RANK), 0.05),
        "mla_kva_g": 1.0 + nrm(ks[6], (L, MLA_KV_RANK), 0.05),
        "mla_w_uq": nrm(ks[7], (L, MLA_Q_RANK, MLA_HEADS * MLA_QK), MLA_Q_RANK ** -0.5),
        "mla_w_ukv": nrm(ks[8], (L, MLA_KV_RANK, MLA_HEADS * (MLA_NOPE + MLA_V)), MLA_KV_RANK ** -0.5),
        "gqa_sink": nrm(ks[9], (L, GQA_HEADS), 0.5),
        "w_br_na": nrm(ks[10], (L, NA_W, D_MODEL), NA_W ** -0.5),
        "w_br_mla": nrm(ks[11], (L, MLA_W, D_MODEL), MLA_W ** -0.5),
        "w_br_gqa": nrm(ks[12], (L, GQA_Q_W, D_MODEL), GQA_Q_W ** -0.5),
        "w_out": nrm(ks[13], (L, D_MODEL, D_MODEL), D_MODEL ** -0.5),
        "norm2_g": 1.0 + nrm(ks[14], (L, D_MODEL), 0.05),
        "w_up": nrm(ks[15], (L, D_MODEL, 2 * D_FF), D_MODEL ** -0.5),
        "conv_w": nrm(ks[16], (L, CONV_W, 2 * D_FF), CONV_W ** -0.5),
        "conv_b": nrm(ks[17], (L, 2 * D_FF), 0.02),
        "w_down": nrm(ks[18], (L, D_FF, D_MODEL), D_FF ** -0.5),
        "final_g": 1.0 + nrm(ks[19], (D_MODEL,), 0.05),
    }


def reference(x, norm1_g, w_in, b_gate, na_rpb, mla_qa_g, mla_kva_g, mla_w_uq, mla_w_ukv,
              gqa_sink, w_br_na, w_br_mla, w_br_gqa, w_out, norm2_g, w_up, conv_w, conv_b,
              w_down, final_g):
    S = x.shape[1]
    cos_mla, sin_mla = rope_tables(S, MLA_ROPE)
    cos_gqa, sin_gqa = rope_tables(S, GQA_HEAD_DIM)
    for l in range(DEPTH):
        h = rmsnorm(x, norm1_g[l])
        x = x + gated_parallel_mixer(h, w_in[l], b_gate[l], na_rpb[l], mla_qa_g[l], mla_kva_g[l],
                                     mla_w_uq[l], mla_w_ukv[l], gqa_sink[l], w_br_na[l], w_br_mla[l],
                                     w_br_gqa[l], w_out[l], cos_mla, sin_mla, cos_gqa, sin_gqa)
        h = rmsnorm(x, norm2_g[l])
        x = x + conv_ffn(h, w_up[l], conv_w[l], conv_b[l], w_down[l])
    return rmsnorm(x, final_g)
```

```python
import numpy as np
import ml_dtypes
from contextlib import ExitStack
import concourse.bass as bass
import concourse.mybir as mybir
from concourse.bass_utils import run_bass_kernel_spmd

F32 = mybir.dt.float32
BF16 = mybir.dt.bfloat16
AF = mybir.ActivationFunctionType
ALU = mybir.AluOpType

D = 1024
L_FULL = 2
T_FULL = 8192
GRID_W = 64
EPS = 1e-6
D_FF = 2816
IN_W = 6080
NEG = -30000.0
COMPUTE = ("pe", "act", "dve", "pool")


class Sched:
    def __init__(self, nc, n_dma_sems=8):
        self.nc = nc
        self.ins = {e: [] for e in ("pe", "act", "dve", "pool", "sp")}
        self.known = {e: {} for e in self.ins}
        self.res = {}
        self.milestones = {e: set() for e in COMPUTE}
        self.streams = {}
        self.n_dma_sems = n_dma_sems
        self.dma_sem_keys = []
        self.dma_latest = {}
        self.last_compute = {e: 0 for e in COMPUTE}

    def _r(self, key):
        r = self.res.get(key)
        if r is None:
            r = ({}, {})
            self.res[key] = r
        return r

    def _deps(self, eng, reads, writes, is_dma):
        deps = {}
        me = ("E", eng)

        def add(k, v, same_ok):
            if (not is_dma) and same_ok and k == me:
                return
            if deps.get(k, 0) < v:
                deps[k] = v

        for r in reads:
            W, R = self._r(r)
            for k, v in W.items():
                add(k, v, False)
        for w in writes:
            W, R = self._r(w)
            for k, v in W.items():
                add(k, v, True)
            for k, v in R.items():
                add(k, v, True)
        out = []
        kn = self.known[eng]
        for k, v in deps.items():
            if kn.get(k, 0) >= v:
                continue
            kn[k] = v
            out.append((k, v))
            if k[0] == "E":
                self.milestones[k[1]].add(v)
        return out

    def _update(self, tok, reads, writes):
        k, v = tok
        for r in reads:
            W, R = self._r(r)
            if R.get(k, 0) < v:
                R[k] = v
        for w in writes:
            W, R = self._r(w)
            W.clear()
            R.clear()
            W[k] = v

    def op(self, eng, name, reads=(), writes=(), **kw):
        waits = self._deps(eng, reads, writes, False)
        self.ins[eng].append((name, kw, waits, None))
        tok = (("E", eng), len(self.ins[eng]))
        self.last_compute[eng] = len(self.ins[eng])
        self._update(tok, reads, writes)

    def dma(self, q, out, in_, reads=(), writes=(), stream="ld"):
        st = self.streams.get(stream)
        if st is None:
            base = len(self.dma_sem_keys)
            keys = [("D", base + i) for i in range(self.n_dma_sems)]
            self.dma_sem_keys += keys
            st = {"keys": keys, "n": 0}
            self.streams[stream] = st
        i = st["n"]
        st["n"] += 1
        K = len(st["keys"])
        key = st["keys"][i % K]
        val = 16 * (i // K + 1)
        waits = self._deps(q, reads, writes, True)
        if val > 16 and self.known[q].get(key, 0) < val - 16:
            self.known[q][key] = val - 16
            waits.append((key, val - 16))
        self.ins[q].append(("dma_start", dict(out=out, in_=in_), waits, (key, 16)))
        self.dma_latest[key] = val
        self._update((key, val), reads, writes)

    def barrier(self):
        toks = []
        for e in COMPUTE:
            if self.last_compute[e]:
                toks.append((("E", e), self.last_compute[e]))
        for k, v in self.dma_latest.items():
            toks.append((k, v))
        for e in self.ins:
            waits = []
            for k, v in toks:
                if k == ("E", e):
                    continue
                if self.known[e].get(k, 0) >= v:
                    continue
                self.known[e][k] = v
                waits.append((k, v))
                if k[0] == "E":
                    self.milestones[k[1]].add(v)
            if waits:
                self.ins[e].append((None, None, waits, None))
        self.res = {}

    def emit(self):
        nc = self.nc
        sems = {}
        for e in COMPUTE:
            sems[("E", e)] = nc.alloc_semaphore("sem_" + e)
        for k in self.dma_sem_keys:
            sems[k] = nc.alloc_semaphore("semd%d" % k[1])
        rank = {}
        for e in COMPUTE:
            ms = sorted(self.milestones[e])
            rank[e] = {v: i + 1 for i, v in enumerate(ms)}

        def run(engine, lst, ename):
            my = rank.get(ename, {})
            mysem = sems.get(("E", ename))
            for idx, (name, kw, waits, dinc) in enumerate(lst):
                for k, v in waits:
                    if k[0] == "E":
                        engine.wait_ge(sems[k], rank[k[1]][v])
                    else:
                        engine.wait_ge(sems[k], v)
                if name is None:
                    continue
                r = getattr(engine, name)(**kw)
                if dinc is not None:
                    r.then_inc(sems[dinc[0]], dinc[1])
                elif (idx + 1) in my:
                    r.then_inc(mysem, 1)

        with nc.Block() as block:
            @block.tensor
            def _(e):
                run(e, self.ins["pe"], "pe")

            @block.scalar
            def _(e):
                run(e, self.ins["act"], "act")

            @block.vector
            def _(e):
                run(e, self.ins["dve"], "dve")

            @block.gpsimd
            def _(e):
                run(e, self.ins["pool"], "pool")

            @block.sync
            def _(e):
                run(e, self.ins["sp"], "sp")


def rope_tables_fm(T):
    inv = 1.0 / (10000.0 ** (np.arange(0, 64, 2, dtype=np.float32) / 64.0))
    ang = np.arange(T, dtype=np.float32)[:, None] * inv[None, :]
    c = np.cos(ang).astype(np.float32).T
    s = np.sin(ang).astype(np.float32).T
    C = np.concatenate([c, c, c, c], axis=0)
    S = np.concatenate([-s, s, -s, s], axis=0)
    return np.ascontiguousarray(C), np.ascontiguousarray(S)


def na_classes(T):
    rows = T // GRID_W
    NT = T // 128
    kk = np.arange(128)
    qq = np.arange(128)
    cls_of_tile = []
    classes = []
    keys = {}
    for t in range(NT):
        kt0 = min(max(t - 2, 0), NT - 5)
        r = 2 * t + qq // 64
        cq = qq % 64
        r_start = np.clip(r - 4, 0, rows - 8)
        c_start = np.clip(cq - 8, 0, GRID_W - 16)
        valid = np.zeros((128, 5, 128), bool)
        ri = np.zeros((128, 5, 128), np.int64)
        ci = np.zeros((128, 5, 128), np.int64)
        for j in range(5):
            kt = kt0 + j
            rk = 2 * kt + kk // 64
            ck = kk % 64
            v = ((rk[:, None] >= r_start[None, :]) & (rk[:, None] < r_start[None, :] + 8)
                 & (ck[:, None] >= c_start[None, :]) & (ck[:, None] < c_start[None, :] + 16))
            valid[:, j, :] = v
            ri[:, j, :] = np.clip(rk[:, None] - r[None, :] + 7, 0, 14)
            ci[:, j, :] = np.clip(ck[:, None] - cq[None, :] + 15, 0, 30)
        key = (valid.tobytes(), ri.tobytes(), ci.tobytes())
        if key not in keys:
            keys[key] = len(classes)
            classes.append((valid, ri, ci))
        cls_of_tile.append(keys[key])
    return cls_of_tile, classes


def na_bias_tables(rpb, classes):
    Lx = rpb.shape[0]
    out = np.empty((Lx, len(classes), 128, 5, 8, 128), np.float32)
    for c, (valid, ri, ci) in enumerate(classes):
        g = rpb[:, :, ri, ci]
        g = np.where(valid[None, None], g, np.float32(NEG))
        out[:, c] = g.transpose(0, 2, 3, 1, 4)
    return out


def gqa_masks():
    kk = np.arange(128)[:, None]
    qq = np.arange(128)[None, :]
    m_prev = np.where(qq <= kk, 0.0, -1.0e6).astype(np.float32)
    m_next = np.where(kk <= qq, 0.0, -1.0e6).astype(np.float32)
    m = np.stack([np.tile(m_prev, (1, 4)), np.tile(m_next, (1, 4))], axis=0)
    return np.ascontiguousarray(m)


def pcol(v, nchunk):
    return np.ascontiguousarray(v.reshape(nchunk, 128).T)


def build(T, L, ncls, cls_of_tile, debug=False, stages=None):
    NB = T // 512
    NT = T // 128
    nc = bass.Bass("TRN2", target_bir_lowering=False)
    S = Sched(nc)

    def din(name, shape, dt=F32):
        return nc.dram_tensor(name, list(shape), dt, kind="ExternalInput").ap()

    def dscr(name, shape, dt):
        return nc.dram_tensor(name, list(shape), dt, kind=("ExternalOutput" if debug else "Internal")).ap()

    xT = din("xT", [D, T])
    w_in = din("w_in", [L, D, IN_W])
    w_uq = din("mla_w_uq", [L, 384, 768])
    w_ukv = din("mla_w_ukv", [L, 256, 1024])
    w_br = [din("w_br_na", [L, 512, D]), din("w_br_mla", [L, 512, D]), din("w_br_gqa", [L, 512, D])]
    w_out = din("w_out", [L, D, D])
    w_up = din("w_up", [L, D, 2 * D_FF])
    w_down = din("w_down", [L, D_FF, D])
    g1 = din("g1", [L, 128, 8])
    g2 = din("g2", [L, 128, 8])
    gf = din("gf", [128, 8])
    bg = din("bg", [L, 128, 24])
    gqa_ = din("gqa", [L, 128, 3])
    gkva = din("gkva", [L, 128, 2])
    sinkr = din("sinkr", [L, 64, 8])
    cw = din("cw", [L, 128, 44 * 3])
    cb = din("cb", [L, 128, 44])
    ropeC = din("ropeC", [128, T])
    ropeS = din("ropeS", [128, T])
    nab = din("nab", [L, ncls, 128, 5 * 8 * 128])
    gmask = din("gmask", [2, 128, 512])

    yT = nc.dram_tensor("yT", [D, T], F32, kind="ExternalOutput").ap()

    XA = dscr("XA", [D, T], F32)
    XB = dscr("XB", [D, T], F32)
    naq = dscr("naq", [8, 64, T], BF16)
    nak = dscr("nak", [8, 64, T], BF16)
    nav = dscr("nav", [T, 512], BF16)
    mqn = dscr("mqn", [4, 128, T], BF16)
    mqp = dscr("mqp", [4, 64, T], BF16)
    mkn = dscr("mkn", [4, 128, T], BF16)
    mkp = dscr("mkp", [64, T], BF16)
    mv = dscr("mv", [4, 128, NT, 128], BF16)
    gq = dscr("gq", [8, 64, T], BF16)
    gk = dscr("gk", [2, 64, T], BF16)
    gv = dscr("gv", [128, NT, 128], BF16)
    yna = dscr("yna", [512, T], BF16)
    ymla = dscr("ymla", [512, T], BF16)
    ygqa = dscr("ygqa", [512, T], BF16)

    def fm(ap):
        return ap.rearrange("(kc p) t -> p kc t", p=128)

    def mm(out, lhsT, rhs, start, stop, reads, writes, skip=False):
        kw = dict(out=out, lhsT=lhsT, rhs=rhs, start=start, stop=stop)
        if skip:
            kw["skip_group_check"] = True
        S.op("pe", "matmul", reads=reads, writes=writes, **kw)

    def act(out, in_, func, reads, writes, **kw):
        S.op("act", "activation", reads=reads, writes=writes, out=out, in_=in_, func=func, **kw)

    def tt(eng, out, in0, in1, op, reads, writes):
        S.op(eng, "tensor_tensor", reads=reads, writes=writes, out=out, in0=in0, in1=in1, op=op)

    def ts(eng, out, in0, s1, s2, op0, op1, reads, writes):
        kw = dict(out=out, in0=in0, scalar1=s1, scalar2=s2, op0=op0)
        if op1 is not None:
            kw["op1"] = op1
        S.op(eng, "tensor_scalar", reads=reads, writes=writes, **kw)

    def stt(out, in0, scalar, in1, op0, op1, reads, writes):
        S.op("dve", "scalar_tensor_tensor", reads=reads, writes=writes, out=out, in0=in0, scalar=scalar,
             in1=in1, op0=op0, op1=op1)

    def cp(eng, out, in_, reads, writes):
        if eng == "act":
            act(out, in_, AF.Copy, reads, writes)
        else:
            S.op(eng, "tensor_copy", reads=reads, writes=writes, out=out, in_=in_)

    def recip(out, in_, reads, writes):
        S.op("dve", "reciprocal", reads=reads, writes=writes, out=out, in_=in_)

    evac_rr = [0]
    SUF = [""]

    def evac(out, in_, reads, writes):
        evac_rr[0] ^= 1
        cp("act" if evac_rr[0] else "dve", out, in_, reads, writes)

    ones_bf = nc.alloc_sbuf_tensor("ones_bf", [128, 128], BF16)
    ones_f = nc.alloc_sbuf_tensor("ones_f", [128, 128], F32)
    S.op("pool", "memset", writes=["ones_bf"], ap=ones_bf[:], constant=1.0)
    S.op("pool", "memset", writes=["ones_f"], ap=ones_f[:], constant=1.0)

    def fm_norm(xs, xres, nch, n, gcol, gres, sq, sqres, ps, psres, tmp, tmpres, rstd, rstdres, xn, xnres, inv_sqrt_dim):
        act(sq[:, 0:nch, 0:n], xs[:, 0:nch, 0:n], AF.Square, [xres], [sqres], scale=inv_sqrt_dim)
        for c in range(nch):
            mm(ps[:, 0:n], ones_bf[:], sq[:, c, 0:n], c == 0, c == nch - 1, ["ones_bf", sqres], [psres])
        act(tmp[:, 0:n], ps[:, 0:n], AF.Ln, [psres], [tmpres], bias=EPS, scale=1.0)
        act(rstd[:, 0:n], tmp[:, 0:n], AF.Exp, [tmpres], [rstdres], scale=-0.5)
        for c in range(nch):
            stt(xn[:, c, 0:n], xs[:, c, 0:n], gcol[:, c:c + 1], rstd[:, 0:n], ALU.mult, ALU.mult,
                [xres, gres, rstdres], [xnres])

    def run_pipeline(steps, depth, defer):
        n = len(steps)
        ring = {}
        pend = []
        for idx in range(n + depth):
            if idx < n:
                ring[idx] = steps[idx][0]()
            if idx >= depth:
                fl = steps[idx - depth][1](ring.pop(idx - depth))
                if fl is not None:
                    for d, f in fl:
                        pend.append((idx + d, f))
                    pend.sort(key=lambda x: x[0])
            while pend and pend[0][0] <= idx:
                pend.pop(0)[1]()
        for _, f in pend:
            f()

    def stage1(l, xsrc):
        with ExitStack() as es:
            def sb(name, shape, dt):
                return es.enter_context(nc.sbuf_tensor(name + SUF[0], list(shape), dt))

            def pst(name, shape):
                return es.enter_context(nc.psum_tensor(name + SUF[0], list(shape), F32))
            Wa = sb("s1_Wa", [128, 8, 3008], BF16)
            Wsw = sb("s1_Wsw", [128, 8, 704], BF16)
            Wq = sb("s1_Wq", [128, 3, 768], BF16)
            Wqp = sb("s1_Wqp", [128, 3, 256], BF16)
            Wqps = sb("s1_Wqps", [128, 3, 256], BF16)
            Wkv = sb("s1_Wkv", [128, 2, 1024], BF16)
            Wv = sb("s1_Wv", [128, 2, 512], BF16)
            g1s = sb("s1_g1", [128, 8], F32)
            gqs = sb("s1_gq", [128, 3], F32)
            gks = sb("s1_gk", [128, 2], F32)
            xs = [sb("s1_x%d" % i, [128, 8, 512], F32) for i in range(2)]
            xn = [sb("s1_xn%d" % i, [128, 8, 512], BF16) for i in range(2)]
            sq = sb("s1_sq", [128, 8, 512], BF16)
            tmp = sb("s1_tmp", [128, 512], F32)
            rstd = sb("s1_rstd", [128, 512], F32)
            Cs = [sb("s1_C%d" % i, [128, 512], F32) for i in range(2)]
            Ss = [sb("s1_S%d" % i, [128, 512], F32) for i in range(2)]
            cq = sb("s1_cq", [128, 3, 512], F32)
            ckv = sb("s1_ckv", [128, 2, 512], F32)
            cqn = sb("s1_cqn", [128, 3, 512], BF16)
            ckvn = sb("s1_ckvn", [128, 2, 512], BF16)
            NST = 6
            stg = [sb("s1_stg%d" % i, [128, 512], BF16) for i in range(NST)]
            t1 = [sb("s1_t1_%d" % i, [128, 512], F32) for i in range(2)]
            t2 = [sb("s1_t2_%d" % i, [128, 512], F32) for i in range(2)]
            NPS = 5
            pss = [pst("s1_ps%d" % i, [128, 512]) for i in range(NPS)]
            ps_ss = pst("s1_pss", [128, 512])
            cnt = {"ps": 0, "stg": 0, "t": 0}

            def nps():
                i = cnt["ps"] % NPS
                cnt["ps"] += 1
                return pss[i], "s1_ps%d" % i

            def nstg():
                i = cnt["stg"] % NST
                cnt["stg"] += 1
                return stg[i], "s1_stg%d" % i

            win_l = w_in[l].rearrange("(kc p) n -> p kc n", p=128)
            WA_GROUPS = [(1536, 2176), (0, 512), (512, 1024), (2176, 2880), (1024, 1536), (2880, 3008)]

            def wa_res(c0):
                for gi, (a0, a1) in enumerate(WA_GROUPS):
                    if a0 <= c0 < a1:
                        return ("s1_Wa", gi)
                raise ValueError(c0)

            def load_wa(gi):
                a0, a1 = WA_GROUPS[gi]
                S.dma("pool", Wa[:, :, a0:a1], win_l[:, :, a0:a1], writes=[("s1_Wa", gi)], stream="w")
            load_wa(0)
            load_wa(1)
            load_wa(2)
            load_wa(3)
            load_wa(4)
            load_wa(5)
            wuq_l = w_uq[l].rearrange("(kc p) n -> p kc n", p=128)
            S.dma("pool", Wq[:], wuq_l, writes=["s1_Wq"], stream="w")
            wukv_l = w_ukv[l].rearrange("(kc p) n -> p kc n", p=128)
            S.dma("pool", Wkv[:], wukv_l, writes=["s1_Wkv"], stream="w")
            for (src0, nh, dst0) in ((2176, 1, 0), (2240, 8, 64), (2752, 2, 576)):
                srcv = Wa[:, :, src0:src0 + nh * 64].rearrange("p k (h two r) -> p k h two r", two=2, r=32)
                dstv = Wsw[:, :, dst0:dst0 + nh * 64].rearrange("p k (h two r) -> p k h two r", two=2, r=32)
                S.op("pool", "tensor_copy", reads=[("s1_Wa", 3)], writes=["s1_Wsw"], out=dstv[:, :, :, 0, :], in_=srcv[:, :, :, 1, :])
                S.op("pool", "tensor_copy", reads=[("s1_Wa", 3)], writes=["s1_Wsw"], out=dstv[:, :, :, 1, :], in_=srcv[:, :, :, 0, :])
            wq4 = Wq[:].rearrange("p k (h c) -> p k h c", c=192)
            S.op("pool", "tensor_copy", reads=["s1_Wq"], writes=["s1_Wqp"],
                 out=Wqp[:].rearrange("p k (h c) -> p k h c", c=64), in_=wq4[:, :, :, 128:192])
            wqps4 = Wqps[:].rearrange("p k (h c) -> p k h c", c=64)
            S.op("pool", "tensor_copy", reads=["s1_Wq"], writes=["s1_Wqps"], out=wqps4[:, :, :, 0:32], in_=wq4[:, :, :, 160:192])
            S.op("pool", "tensor_copy", reads=["s1_Wq"], writes=["s1_Wqps"], out=wqps4[:, :, :, 32:64], in_=wq4[:, :, :, 128:160])
            S.op("pool", "tensor_copy", reads=["s1_Wkv"], writes=["s1_Wv"],
                 out=Wv[:].rearrange("p k (h c) -> p k h c", c=128),
                 in_=Wkv[:].rearrange("p k (h c) -> p k h c", c=256)[:, :, :, 128:256])
            S.dma("sp", g1s[:], g1[l], writes=["s1_g1"], stream="ld")
            S.dma("sp", gqs[:], gqa_[l], writes=["s1_gq"], stream="ld")
            S.dma("sp", gks[:], gkva[l], writes=["s1_gk"], stream="ld")

            xsrc_f = fm(xsrc)
            naq_f = naq.rearrange("h d t -> (h d) t")
            nak_f = nak.rearrange("h d t -> (h d) t")
            mqp_f = mqp.rearrange("h d t -> (h d) t")
            gq_f = gq.rearrange("h d t -> (h d) t")
            gk_f = gk.rearrange("h d t -> (h d) t")

            def load(b):
                sl = b % 2
                t0 = b * 512
                S.dma("sp", xs[sl][:], xsrc_f[:, :, t0:t0 + 512], reads=[("X", id(xsrc), b)], writes=["s1_x%d" % sl], stream="ld")
                S.dma("sp", Cs[sl][:], ropeC[:, t0:t0 + 512], writes=["s1_C%d" % sl], stream="ld")
                S.dma("sp", Ss[sl][:], ropeS[:, t0:t0 + 512], writes=["s1_S%d" % sl], stream="ld")

            def store(dst, src, srcres, dstres):
                S.dma("sp", dst, src, reads=[srcres], writes=[dstres], stream="st")

            def normA(xin, nch, sqb, sqres, xres, scl):
                act(sqb[:, 0:nch, :], xin[:, 0:nch, :], AF.Square, [xres], [sqres], scale=scl)

            def normB(sqb, nch, ps, psres, sqres):
                for c in range(nch):
                    mm(ps[:], ones_bf[:], sqb[:, c, :], c == 0, c == nch - 1, ["ones_bf", sqres], [psres])

            def normC(ps, psres, tm, tmres, rs, rsres, xin, xres, nch, gcol, gres, xo, xores):
                act(tm[:], ps[:], AF.Ln, [psres], [tmres], bias=EPS, scale=1.0)
                act(rs[:], tm[:], AF.Exp, [tmres], [rsres], scale=-0.5)
                for c in range(nch):
                    stt(xo[:, c, :], xin[:, c, :], gcol[:, c:c + 1], rs[:], ALU.mult, ALU.mult,
                        [xres, gres, rsres], [xores])

            sqc = sb("s1_sqc", [128, 5, 512], BF16)
            tmp_c = sb("s1_tmpc", [128, 512], F32)
            rstd_c = sb("s1_rstdc", [128, 512], F32)
            tmp_k = sb("s1_tmpk", [128, 512], F32)
            rstd_k = sb("s1_rstdk", [128, 512], F32)
            ps_ssc = pst("s1_pssc", [128, 512])
            ps_ssk = pst("s1_pssk", [128, 512])

            def xnorm(b):
                sl_ = b % 2
                xr_, xnr_ = "s1_x%d" % sl_, "s1_xn%d" % sl_
                normA(xs[sl_], 8, sq, "s1_sq", xr_, 1.0 / 32.0)
                normB(sq, 8, ps_ss, "s1_pss", "s1_sq")
                normC(ps_ss, "s1_pss", tmp, "s1_tmp", rstd, "s1_rstd", xs[sl_], xr_, 8, g1s, "s1_g1", xn[sl_], xnr_)

            load(0)
            xnorm(0)
            for b in range(NB):
                sl = b % 2
                t0 = b * 512
                if b + 1 < NB:
                    load(b + 1)
                xr, xnr = "s1_x%d" % sl, "s1_xn%d" % sl
                X = xn[sl]

                def proj_fm(W, Wres, c0, M):
                    ps, psr = nps()
                    if Wres == "s1_Wa":
                        Wres = wa_res(c0)
                    for kc in range(8):
                        mm(ps[0:M, :], W[:, kc, c0:c0 + M], X[:, kc, :], kc == 0, kc == 7, [Wres, xnr], [psr])
                    return ps, psr

                def plain_fm(c0, M, dst):
                    ps, psr = proj_fm(Wa, "s1_Wa", c0, M)
                    st_, sr = nstg()
                    evac(st_[0:M, :], ps[0:M, :], [psr], [sr])
                    store(dst, st_[0:M, :], sr, ("scr", id(dst), b))

                def rope_fm(c0, csw, M, dst, dres):
                    pa, par = proj_fm(Wa, "s1_Wa", c0, M)
                    pb, pbr = proj_fm(Wsw, "s1_Wsw", csw, M)
                    i = cnt["t"] % 2
                    cnt["t"] += 1
                    tt("dve", t1[i][0:M, :], pa[0:M, :], Cs[sl][0:M, :], ALU.mult, [par, "s1_C%d" % sl], ["s1_t1_%d" % i])
                    tt("dve", t2[i][0:M, :], pb[0:M, :], Ss[sl][0:M, :], ALU.mult, [pbr, "s1_S%d" % sl], ["s1_t2_%d" % i])
                    st_, sr = nstg()
                    tt("pool", st_[0:M, :], t1[i][0:M, :], t2[i][0:M, :], ALU.add, ["s1_t1_%d" % i, "s1_t2_%d" % i], [sr])
                    store(dst, st_[0:M, :], sr, dres)

                for c in range(3):
                    ps, psr = proj_fm(Wa, "s1_Wa", 1536 + c * 128, 128)
                    cp("dve", cq[:, c, :], ps[:], [psr], ["s1_cq"])
                for c in range(2):
                    ps, psr = proj_fm(Wa, "s1_Wa", 1920 + c * 128, 128)
                    cp("dve", ckv[:, c, :], ps[:], [psr], ["s1_ckv"])
                normA(cq, 3, sqc[:, 0:3, :], "s1_sqc", "s1_cq", 1.0 / np.sqrt(384.0))
                normA(ckv, 2, sqc[:, 3:5, :], "s1_sqk", "s1_ckv", 1.0 / 16.0)
                for c in range(4):
                    plain_fm(c * 128, 128, naq_f[c * 128:(c + 1) * 128, t0:t0 + 512])
                normB(sqc[:, 0:3, :], 3, ps_ssc, "s1_pssc", "s1_sqc")
                normB(sqc[:, 3:5, :], 2, ps_ssk, "s1_pssk", "s1_sqk")
                normC(ps_ssc, "s1_pssc", tmp_c, "s1_tmpc", rstd_c, "s1_rstdc", cq, "s1_cq", 3, gqs, "s1_gq", cqn, "s1_cqn")
                normC(ps_ssk, "s1_pssk", tmp_k, "s1_tmpk", rstd_k, "s1_rstdk", ckv, "s1_ckv", 2, gks, "s1_gk", ckvn, "s1_ckvn")
                for c in range(4):
                    plain_fm(512 + c * 128, 128, nak_f[c * 128:(c + 1) * 128, t0:t0 + 512])
                rope_fm(2176, 0, 64, mkp[:, t0:t0 + 512], ("scr", "mkp", b))
                for c in range(4):
                    rope_fm(2240 + c * 128, 64 + c * 128, 128, gq_f[c * 128:(c + 1) * 128, t0:t0 + 512], ("scr", "gq", b, c))
                rope_fm(2752, 576, 128, gk_f[:, t0:t0 + 512], ("scr", "gk", b))
                for i in range(4):
                    ps, psr = nps()
                    for kc in range(8):
                        mm(ps[:], X[:, kc, i * 128:(i + 1) * 128], Wa[:, kc, 1024:1536], kc == 0, kc == 7, [wa_res(1024), xnr], [psr])
                    st_, sr = nstg()
                    evac(st_[:], ps[:], [psr], [sr])
                    store(nav[t0 + i * 128:t0 + (i + 1) * 128, :], st_[:], sr, ("scr", "nav", b, i))
                ps, psr = nps()
                for i in range(4):
                    for kc in range(8):
                        mm(ps[:, i * 128:(i + 1) * 128], X[:, kc, i * 128:(i + 1) * 128], Wa[:, kc, 2880:3008],
                           kc == 0, kc == 7, [wa_res(2880), xnr], [psr])
                st_, sr = nstg()
                evac(st_[:], ps[:], [psr], [sr])
                store(gv[:, b * 4:(b + 1) * 4, :],
                      st_[:].rearrange("p (i d) -> p i d", d=128), sr, ("scr", "gv", b))
                if b + 1 < NB:
                    xnorm(b + 1)
                for h in range(4):
                    ps, psr = nps()
                    for kc in range(3):
                        mm(ps[:], Wq[:, kc, h * 192:h * 192 + 128], cqn[:, kc, :], kc == 0, kc == 2, ["s1_Wq", "s1_cqn"], [psr])
                    st_, sr = nstg()
                    evac(st_[:], ps[:], [psr], [sr])
                    store(mqn[h, :, t0:t0 + 512], st_[:], sr, ("scr", "mqn", b, h))
                for hp in range(2):
                    pa, par = nps()
                    for kc in range(3):
                        mm(pa[:], Wqp[:, kc, hp * 128:(hp + 1) * 128], cqn[:, kc, :], kc == 0, kc == 2, ["s1_Wqp", "s1_cqn"], [par])
                    pb, pbr = nps()
                    for kc in range(3):
                        mm(pb[:], Wqps[:, kc, hp * 128:(hp + 1) * 128], cqn[:, kc, :], kc == 0, kc == 2, ["s1_Wqps", "s1_cqn"], [pbr])
                    i = cnt["t"] % 2
                    cnt["t"] += 1
                    tt("dve", t1[i][:], pa[:], Cs[sl][:], ALU.mult, [par, "s1_C%d" % sl], ["s1_t1_%d" % i])
                    tt("dve", t2[i][:], pb[:], Ss[sl][:], ALU.mult, [pbr, "s1_S%d" % sl], ["s1_t2_%d" % i])
                    st_, sr = nstg()
                    tt("pool", st_[:], t1[i][:], t2[i][:], ALU.add, ["s1_t1_%d" % i, "s1_t2_%d" % i], [sr])
                    store(mqp_f[hp * 128:(hp + 1) * 128, t0:t0 + 512], st_[:], sr, ("scr", "mqp", b, hp))
                for h in range(4):
                    ps, psr = nps()
                    for kc in range(2):
                        mm(ps[:], Wkv[:, kc, h * 256:h * 256 + 128], ckvn[:, kc, :], kc == 0, kc == 1, ["s1_Wkv", "s1_ckvn"], [psr])
                    st_, sr = nstg()
                    evac(st_[:], ps[:], [psr], [sr])
                    store(mkn[h, :, t0:t0 + 512], st_[:], sr, ("scr", "mkn", b, h))
                for i in range(4):
                    ps, psr = nps()
                    for kc in range(2):
                        mm(ps[:], ckvn[:, kc, i * 128:(i + 1) * 128], Wv[:, kc, :], kc == 0, kc == 1, ["s1_Wv", "s1_ckvn"], [psr])
                    st_, sr = nstg()
                    evac(st_[:], ps[:], [psr], [sr])
                    store(mv[:, :, b * 4 + i, :].rearrange("h p d -> p h d"),
                          st_[:].rearrange("p (h d) -> p h d", d=128), sr, ("scr", "mv", b, i))
        S.barrier()

    def stage_mla(l, prefetch=None):
        scale = 192.0 ** -0.5
        with ExitStack() as es:
            def sb(name, shape, dt):
                return es.enter_context(nc.sbuf_tensor(name + SUF[0], list(shape), dt))

            def pst(name, shape):
                return es.enter_context(nc.psum_tensor(name + SUF[0], list(shape), F32))
            kpe = sb("ml_kpe", [128, T], BF16)
            Ks = [sb("ml_K%d" % i, [128, T], BF16) for i in range(2)]
            Vs = [sb("ml_V%d" % i, [128, NT, 128], BF16) for i in range(2)]
            Qn = [sb("ml_Qn%d" % i, [128, 512], BF16) for i in range(2)]
            Qp = [sb("ml_Qp%d" % i, [128, 512], BF16) for i in range(2)]
            NP = 3
            NPT = 4
            pt = [sb("ml_pt%d" % i, [128, 512], BF16) for i in range(NPT)]
            acc = [sb("ml_acc%d" % i, [128, 512], F32) for i in range(2)]
            tr1 = sb("ml_tr1", [128, 512], BF16)
            tr2 = sb("ml_tr2", [128, 512], BF16)
            tr3 = sb("ml_tr3", [128, 512], BF16)
            lnt = sb("ml_ln", [128, 512], F32)
            rec = sb("ml_rec", [128, 512], F32)
            yst = [sb("ml_y%d" % i, [128, 512], BF16) for i in range(2)]
            ps_s = [pst("ml_pss%d" % i, [128, 512]) for i in range(NP)]
            ps_o = [pst("ml_pso%d" % i, [128, 512]) for i in range(2)]
            ps_sum = pst("ml_psum", [128, 512])

            S.op("pool", "memset", writes=["ml_kpe"], ap=kpe[64:128, :], constant=0.0)
            for i in range(2):
                S.op("pool", "memset", writes=["ml_Qp%d" % i], ap=Qp[i][64:128, :], constant=0.0)
            S.dma("sp", kpe[0:64, :], mkp[:, :], writes=["ml_kpe"], stream="ld")

            def loadKV(h):
                s = h % 2
                S.dma("sp", Ks[s][:], mkn[h, :, :], writes=["ml_K%d" % s], stream="ld")
                S.dma("sp", Vs[s][:], mv[h], writes=["ml_V%d" % s], stream="ld")

            def loadQ(h, qb, it):
                s = it % 2
                S.dma("sp", Qn[s][:], mqn[h, :, qb * 512:(qb + 1) * 512], writes=["ml_Qn%d" % s], stream="ld")
                S.dma("sp", Qp[s][0:64, :], mqp[h, :, qb * 512:(qb + 1) * 512], writes=["ml_Qp%d" % s], stream="ld")

            items = [(h, qb) for h in range(4) for qb in range(NB)]
            loadKV(0)
            loadQ(0, 0, 0)
            gcount = [0]
            steps = []
            flat = []
            for it, (h, qb) in enumerate(items):
                hs = h % 2
                qs = it % 2
                for kb in range(NT):
                    def qk(it=it, h=h, qb=qb, hs=hs, qs=qs, kb=kb):
                        if kb == min(8, NT - 1) and it == 0 and prefetch is not None:
                            prefetch()
                        if kb == 0 and it + 1 < len(items):
                            nh, nqb = items[it + 1]
                            if nh != h:
                                loadKV(nh)
                            loadQ(nh, nqb, it + 1)
                        i = gcount[0] % NP
                        ip = gcount[0] % NPT
                        gcount[0] += 1
                        mm(ps_s[i][:], Ks[hs][:, kb * 128:(kb + 1) * 128], Qn[qs][:], True, False,
                           ["ml_K%d" % hs, "ml_Qn%d" % qs], ["ml_pss%d" % i])
                        mm(ps_s[i][:], kpe[:, kb * 128:(kb + 1) * 128], Qp[qs][:], False, True,
                           ["ml_kpe", "ml_Qp%d" % qs], ["ml_pss%d" % i])
                        act(pt[ip][:], ps_s[i][:], AF.Exp, ["ml_pss%d" % i], ["ml_pt%d" % ip], scale=scale)
                        return ip
                    steps.append((kb, qk))
                    flat.append((it, h, qb, kb))
            ring = {}
            pend = []
            iprev = [0]
            depth = 2
            n = len(flat)
            for idx in range(n + depth):
                if idx < n:
                    it, h, qb, kb = flat[idx]
                    ring[idx] = steps[idx][1]()
                if idx >= depth:
                    j = idx - depth
                    it, h, qb, kb = flat[j]
                    i = ring.pop(j)
                    hs = h % 2
                    a = it % 2
                    mm(ps_o[a][:], Vs[hs][:, kb, :], pt[i][:], kb == 0, kb == NT - 1,
                       ["ml_V%d" % hs, "ml_pt%d" % i], ["ml_pso%d" % a])
                    if kb % 4 == 1:
                        tt("dve", tr1[:], pt[iprev[0]][:], pt[i][:], ALU.add, ["ml_pt%d" % iprev[0], "ml_pt%d" % i], ["ml_tr1"])
                    elif kb % 4 == 3:
                        tt("dve", tr2[:], pt[iprev[0]][:], pt[i][:], ALU.add, ["ml_pt%d" % iprev[0], "ml_pt%d" % i], ["ml_tr2"])
                        if kb == 3:
                            tt("dve", acc[a][:], tr1[:], tr2[:], ALU.add, ["ml_tr1", "ml_tr2"], ["ml_acc%d" % a])
                        else:
                            tt("dve", tr3[:], tr1[:], tr2[:], ALU.add, ["ml_tr1", "ml_tr2"], ["ml_tr3"])
                            tt("dve", acc[a][:], acc[a][:], tr3[:], ALU.add, ["ml_acc%d" % a, "ml_tr3"], ["ml_acc%d" % a])
                    iprev[0] = i
                    if kb == NT - 1:
                        def fin(a=a, h=h, qb=qb):
                            mm(ps_sum[:], ones_f[:], acc[a][:], True, True, ["ones_f", "ml_acc%d" % a], ["ml_psum"])
                            act(lnt[:], ps_sum[:], AF.Ln, ["ml_psum"], ["ml_ln"])
                            act(rec[:], lnt[:], AF.Exp, ["ml_ln"], ["ml_rec"], scale=-1.0)
                            tt("dve", yst[a][:], ps_o[a][:], rec[:], ALU.mult, ["ml_pso%d" % a, "ml_rec"], ["ml_y%d" % a])
                            S.dma("sp", ymla[h * 128:(h + 1) * 128, qb * 512:(qb + 1) * 512], yst[a][:],
                                  reads=["ml_y%d" % a], writes=[("scr", "ymla", h, qb)], stream="st")
                        pend.append((idx + 4, fin))
                while pend and pend[0][0] <= idx:
                    pend.pop(0)[1]()
            for _, f in pend:
                f()
        S.barrier()

    def stage_na(l):
        with ExitStack() as es:
            def sb(name, shape, dt):
                return es.enter_context(nc.sbuf_tensor(name + SUF[0], list(shape), dt))

            def pst(name, shape):
                return es.enter_context(nc.psum_tensor(name + SUF[0], list(shape), F32))
            RING = 8
            kT = [sb("na_k%d" % i, [128, 4, 128], BF16) for i in range(RING)]
            vr = [sb("na_v%d" % i, [128, 512], BF16) for i in range(RING)]
            qT = [sb("na_q%d" % i, [128, 4, 2, 128], BF16) for i in range(2)]
            bias_int = sb("na_bint", [128, 5, 8, 128], F32)
            bias_sp = sb("na_bsp", [128, 5, 8, 128], F32)
            NP = 3
            sbf = [sb("na_sb%d" % i, [128, 4, 128], F32) for i in range(NP)]
            pt = [sb("na_pt%d" % i, [128, 4, 128], BF16) for i in range(NP)]
            rec = [sb("na_rec%d" % i, [64, 512], F32) for i in range(2)]
            lnt = sb("na_ln", [64, 512], F32)
            yst = [sb("na_y%d" % i, [64, 4, 128], BF16) for i in range(2)]
            ps_s = [pst("na_pss%d" % i, [128, 4, 128]) for i in range(NP)]
            ps_o = [pst("na_pso%d" % i, [64, 4, 128]) for i in range(2)]
            ps_sum = [pst("na_psum%d" % i, [64, 512]) for i in range(2)]

            counts = {}
            for c in cls_of_tile:
                counts[c] = counts.get(c, 0) + 1
            c_int = max(counts, key=lambda c: counts[c])
            S.dma("sp", bias_int[:].rearrange("p j h q -> p (j h q)"), nab[l, c_int], writes=["na_bint"], stream="ld")
            nak_v = nak.rearrange("h d t -> (h d) t").rearrange("(hp p) t -> p hp t", p=128)
            naq_v = naq.rearrange("(hp two) d t -> d two hp t", two=2)
            for i in range(2):
                S.op("pool", "memset", writes=["na_q%d" % i], ap=qT[i][:].rearrange("p a b q -> p (a b q)"), constant=0.0)
            yna_v = yna.rearrange("(h d) t -> d h t", d=64)
            loaded = [-1]

            def ensure(kt_hi):
                while loaded[0] < kt_hi:
                    kt = loaded[0] + 1
                    s = kt % RING
                    S.dma("sp", kT[s][:], nak_v[:, :, kt * 128:(kt + 1) * 128], writes=["na_k%d" % s], stream="ld")
                    S.dma("sp", vr[s][:], nav[kt * 128:(kt + 1) * 128, :], writes=["na_v%d" % s], stream="ld")
                    loaded[0] = kt

            def loadq(t):
                s = t % 2
                S.dma("sp", qT[s][0:64, :, 0, :], naq_v[:, 0, :, t * 128:(t + 1) * 128], writes=["na_q%d" % s], stream="ld")
                S.dma("sp", qT[s][64:128, :, 1, :], naq_v[:, 1, :, t * 128:(t + 1) * 128], writes=["na_q%d" % s], stream="ld")

            kt0s = [min(max(t - 2, 0), NT - 5) for t in range(NT)]
            ensure(kt0s[0] + 4)
            loadq(0)
            gcount = [0]
            grp = [0]
            steps = []
            for t in range(NT):
                kt0 = kt0s[t]
                cls = cls_of_tile[t]
                for g in range(2):
                    a = (t * 2 + g) % 2
                    for j in range(5):
                        def front(t=t, g=g, j=j, kt0=kt0, cls=cls, holder=None):
                            if g == 0 and j == 0:
                                if t + 1 < NT:
                                    ensure(kt0s[t + 1] + 4)
                                    loadq(t + 1)
                                if cls != c_int:
                                    S.dma("sp", bias_sp[:].rearrange("p j h q -> p (j h q)"), nab[l, cls],
                                          writes=["na_bsp"], stream="ld")
                            bias, bres = (bias_int, "na_bint") if cls == c_int else (bias_sp, "na_bsp")
                            kt = kt0 + j
                            s = kt % RING
                            qs = t % 2
                            i = gcount[0] % NP
                            gcount[0] += 1
                            for h2 in range(2):
                                hp = g * 2 + h2
                                mm(ps_s[i][:, h2 * 2:(h2 + 1) * 2, :], kT[s][:, hp, :], qT[qs][:, hp, :, :], True, True,
                                   ["na_k%d" % s, "na_q%d" % qs], ["na_pss%d" % i])
                            stt(sbf[i][:], ps_s[i][:], 0.125, bias[:, j, g * 4:(g + 1) * 4, :], ALU.mult, ALU.add,
                                ["na_pss%d" % i, bres], ["na_sb%d" % i])
                            act(pt[i][:], sbf[i][:], AF.Exp, ["na_sb%d" % i], ["na_pt%d" % i])
                            return i

                        def back(i, t=t, g=g, j=j, kt0=kt0, a=a):
                            kt = kt0 + j
                            s = kt % RING
                            for hh in range(4):
                                h = g * 4 + hh
                                mm(ps_o[a][:, hh, :], vr[s][:, h * 64:(h + 1) * 64], pt[i][:, hh, :], j == 0 and hh == 0, j == 4,
                                   ["na_v%d" % s, "na_pt%d" % i], ["na_pso%d" % a], skip=True)
                            mm(ps_sum[a][:], ones_bf[:, 0:64], pt[i][:].rearrange("p h q -> p (h q)"), j == 0, j == 4,
                               ["ones_bf", "na_pt%d" % i], ["na_psum%d" % a])
                            if j == 4:
                                def fin_a():
                                    act(lnt[:], ps_sum[a][:], AF.Ln, ["na_psum%d" % a], ["na_ln"])
                                    act(rec[a][:], lnt[:], AF.Exp, ["na_ln"], ["na_rec%d" % a], scale=-1.0)

                                def fin_b():
                                    tt("dve", yst[a][:].rearrange("p h q -> p (h q)"), ps_o[a][:].rearrange("p h q -> p (h q)"),
                                       rec[a][:], ALU.mult, ["na_pso%d" % a, "na_rec%d" % a], ["na_y%d" % a])
                                    S.dma("sp", yna_v[:, g * 4:(g + 1) * 4, t * 128:(t + 1) * 128], yst[a][:],
                                          reads=["na_y%d" % a], writes=[("scr", "yna", t, g)], stream="st")
                                return [(1, fin_a), (4, fin_b)]
                            return None
                        steps.append((front, back))
            run_pipeline(steps, 2, 0)
        S.barrier()

    def stage_gqa(l):
        with ExitStack() as es:
            def sb(name, shape, dt):
                return es.enter_context(nc.sbuf_tensor(name + SUF[0], list(shape), dt))

            def pst(name, shape):
                return es.enter_context(nc.psum_tensor(name + SUF[0], list(shape), F32))
            Kall = sb("gq_K", [128, T], BF16)
            Vall = sb("gq_V", [128, NT, 128], BF16)
            qT = [sb("gq_q%d" % i, [128, 2, 4, 128], BF16) for i in range(2)]
            msk = sb("gq_msk", [128, 2, 512], F32)
            sk = sb("gq_sink", [64, 8], F32)
            esk = sb("gq_esink", [64, 8], F32)
            NP = 3
            sbf = [sb("gq_sb%d" % i, [128, 512], F32) for i in range(NP)]
            pt = [sb("gq_pt%d" % i, [128, 512], BF16) for i in range(NP)]
            rec = [sb("gq_rec%d" % i, [64, 512], F32) for i in range(3)]
            lnt = sb("gq_ln", [64, 512], F32)
            yst = [sb("gq_y%d" % i, [64, 4, 128], BF16) for i in range(3)]
            ps_s = [pst("gq_pss%d" % i, [128, 512]) for i in range(NP)]
            ps_o = [pst("gq_pso%d" % i, [64, 512]) for i in range(3)]
            ps_sum = [pst("gq_psum%d" % i, [64, 512]) for i in range(2)]

            S.dma("sp", Kall[:], gk.rearrange("h d t -> (h d) t"), writes=["gq_K"], stream="ld")
            for i in range(2):
                S.op("pool", "memset", writes=["gq_q%d" % i], ap=qT[i][:].rearrange("p a b q -> p (a b q)"), constant=0.0)
            S.dma("sp", Vall[:], gv, writes=["gq_V"], stream="ld")
            S.dma("sp", msk[:], gmask.rearrange("m p n -> p m n"), writes=["gq_msk"], stream="ld")
            S.dma("sp", sk[:], sinkr[l], writes=["gq_sink"], stream="ld")
            act(esk[:], sk[:], AF.Exp, ["gq_sink"], ["gq_esink"])
            gq_v = gq.rearrange("h d t -> d h t")
            ygqa_v = ygqa.rearrange("(h d) t -> d h t", d=64)

            def loadq(t):
                s = t % 2
                S.dma("sp", qT[s][0:64, 0, :, :], gq_v[:, 0:4, t * 128:(t + 1) * 128], writes=["gq_q%d" % s], stream="ld")
                S.dma("sp", qT[s][64:128, 1, :, :], gq_v[:, 4:8, t * 128:(t + 1) * 128], writes=["gq_q%d" % s], stream="ld")

            loadq(0)
            gcount = [0]
            steps = []
            for t in range(NT):
                for kvh in range(2):
                    a = (t * 2 + kvh) % 3
                    a2 = (t * 2 + kvh) % 2
                    kts = [kt for kt in (t - 1, t, t + 1) if 0 <= kt < NT]
                    for jj, kt in enumerate(kts):
                        first = jj == 0
                        last = jj == len(kts) - 1

                        def front(t=t, kvh=kvh, kt=kt, first=first):
                            if kvh == 0 and first and t + 1 < NT:
                                loadq(t + 1)
                            qs = t % 2
                            i = gcount[0] % NP
                            gcount[0] += 1
                            mm(ps_s[i][:], Kall[:, kt * 128:(kt + 1) * 128],
                               qT[qs][:, kvh, :, :], True, True,
                               ["gq_K", "gq_q%d" % qs], ["gq_pss%d" % i])
                            if kt != t:
                                m = 0 if kt < t else 1
                                tt("dve", sbf[i][:], ps_s[i][:], msk[:, m, :], ALU.add, ["gq_pss%d" % i, "gq_msk"], ["gq_sb%d" % i])
                                act(pt[i][:], sbf[i][:], AF.Exp, ["gq_sb%d" % i], ["gq_pt%d" % i], scale=0.125)
                            else:
                                act(pt[i][:], ps_s[i][:], AF.Exp, ["gq_pss%d" % i], ["gq_pt%d" % i], scale=0.125)
                            return i

                        def back(i, t=t, kvh=kvh, kt=kt, first=first, last=last, a=a, a2=a2):
                            mm(ps_o[a][:], Vall[:, kt, kvh * 64:(kvh + 1) * 64], pt[i][:], first, last,
                               ["gq_V", "gq_pt%d" % i], ["gq_pso%d" % a])
                            mm(ps_sum[a2][:], ones_bf[:, 0:64], pt[i][:], first, last,
                               ["ones_bf", "gq_pt%d" % i], ["gq_psum%d" % a2])
                            if last:
                                def fin_a():
                                    for g in range(4):
                                        h = kvh * 4 + g
                                        act(lnt[:, g * 128:(g + 1) * 128], ps_sum[a2][:, g * 128:(g + 1) * 128], AF.Ln,
                                            ["gq_psum%d" % a2, "gq_esink"], ["gq_ln"], bias=esk[:, h:h + 1], scale=1.0)
                                    act(rec[a][:], lnt[:], AF.Exp, ["gq_ln"], ["gq_rec%d" % a], scale=-1.0)

                                def fin_b():
                                    tt("dve", yst[a][:].rearrange("p h q -> p (h q)"), ps_o[a][:], rec[a][:], ALU.mult,
                                       ["gq_pso%d" % a, "gq_rec%d" % a], ["gq_y%d" % a])
                                    S.dma("sp", ygqa_v[:, kvh * 4:(kvh + 1) * 4, t * 128:(t + 1) * 128], yst[a][:],
                                          reads=["gq_y%d" % a], writes=[("scr", "ygqa", t, kvh)], stream="st")
                                return [(1, fin_a), (4, fin_b)]
                            return None
                        steps.append((front, back))
            run_pipeline(steps, 2, 0)
        S.barrier()

    def stage3(l, xsrc, xdst, W3):
        with ExitStack() as es:
            def sb(name, shape, dt):
                return es.enter_context(nc.sbuf_tensor(name + SUF[0], list(shape), dt))

            def pst(name, shape):
                return es.enter_context(nc.psum_tensor(name + SUF[0], list(shape), F32))
            Wg, Wb, Wo = W3
            g1s = sb("s3_g1", [128, 8], F32)
            bgs = sb("s3_bg", [128, 24], F32)
            xs = [sb("s3_x%d" % i, [128, 8, 512], F32) for i in range(2)]
            xn2 = [sb("s3_xn%d" % i, [128, 8, 512], BF16) for i in range(2)]
            tmp = sb("s3_tmp", [128, 512], F32)
            rstd = sb("s3_rstd", [128, 512], F32)
            ys = [[sb("s3_y%d_%d" % (i, s), [128, 4, 512], BF16) for s in range(2)] for i in range(3)]
            mg = sb("s3_mg", [128, 8, 512], BF16)
            gt = [sb("s3_gt%d" % i, [128, 512], F32) for i in range(3)]
            pr = [sb("s3_pr%d" % i, [128, 512], F32) for i in range(3)]
            m01 = sb("s3_m01", [128, 512], F32)
            xo = [sb("s3_xo%d" % i, [128, 512], F32) for i in range(3)]
            NPS = 6
            pss = [pst("s3_ps%d" % i, [128, 512]) for i in range(NPS)]
            ps_ss = pst("s3_pss", [128, 512])
            cnt = {"ps": 0, "xo": 0}

            def nps():
                i = cnt["ps"] % NPS
                cnt["ps"] += 1
                return pss[i], "s3_ps%d" % i

            S.dma("sp", g1s[:], g1[l], writes=["s3_g1"], stream="ld")
            S.dma("sp", bgs[:], bg[l], writes=["s3_bg"], stream="ld")
            xsrc_f = fm(xsrc)
            xdst_f = fm(xdst)
            ysrc = [fm(yna), fm(ymla), fm(ygqa)]

            def load(b):
                sl = b % 2
                t0 = b * 512
                S.dma("sp", xs[sl][:], xsrc_f[:, :, t0:t0 + 512], writes=["s3_x%d" % sl], stream="ld")
                for i in range(3):
                    S.dma("sp", ys[i][sl][:], ysrc[i][:, :, t0:t0 + 512], writes=["s3_y%d_%d" % (i, sl)], stream="ld")

            def xnorm3(b):
                sl_ = b % 2
                fm_norm(xs[sl_], "s3_x%d" % sl_, 8, 512, g1s, "s3_g1", xn2[sl_], "s3_xn%d" % sl_, ps_ss, "s3_pss", tmp, "s3_tmp",
                        rstd, "s3_rstd", xn2[sl_], "s3_xn%d" % sl_, 1.0 / 32.0)

            load(0)
            xnorm3(0)
            for b in range(NB):
                sl = b % 2
                t0 = b * 512
                if b + 1 < NB:
                    load(b + 1)
                xr = "s3_x%d" % sl
                xn = xn2[sl]
                xnres = "s3_xn%d" % sl
                for m in range(8):
                    if m == 4 and b + 1 < NB:
                        xnorm3(b + 1)
                    for i in range(3):
                        pg, pgr = nps()
                        for kc in range(8):
                            mm(pg[:], Wg[:, kc, i * 1024 + m * 128:i * 1024 + (m + 1) * 128], xn[:, kc, :],
                               kc == 0, kc == 7, [xnres], [pgr])
                        py, pyr = nps()
                        yr = "s3_y%d_%d" % (i, sl)
                        for kc in range(4):
                            mm(py[:], Wb[:, i, kc, m * 128:(m + 1) * 128], ys[i][sl][:, kc, :], kc == 0, kc == 3,
                               [yr], [pyr])
                        act(gt[i][:], pg[:], AF.Sigmoid, [pgr, "s3_bg"], ["s3_gt%d" % i],
                            bias=bgs[:, i * 8 + m:i * 8 + m + 1], scale=1.0)
                        tt("dve", pr[i][:], gt[i][:], py[:], ALU.mult, ["s3_gt%d" % i, pyr], ["s3_pr%d" % i])
                    tt("pool", m01[:], pr[0][:], pr[1][:], ALU.add, ["s3_pr0", "s3_pr1"], ["s3_m01"])
                    tt("pool", mg[:, m, :], m01[:], pr[2][:], ALU.add, ["s3_m01", "s3_pr2"], [("s3_mg", m)])
                for m in range(8):
                    po, por = nps()
                    for kc in range(8):
                        mm(po[:], Wo[:, kc, m * 128:(m + 1) * 128], mg[:, kc, :], kc == 0, kc == 7,
                           [("s3_mg", kc)], [por])
                    i = cnt["xo"] % 3
                    cnt["xo"] += 1
                    tt("dve", xo[i][:], po[:], xs[sl][:, m, :], ALU.add, [por, xr], ["s3_xo%d" % i])
                    S.dma("sp", xdst_f[:, m, t0:t0 + 512], xo[i][:], reads=["s3_xo%d" % i],
                          writes=[("scr", "x1", b, m)], stream="st")
        S.barrier()

    def stage_ffn(l, xsrc, xdst):
        NOUT = 382
        FW = 384
        wins = []
        o0 = 0
        while o0 < T:
            o1 = min(o0 + NOUT, T)
            u0 = max(o0 - 1, 0)
            u1 = min(o1 + 1, T)
            wins.append((o0, o1, u0, u1))
            o0 = o1
        with ExitStack() as es:
            def sb(name, shape, dt):
                return es.enter_context(nc.sbuf_tensor(name + SUF[0], list(shape), dt))

            def pst(name, shape):
                return es.enter_context(nc.psum_tensor(name + SUF[0], list(shape), F32))
            Wu = sb("ff_Wu", [128, 8, 2 * D_FF], BF16)
            Wd = sb("ff_Wd", [128, 22, 1024], BF16)
            g2s = sb("ff_g2", [128, 8], F32)
            cws = sb("ff_cw", [128, 44, 3], F32)
            cbs = sb("ff_cb", [128, 44], F32)
            xs = sb("ff_x", [128, 8, FW], F32)
            xn = sb("ff_xn", [128, 8, FW], BF16)
            tmp = sb("ff_tmp", [128, FW], F32)
            rstd = sb("ff_rstd", [128, FW], F32)
            hm = sb("ff_hm", [128, 22, FW], BF16)
            av = [sb("ff_a%d" % i, [128, FW], F32) for i in range(3)]
            gvv = [sb("ff_gv%d" % i, [128, FW], F32) for i in range(3)]
            gg = [sb("ff_gg%d" % i, [128, FW], F32) for i in range(3)]
            xres = [sb("ff_xr%d" % i, [128, FW], F32) for i in range(2)]
            xo = [sb("ff_xo%d" % i, [128, FW], F32) for i in range(2)]
            NPS = 6
            pss = [pst("ff_ps%d" % i, [128, 512]) for i in range(NPS)]
            ps_ss = pst("ff_pss", [128, 512])
            cnt = {"ps": 0, "c": 0, "xo": 0}

            def nps():
                i = cnt["ps"] % NPS
                cnt["ps"] += 1
                return pss[i], "ff_ps%d" % i

            wu_l = w_up[l].rearrange("(kc p) n -> p kc n", p=128)
            for c in range(22):
                for cc in (c, 22 + c):
                    S.dma("pool", Wu[:, :, cc * 128:(cc + 1) * 128], wu_l[:, :, cc * 128:(cc + 1) * 128],
                          writes=[("ff_Wu", cc)], stream="w")
            wd_l = w_down[l].rearrange("(kc p) n -> p kc n", p=128)
            for kc in range(22):
                S.dma("pool", Wd[:, kc, :], wd_l[:, kc, :], writes=[("ff_Wd", kc)], stream="w")
            S.dma("sp", g2s[:], g2[l], writes=["ff_g2"], stream="ld")
            S.dma("sp", cws[:].rearrange("p c j -> p (c j)"), cw[l], writes=["ff_cw"], stream="ld")
            S.dma("sp", cbs[:], cb[l], writes=["ff_cb"], stream="ld")
            xsrc_f = fm(xsrc)
            xdst_f = fm(xdst)

            def load(w):
                o0, o1, u0, u1 = wins[w]
                S.dma("sp", xs[:, :, 0:u1 - u0], xsrc_f[:, :, u0:u1], writes=["ff_x"], stream="ld")

            load(0)
            for w, (o0, o1, u0, u1) in enumerate(wins):
                nu = u1 - u0
                no = o1 - o0
                uoff = o0 - u0
                if w == 0:
                    fm_norm(xs, "ff_x", 8, nu, g2s, "ff_g2", xn, "ff_xn", ps_ss, "ff_pss",
                            tmp, "ff_tmp", rstd, "ff_rstd", xn, "ff_xn", 1.0 / 32.0)
                if w + 1 < len(wins):
                    load(w + 1)
                defer = []
                for c in range(22):
                    k = cnt["c"] % 3
                    cnt["c"] += 1
                    outs = []
                    for (cc, dst, dres) in ((c, av[k], "ff_a%d" % k), (22 + c, gvv[k], "ff_gv%d" % k)):
                        ps, psr = nps()
                        for kc in range(8):
                            mm(ps[:, 0:nu], Wu[:, kc, cc * 128:(cc + 1) * 128], xn[:, kc, 0:nu], kc == 0, kc == 7,
                               [("ff_Wu", cc), "ff_xn"], [psr])
                        act(dst[:, 0:no], ps[:, uoff:uoff + no], AF.Identity, [psr, "ff_cw", "ff_cb"], [dres],
                            bias=cbs[:, cc:cc + 1], scale=cws[:, cc, 1:2])
                        lo = 0 if uoff >= 1 else 1
                        stt(dst[:, lo:no], ps[:, lo + uoff - 1:no + uoff - 1], cws[:, cc, 0:1], dst[:, lo:no],
                            ALU.mult, ALU.add, [psr, "ff_cw", dres], [dres])
                        hi = min(no, nu - uoff - 1)
                        stt(dst[:, 0:hi], ps[:, uoff + 1:uoff + 1 + hi], cws[:, cc, 2:3], dst[:, 0:hi],
                            ALU.mult, ALU.add, [psr, "ff_cw", dres], [dres])
                    def gl(c=c, k=k, no=no):
                        act(gg[k][:, 0:no], gvv[k][:, 0:no], AF.Gelu_apprx_tanh, ["ff_gv%d" % k], ["ff_gg%d" % k])
                        tt("pool", hm[:, c, 0:no], gg[k][:, 0:no], av[k][:, 0:no], ALU.mult,
                           ["ff_gg%d" % k, "ff_a%d" % k], [("ff_hm", c)])
                    while defer:
                        defer.pop(0)()
                    defer.append(gl)
                while defer:
                    defer.pop(0)()
                if w + 1 < len(wins):
                    n_o0, n_o1, n_u0, n_u1 = wins[w + 1]
                    fm_norm(xs, "ff_x", 8, n_u1 - n_u0, g2s, "ff_g2", xn, "ff_xn", ps_ss, "ff_pss",
                            tmp, "ff_tmp", rstd, "ff_rstd", xn, "ff_xn", 1.0 / 32.0)
                for m in range(8):
                    i = cnt["xo"] % 2
                    cnt["xo"] += 1
                    S.dma("sp", xres[i][:, 0:no], xsrc_f[:, m, o0:o1], writes=["ff_xr%d" % i], stream="ld2")
                    pd, pdr = nps()
                    for c in range(22):
                        mm(pd[:, 0:no], Wd[:, c, m * 128:(m + 1) * 128], hm[:, c, 0:no], c == 0, c == 21,
                           [("ff_Wd", c), ("ff_hm", c)], [pdr])
                    tt("dve", xo[i][:, 0:no], pd[:, 0:no], xres[i][:, 0:no], ALU.add, [pdr, "ff_xr%d" % i], ["ff_xo%d" % i])
                    S.dma("sp", xdst_f[:, m, o0:o1], xo[i][:, 0:no], reads=["ff_xo%d" % i],
                          writes=[("scr", "x2", w, m)], stream="st")
        S.barrier()

    def stage_final(xsrc):
        with ExitStack() as es:
            def sb(name, shape, dt):
                return es.enter_context(nc.sbuf_tensor(name + SUF[0], list(shape), dt))
            gfs = sb("fn_g", [128, 8], F32)
            xs = [sb("fn_x%d" % i, [128, 8, 512], F32) for i in range(2)]
            sq = sb("fn_sq", [128, 8, 512], BF16)
            tmp = sb("fn_tmp", [128, 512], F32)
            rstd = sb("fn_rstd", [128, 512], F32)
            xo = [sb("fn_xo%d" % i, [128, 8, 512], F32) for i in range(2)]
            ps_ss = es.enter_context(nc.psum_tensor("fn_pss" + SUF[0], [128, 512], F32))
            S.dma("sp", gfs[:], gf, writes=["fn_g"], stream="ld")
            xsrc_f = fm(xsrc)
            y_f = fm(yT)
            S.dma("sp", xs[0][:], xsrc_f[:, :, 0:512], writes=["fn_x0"], stream="ld")
            for b in range(NB):
                sl = b % 2
                t0 = b * 512
                if b + 1 < NB:
                    S.dma("sp", xs[1 - sl][:], xsrc_f[:, :, t0 + 512:t0 + 1024], writes=["fn_x%d" % (1 - sl)], stream="ld")
                fm_norm(xs[sl], "fn_x%d" % sl, 8, 512, gfs, "fn_g", sq, "fn_sq", ps_ss, "fn_pss", tmp, "fn_tmp",
                        rstd, "fn_rstd", xo[sl], "fn_xo%d" % sl, 1.0 / 32.0)
                S.dma("sp", y_f[:, :, t0:t0 + 512], xo[sl][:], reads=["fn_xo%d" % sl], writes=[("out", b)], stream="st")
        S.barrier()

    xcur = xT
    for l in range(L):
        SUF[0] = "_L%d" % l
        if stages is None or "s1" in stages:
            stage1(l, xcur)
        with ExitStack() as es3:
            Wg = es3.enter_context(nc.sbuf_tensor("s3_Wg" + SUF[0], [128, 8, 3072], BF16))
            Wb = es3.enter_context(nc.sbuf_tensor("s3_Wb" + SUF[0], [128, 3, 4, 1024], BF16))
            Wo = es3.enter_context(nc.sbuf_tensor("s3_Wo" + SUF[0], [128, 8, 1024], BF16))
            def pre3(l=l, Wg=Wg, Wb=Wb, Wo=Wo):
                win_l = w_in[l].rearrange("(kc p) n -> p kc n", p=128)
                for kc in range(8):
                    S.dma("pool", Wg[:, kc, :], win_l[:, kc, 3008:6080], writes=["s3_Wg"], stream="w")
                for i in range(3):
                    wv = w_br[i][l].rearrange("(kc p) n -> p kc n", p=128)
                    for kc in range(4):
                        S.dma("pool", Wb[:, i, kc, :], wv[:, kc, :], writes=["s3_Wb"], stream="w")
                wo_l = w_out[l].rearrange("(kc p) n -> p kc n", p=128)
                for kc in range(8):
                    S.dma("pool", Wo[:, kc, :], wo_l[:, kc, :], writes=["s3_Wo"], stream="w")
            if stages is None or "mla" in stages:
                stage_mla(l, pre3 if (stages is None or "s3" in stages) else None)
            elif stages is not None and "s3" in stages:
                pre3()
            if stages is None or "na" in stages:
                stage_na(l)
            if stages is None or "gqa" in stages:
                stage_gqa(l)
            if stages is None or "s3" in stages:
                stage3(l, xcur, XA, (Wg, Wb, Wo))
        if stages is None or "ffn" in stages:
            stage_ffn(l, XA, XB)
        xcur = XB
    if stages is None or "fin" in stages:
        stage_final(XB)
    S.barrier()
    S.emit()
    return nc


def prep_shared(inputs, T, L):
    f = lambda a: np.ascontiguousarray(np.asarray(a, dtype=np.float32))
    cls_of_tile, classes = na_classes(T)
    C, Sn = rope_tables_fm(T)
    sh = {
        "w_in": f(inputs["w_in"]), "mla_w_uq": f(inputs["mla_w_uq"]), "mla_w_ukv": f(inputs["mla_w_ukv"]),
        "w_br_na": f(inputs["w_br_na"]), "w_br_mla": f(inputs["w_br_mla"]), "w_br_gqa": f(inputs["w_br_gqa"]),
        "w_out": f(inputs["w_out"]), "w_up": f(inputs["w_up"]), "w_down": f(inputs["w_down"]),
        "g1": np.stack([pcol(f(inputs["norm1_g"])[l], 8) for l in range(L)]),
        "g2": np.stack([pcol(f(inputs["norm2_g"])[l], 8) for l in range(L)]),
        "gf": pcol(f(inputs["final_g"]), 8),
        "bg": np.stack([pcol(f(inputs["b_gate"])[l], 24) for l in range(L)]),
        "gqa": np.stack([pcol(f(inputs["mla_qa_g"])[l], 3) for l in range(L)]),
        "gkva": np.stack([pcol(f(inputs["mla_kva_g"])[l], 2) for l in range(L)]),
        "sinkr": np.ascontiguousarray(np.broadcast_to(f(inputs["gqa_sink"])[:, None, :], (L, 64, 8))),
        "cw": np.stack([np.ascontiguousarray(
            f(inputs["conv_w"])[l].reshape(3, 44, 128).transpose(2, 1, 0).reshape(128, 132)) for l in range(L)]),
        "cb": np.stack([pcol(f(inputs["conv_b"])[l], 44) for l in range(L)]),
        "ropeC": C, "ropeS": Sn,
        "nab": na_bias_tables(f(inputs["na_rpb"]), classes).reshape(L, len(classes), 128, 5 * 8 * 128),
        "gmask": gqa_masks(),
    }
    return sh, cls_of_tile, len(classes)


_CACHE = {}


def kernel(**inputs):
    x = np.asarray(inputs["x"], dtype=np.float32)
    B, T, _ = x.shape
    L = np.asarray(inputs["w_in"]).shape[0]
    sh, cls_of_tile, ncls = prep_shared(inputs, T, L)
    key = (T, L, ncls)
    if key not in _CACHE:
        _CACHE[key] = build(T, L, ncls, cls_of_tile)
    nc = _CACHE[key]
    in_maps = []
    for b in range(B):
        m = dict(sh)
        m["xT"] = np.ascontiguousarray(x[b].T)
        in_maps.append(m)
    res = run_bass_kernel_spmd(nc, in_maps, core_ids=list(range(B)))
    out = np.empty((B, T, D), np.float32)
    for b in range(B):
        out[b] = np.asarray(res.results[b]["yT"]).T
    return out
```

```python
import numpy as np
import ml_dtypes
from contextlib import ExitStack
import concourse.bass as bass
import concourse.mybir as mybir
from concourse.bass_utils import run_bass_kernel_spmd

F32 = mybir.dt.float32
BF16 = mybir.dt.bfloat16
AF = mybir.ActivationFunctionType
ALU = mybir.AluOpType

D = 1024
L_FULL = 2
T_FULL = 8192
GRID_W = 64
EPS = 1e-6
D_FF = 2816
IN_W = 6080
NEG = -30000.0
COMPUTE = ("pe", "act", "dve", "pool")


class Sched:
    def __init__(self, nc, n_dma_sems=8):
        self.nc = nc
        self.ins = {e: [] for e in ("pe", "act", "dve", "pool", "sp")}
        self.known = {e: {} for e in self.ins}
        self.res = {}
        self.milestones = {e: set() for e in COMPUTE}
        self.streams = {}
        self.n_dma_sems = n_dma_sems
        self.dma_sem_keys = []
        self.dma_latest = {}
        self.last_compute = {e: 0 for e in COMPUTE}

    def _r(self, key):
        r = self.res.get(key)
        if r is None:
            r = ({}, {})
            self.res[key] = r
        return r

    def _deps(self, eng, reads, writes, is_dma):
        deps = {}
        me = ("E", eng)

        def add(k, v, same_ok):
            if (not is_dma) and same_ok and k == me:
                return
            if deps.get(k, 0) < v:
                deps[k] = v

        for r in reads:
            W, R = self._r(r)
            for k, v in W.items():
                add(k, v, False)
        for w in writes:
            W, R = self._r(w)
            for k, v in W.items():
                add(k, v, True)
            for k, v in R.items():
                add(k, v, True)
        out = []
        kn = self.known[eng]
        for k, v in deps.items():
            if kn.get(k, 0) >= v:
                continue
            kn[k] = v
            out.append((k, v))
            if k[0] == "E":
                self.milestones[k[1]].add(v)
        return out

    def _update(self, tok, reads, writes):
        k, v = tok
        for r in reads:
            W, R = self._r(r)
            if R.get(k, 0) < v:
                R[k] = v
        for w in writes:
            W, R = self._r(w)
            W.clear()
            R.clear()
            W[k] = v

    def op(self, eng, name, reads=(), writes=(), **kw):
        waits = self._deps(eng, reads, writes, False)
        self.ins[eng].append((name, kw, waits, None))
        tok = (("E", eng), len(self.ins[eng]))
        self.last_compute[eng] = len(self.ins[eng])
        self._update(tok, reads, writes)

    def dma(self, q, out, in_, reads=(), writes=(), stream="ld"):
        st = self.streams.get(stream)
        if st is None:
            base = len(self.dma_sem_keys)
            keys = [("D", base + i) for i in range(self.n_dma_sems)]
            self.dma_sem_keys += keys
            st = {"keys": keys, "n": 0}
            self.streams[stream] = st
        i = st["n"]
        st["n"] += 1
        K = len(st["keys"])
        key = st["keys"][i % K]
        val = 16 * (i // K + 1)
        waits = self._deps(q, reads, writes, True)
        if val > 16 and self.known[q].get(key, 0) < val - 16:
            self.known[q][key] = val - 16
            waits.append((key, val - 16))
        self.ins[q].append(("dma_start", dict(out=out, in_=in_), waits, (key, 16)))
        self.dma_latest[key] = val
        self._update((key, val), reads, writes)

    def barrier(self):
        toks = []
        for e in COMPUTE:
            if self.last_compute[e]:
                toks.append((("E", e), self.last_compute[e]))
        for k, v in self.dma_latest.items():
            toks.append((k, v))
        for e in self.ins:
            waits = []
            for k, v in toks:
                if k == ("E", e):
                    continue
                if self.known[e].get(k, 0) >= v:
                    continue
                self.known[e][k] = v
                waits.append((k, v))
                if k[0] == "E":
                    self.milestones[k[1]].add(v)
            if waits:
                self.ins[e].append((None, None, waits, None))
        self.res = {}

    def emit(self):
        nc = self.nc
        sems = {}
        for e in COMPUTE:
            sems[("E", e)] = nc.alloc_semaphore("sem_" + e)
        for k in self.dma_sem_keys:
            sems[k] = nc.alloc_semaphore("semd%d" % k[1])
        rank = {}
        for e in COMPUTE:
            ms = sorted(self.milestones[e])
            rank[e] = {v: i + 1 for i, v in enumerate(ms)}

        def run(engine, lst, ename):
            my = rank.get(ename, {})
            mysem = sems.get(("E", ename))
            for idx, (name, kw, waits, dinc) in enumerate(lst):
                for k, v in waits:
                    if k[0] == "E":
                        engine.wait_ge(sems[k], rank[k[1]][v])
                    else:
                        engine.wait_ge(sems[k], v)
                if name is None:
                    continue
                r = getattr(engine, name)(**kw)
                if dinc is not None:
                    r.then_inc(sems[dinc[0]], dinc[1])
                elif (idx + 1) in my:
                    r.then_inc(mysem, 1)

        with nc.Block() as block:
            @block.tensor
            def _(e):
                run(e, self.ins["pe"], "pe")

            @block.scalar
            def _(e):
                run(e, self.ins["act"], "act")

            @block.vector
            def _(e):
                run(e, self.ins["dve"], "dve")

            @block.gpsimd
            def _(e):
                run(e, self.ins["pool"], "pool")

            @block.sync
            def _(e):
                run(e, self.ins["sp"], "sp")


def rope_tables_fm(T):
    inv = 1.0 / (10000.0 ** (np.arange(0, 64, 2, dtype=np.float32) / 64.0))
    ang = np.arange(T, dtype=np.float32)[:, None] * inv[None, :]
    c = np.cos(ang).astype(np.float32).T
    s = np.sin(ang).astype(np.float32).T
    C = np.concatenate([c, c, c, c], axis=0)
    S = np.concatenate([-s, s, -s, s], axis=0)
    return np.ascontiguousarray(C), np.ascontiguousarray(S)


def na_classes(T):
    rows = T // GRID_W
    NT = T // 128
    kk = np.arange(128)
    qq = np.arange(128)
    cls_of_tile = []
    classes = []
    keys = {}
    for t in range(NT):
        kt0 = min(max(t - 2, 0), NT - 5)
        r = 2 * t + qq // 64
        cq = qq % 64
        r_start = np.clip(r - 4, 0, rows - 8)
        c_start = np.clip(cq - 8, 0, GRID_W - 16)
        valid = np.zeros((128, 5, 128), bool)
        ri = np.zeros((128, 5, 128), np.int64)
        ci = np.zeros((128, 5, 128), np.int64)
        for j in range(5):
            kt = kt0 + j
            rk = 2 * kt + kk // 64
            ck = kk % 64
            v = ((rk[:, None] >= r_start[None, :]) & (rk[:, None] < r_start[None, :] + 8)
                 & (ck[:, None] >= c_start[None, :]) & (ck[:, None] < c_start[None, :] + 16))
            valid[:, j, :] = v
            ri[:, j, :] = np.clip(rk[:, None] - r[None, :] + 7, 0, 14)
            ci[:, j, :] = np.clip(ck[:, None] - cq[None, :] + 15, 0, 30)
        key = (valid.tobytes(), ri.tobytes(), ci.tobytes())
        if key not in keys:
            keys[key] = len(classes)
            classes.append((valid, ri, ci))
        cls_of_tile.append(keys[key])
    return cls_of_tile, classes


def na_bias_tables(rpb, classes):
    Lx = rpb.shape[0]
    out = np.empty((Lx, len(classes), 128, 5, 8, 128), np.float32)
    for c, (valid, ri, ci) in enumerate(classes):
        g = rpb[:, :, ri, ci]
        g = np.where(valid[None, None], g, np.float32(NEG))
        out[:, c] = g.transpose(0, 2, 3, 1, 4)
    return out


def gqa_masks():
    kk = np.arange(128)[:, None]
    qq = np.arange(128)[None, :]
    m_prev = np.where(qq <= kk, 0.0, -1.0e6).astype(np.float32)
    m_next = np.where(kk <= qq, 0.0, -1.0e6).astype(np.float32)
    m = np.stack([np.tile(m_prev, (1, 4)), np.tile(m_next, (1, 4))], axis=0)
    return np.ascontiguousarray(m)


def pcol(v, nchunk):
    return np.ascontiguousarray(v.reshape(nchunk, 128).T)


def build(T, L, ncls, cls_of_tile, debug=False, stages=None):
    NB = T // 512
    NT = T // 128
    nc = bass.Bass("TRN2", target_bir_lowering=False)
    S = Sched(nc)

    def din(name, shape, dt=F32):
        return nc.dram_tensor(name, list(shape), dt, kind="ExternalInput").ap()

    def dscr(name, shape, dt):
        return nc.dram_tensor(name, list(shape), dt, kind=("ExternalOutput" if debug else "Internal")).ap()

    xT = din("xT", [D, T])
    w_in = din("w_in", [L, D, IN_W])
    w_uq = din("mla_w_uq", [L, 384, 768])
    w_ukv = din("mla_w_ukv", [L, 256, 1024])
    w_br = [din("w_br_na", [L, 512, D]), din("w_br_mla", [L, 512, D]), din("w_br_gqa", [L, 512, D])]
    w_out = din("w_out", [L, D, D])
    w_up = din("w_up", [L, D, 2 * D_FF])
    w_down = din("w_down", [L, D_FF, D])
    g1 = din("g1", [L, 128, 8])
    g2 = din("g2", [L, 128, 8])
    gf = din("gf", [128, 8])
    bg = din("bg", [L, 128, 24])
    gqa_ = din("gqa", [L, 128, 3])
    gkva = din("gkva", [L, 128, 2])
    sinkr = din("sinkr", [L, 64, 8])
    cw = din("cw", [L, 128, 44 * 3])
    cb = din("cb", [L, 128, 44])
    ropeC = din("ropeC", [128, T])
    ropeS = din("ropeS", [128, T])
    nab = din("nab", [L, ncls, 128, 5 * 8 * 128])
    gmask = din("gmask", [2, 128, 512])

    yT = nc.dram_tensor("yT", [D, T], F32, kind="ExternalOutput").ap()

    XA = dscr("XA", [D, T], F32)
    XB = dscr("XB", [D, T], F32)
    naq = dscr("naq", [8, 64, T], BF16)
    nak = dscr("nak", [8, 64, T], BF16)
    nav = dscr("nav", [T, 512], BF16)
    mqn = dscr("mqn", [4, 128, T], BF16)
    mqp = dscr("mqp", [4, 64, T], BF16)
    mkn = dscr("mkn", [4, 128, T], BF16)
    mkp = dscr("mkp", [64, T], BF16)
    mv = dscr("mv", [4, 128, NT, 128], BF16)
    gq = dscr("gq", [8, 64, T], BF16)
    gk = dscr("gk", [2, 64, T], BF16)
    gv = dscr("gv", [128, NT, 128], BF16)
    yna = dscr("yna", [512, T], BF16)
    ymla = dscr("ymla", [512, T], BF16)
    ygqa = dscr("ygqa", [512, T], BF16)

    def fm(ap):
        return ap.rearrange("(kc p) t -> p kc t", p=128)

    def mm(out, lhsT, rhs, start, stop, reads, writes, skip=False):
        kw = dict(out=out, lhsT=lhsT, rhs=rhs, start=start, stop=stop)
        if skip:
            kw["skip_group_check"] = True
        S.op("pe", "matmul", reads=reads, writes=writes, **kw)

    def act(out, in_, func, reads, writes, **kw):
        S.op("act", "activation", reads=reads, writes=writes, out=out, in_=in_, func=func, **kw)

    def tt(eng, out, in0, in1, op, reads, writes):
        S.op(eng, "tensor_tensor", reads=reads, writes=writes, out=out, in0=in0, in1=in1, op=op)

    def ts(eng, out, in0, s1, s2, op0, op1, reads, writes):
        kw = dict(out=out, in0=in0, scalar1=s1, scalar2=s2, op0=op0)
        if op1 is not None:
            kw["op1"] = op1
        S.op(eng, "tensor_scalar", reads=reads, writes=writes, **kw)

    def stt(out, in0, scalar, in1, op0, op1, reads, writes):
        S.op("dve", "scalar_tensor_tensor", reads=reads, writes=writes, out=out, in0=in0, scalar=scalar,
             in1=in1, op0=op0, op1=op1)

    def cp(eng, out, in_, reads, writes):
        if eng == "act":
            act(out, in_, AF.Copy, reads, writes)
        else:
            S.op(eng, "tensor_copy", reads=reads, writes=writes, out=out, in_=in_)

    def recip(out, in_, reads, writes):
        S.op("dve", "reciprocal", reads=reads, writes=writes, out=out, in_=in_)

    evac_rr = [0]
    SUF = [""]

    def evac(out, in_, reads, writes):
        evac_rr[0] ^= 1
        cp("act" if evac_rr[0] else "dve", out, in_, reads, writes)

    ones_bf = nc.alloc_sbuf_tensor("ones_bf", [128, 128], BF16)
    ones_f = nc.alloc_sbuf_tensor("ones_f", [128, 128], F32)
    S.op("pool", "memset", writes=["ones_bf"], ap=ones_bf[:], constant=1.0)
    S.op("pool", "memset", writes=["ones_f"], ap=ones_f[:], constant=1.0)

    def fm_norm(xs, xres, nch, n, gcol, gres, sq, sqres, ps, psres, tmp, tmpres, rstd, rstdres, xn, xnres, inv_sqrt_dim):
        act(sq[:, 0:nch, 0:n], xs[:, 0:nch, 0:n], AF.Square, [xres], [sqres], scale=inv_sqrt_dim)
        for c in range(nch):
            mm(ps[:, 0:n], ones_bf[:], sq[:, c, 0:n], c == 0, c == nch - 1, ["ones_bf", sqres], [psres])
        act(tmp[:, 0:n], ps[:, 0:n], AF.Ln, [psres], [tmpres], bias=EPS, scale=1.0)
        act(rstd[:, 0:n], tmp[:, 0:n], AF.Exp, [tmpres], [rstdres], scale=-0.5)
        for c in range(nch):
            stt(xn[:, c, 0:n], xs[:, c, 0:n], gcol[:, c:c + 1], rstd[:, 0:n], ALU.mult, ALU.mult,
                [xres, gres, rstdres], [xnres])

    def run_pipeline(steps, depth, defer):
        n = len(steps)
        ring = {}
        pend = []
        for idx in range(n + depth):
            if idx < n:
                ring[idx] = steps[idx][0]()
            if idx >= depth:
                fl = steps[idx - depth][1](ring.pop(idx - depth))
                if fl is not None:
                    for d, f in fl:
                        pend.append((idx + d, f))
                    pend.sort(key=lambda x: x[0])
            while pend and pend[0][0] <= idx:
                pend.pop(0)[1]()
        for _, f in pend:
            f()

    def stage1(l, xsrc):
        with ExitStack() as es:
            def sb(name, shape, dt):
                return es.enter_context(nc.sbuf_tensor(name + SUF[0], list(shape), dt))

            def pst(name, shape):
                return es.enter_context(nc.psum_tensor(name + SUF[0], list(shape), F32))
            Wa = sb("s1_Wa", [128, 8, 3008], BF16)
            Wsw = sb("s1_Wsw", [128, 8, 704], BF16)
            Wq = sb("s1_Wq", [128, 3, 768], BF16)
            Wqp = sb("s1_Wqp", [128, 3, 256], BF16)
            Wqps = sb("s1_Wqps", [128, 3, 256], BF16)
            Wkv = sb("s1_Wkv", [128, 2, 1024], BF16)
            Wv = sb("s1_Wv", [128, 2, 512], BF16)
            g1s = sb("s1_g1", [128, 8], F32)
            gqs = sb("s1_gq", [128, 3], F32)
            gks = sb("s1_gk", [128, 2], F32)
            xs = [sb("s1_x%d" % i, [128, 8, 512], F32) for i in range(2)]
            xn = [sb("s1_xn%d" % i, [128, 8, 512], BF16) for i in range(2)]
            sq = sb("s1_sq", [128, 8, 512], BF16)
            tmp = sb("s1_tmp", [128, 512], F32)
            rstd = sb("s1_rstd", [128, 512], F32)
            Cs = [sb("s1_C%d" % i, [128, 512], F32) for i in range(2)]
            Ss = [sb("s1_S%d" % i, [128, 512], F32) for i in range(2)]
            cq = sb("s1_cq", [128, 3, 512], F32)
            ckv = sb("s1_ckv", [128, 2, 512], F32)
            cqn = sb("s1_cqn", [128, 3, 512], BF16)
            ckvn = sb("s1_ckvn", [128, 2, 512], BF16)
            NST = 6
            stg = [sb("s1_stg%d" % i, [128, 512], BF16) for i in range(NST)]
            t1 = [sb("s1_t1_%d" % i, [128, 512], F32) for i in range(2)]
            t2 = [sb("s1_t2_%d" % i, [128, 512], F32) for i in range(2)]
            NPS = 5
            pss = [pst("s1_ps%d" % i, [128, 512]) for i in range(NPS)]
            ps_ss = pst("s1_pss", [128, 512])
            cnt = {"ps": 0, "stg": 0, "t": 0}

            def nps():
                i = cnt["ps"] % NPS
                cnt["ps"] += 1
                return pss[i], "s1_ps%d" % i

            def nstg():
                i = cnt["stg"] % NST
                cnt["stg"] += 1
                return stg[i], "s1_stg%d" % i

            win_l = w_in[l].rearrange("(kc p) n -> p kc n", p=128)
            WA_GROUPS = [(1536, 2176), (0, 512), (512, 1024), (2176, 2880), (1024, 1536), (2880, 3008)]

            def wa_res(c0):
                for gi, (a0, a1) in enumerate(WA_GROUPS):
                    if a0 <= c0 < a1:
                        return ("s1_Wa", gi)
                raise ValueError(c0)

            def load_wa(gi):
                a0, a1 = WA_GROUPS[gi]
                S.dma("pool", Wa[:, :, a0:a1], win_l[:, :, a0:a1], writes=[("s1_Wa", gi)], stream="w")
            load_wa(0)
            load_wa(1)
            load_wa(2)
            load_wa(3)
            load_wa(4)
            load_wa(5)
            wuq_l = w_uq[l].rearrange("(kc p) n -> p kc n", p=128)
            S.dma("pool", Wq[:], wuq_l, writes=["s1_Wq"], stream="w")
            wukv_l = w_ukv[l].rearrange("(kc p) n -> p kc n", p=128)
            S.dma("pool", Wkv[:], wukv_l, writes=["s1_Wkv"], stream="w")
            for (src0, nh, dst0) in ((2176, 1, 0), (2240, 8, 64), (2752, 2, 576)):
                srcv = Wa[:, :, src0:src0 + nh * 64].rearrange("p k (h two r) -> p k h two r", two=2, r=32)
                dstv = Wsw[:, :, dst0:dst0 + nh * 64].rearrange("p k (h two r) -> p k h two r", two=2, r=32)
                S.op("pool", "tensor_copy", reads=[("s1_Wa", 3)], writes=["s1_Wsw"], out=dstv[:, :, :, 0, :], in_=srcv[:, :, :, 1, :])
                S.op("pool", "tensor_copy", reads=[("s1_Wa", 3)], writes=["s1_Wsw"], out=dstv[:, :, :, 1, :], in_=srcv[:, :, :, 0, :])
            wq4 = Wq[:].rearrange("p k (h c) -> p k h c", c=192)
            S.op("pool", "tensor_copy", reads=["s1_Wq"], writes=["s1_Wqp"],
                 out=Wqp[:].rearrange("p k (h c) -> p k h c", c=64), in_=wq4[:, :, :, 128:192])
            wqps4 = Wqps[:].rearrange("p k (h c) -> p k h c", c=64)
            S.op("pool", "tensor_copy", reads=["s1_Wq"], writes=["s1_Wqps"], out=wqps4[:, :, :, 0:32], in_=wq4[:, :, :, 160:192])
            S.op("pool", "tensor_copy", reads=["s1_Wq"], writes=["s1_Wqps"], out=wqps4[:, :, :, 32:64], in_=wq4[:, :, :, 128:160])
            S.op("pool", "tensor_copy", reads=["s1_Wkv"], writes=["s1_Wv"],
                 out=Wv[:].rearrange("p k (h c) -> p k h c", c=128),
                 in_=Wkv[:].rearrange("p k (h c) -> p k h c", c=256)[:, :, :, 128:256])
            S.dma("sp", g1s[:], g1[l], writes=["s1_g1"], stream="ld")
            S.dma("sp", gqs[:], gqa_[l], writes=["s1_gq"], stream="ld")
            S.dma("sp", gks[:], gkva[l], writes=["s1_gk"], stream="ld")

            xsrc_f = fm(xsrc)
            naq_f = naq.rearrange("h d t -> (h d) t")
            nak_f = nak.rearrange("h d t -> (h d) t")
            mqp_f = mqp.rearrange("h d t -> (h d) t")
            gq_f = gq.rearrange("h d t -> (h d) t")
            gk_f = gk.rearrange("h d t -> (h d) t")

            def load(b):
                sl = b % 2
                t0 = b * 512
                S.dma("sp", xs[sl][:], xsrc_f[:, :, t0:t0 + 512], reads=[("X", id(xsrc), b)], writes=["s1_x%d" % sl], stream="ld")
                S.dma("sp", Cs[sl][:], ropeC[:, t0:t0 + 512], writes=["s1_C%d" % sl], stream="ld")
                S.dma("sp", Ss[sl][:], ropeS[:, t0:t0 + 512], writes=["s1_S%d" % sl], stream="ld")

            def store(dst, src, srcres, dstres):
                S.dma("sp", dst, src, reads=[srcres], writes=[dstres], stream="st")

            def normA(xin, nch, sqb, sqres, xres, scl):
                act(sqb[:, 0:nch, :], xin[:, 0:nch, :], AF.Square, [xres], [sqres], scale=scl)

            def normB(sqb, nch, ps, psres, sqres):
                for c in range(nch):
                    mm(ps[:], ones_bf[:], sqb[:, c, :], c == 0, c == nch - 1, ["ones_bf", sqres], [psres])

            def normC(ps, psres, tm, tmres, rs, rsres, xin, xres, nch, gcol, gres, xo, xores):
                act(tm[:], ps[:], AF.Ln, [psres], [tmres], bias=EPS, scale=1.0)
                act(rs[:], tm[:], AF.Exp, [tmres], [rsres], scale=-0.5)
                for c in range(nch):
                    stt(xo[:, c, :], xin[:, c, :], gcol[:, c:c + 1], rs[:], ALU.mult, ALU.mult,
                        [xres, gres, rsres], [xores])

            sqc = sb("s1_sqc", [128, 5, 512], BF16)
            tmp_c = sb("s1_tmpc", [128, 512], F32)
            rstd_c = sb("s1_rstdc", [128, 512], F32)
            tmp_k = sb("s1_tmpk", [128, 512], F32)
            rstd_k = sb("s1_rstdk", [128, 512], F32)
            ps_ssc = pst("s1_pssc", [128, 512])
            ps_ssk = pst("s1_pssk", [128, 512])

            def xnorm(b):
                sl_ = b % 2
                xr_, xnr_ = "s1_x%d" % sl_, "s1_xn%d" % sl_
                normA(xs[sl_], 8, sq, "s1_sq", xr_, 1.0 / 32.0)
                normB(sq, 8, ps_ss, "s1_pss", "s1_sq")
                normC(ps_ss, "s1_pss", tmp, "s1_tmp", rstd, "s1_rstd", xs[sl_], xr_, 8, g1s, "s1_g1", xn[sl_], xnr_)

            load(0)
            xnorm(0)
            for b in range(NB):
                sl = b % 2
                t0 = b * 512
                if b + 1 < NB:
                    load(b + 1)
                xr, xnr = "s1_x%d" % sl, "s1_xn%d" % sl
                X = xn[sl]

                def proj_fm(W, Wres, c0, M):
                    ps, psr = nps()
                    if Wres == "s1_Wa":
                        Wres = wa_res(c0)
                    for kc in range(8):
                        mm(ps[0:M, :], W[:, kc, c0:c0 + M], X[:, kc, :], kc == 0, kc == 7, [Wres, xnr], [psr])
                    return ps, psr

                def plain_fm(c0, M, dst):
                    ps, psr = proj_fm(Wa, "s1_Wa", c0, M)
                    st_, sr = nstg()
                    evac(st_[0:M, :], ps[0:M, :], [psr], [sr])
                    store(dst, st_[0:M, :], sr, ("scr", id(dst), b))

                def rope_fm(c0, csw, M, dst, dres):
                    pa, par = proj_fm(Wa, "s1_Wa", c0, M)
                    pb, pbr = proj_fm(Wsw, "s1_Wsw", csw, M)
                    i = cnt["t"] % 2
                    cnt["t"] += 1
                    tt("dve", t1[i][0:M, :], pa[0:M, :], Cs[sl][0:M, :], ALU.mult, [par, "s1_C%d" % sl], ["s1_t1_%d" % i])
                    tt("dve", t2[i][0:M, :], pb[0:M, :], Ss[sl][0:M, :], ALU.mult, [pbr, "s1_S%d" % sl], ["s1_t2_%d" % i])
                    st_, sr = nstg()
                    tt("pool", st_[0:M, :], t1[i][0:M, :], t2[i][0:M, :], ALU.add, ["s1_t1_%d" % i, "s1_t2_%d" % i], [sr])
                    store(dst, st_[0:M, :], sr, dres)

                for c in range(3):
                    ps, psr = proj_fm(Wa, "s1_Wa", 1536 + c * 128, 128)
                    cp("dve", cq[:, c, :], ps[:], [psr], ["s1_cq"])
                for c in range(2):
                    ps, psr = proj_fm(Wa, "s1_Wa", 1920 + c * 128, 128)
                    cp("dve", ckv[:, c, :], ps[:], [psr], ["s1_ckv"])
                normA(cq, 3, sqc[:, 0:3, :], "s1_sqc", "s1_cq", 1.0 / np.sqrt(384.0))
                normA(ckv, 2, sqc[:, 3:5, :], "s1_sqk", "s1_ckv", 1.0 / 16.0)
                for c in range(4):
                    plain_fm(c * 128, 128, naq_f[c * 128:(c + 1) * 128, t0:t0 + 512])
                normB(sqc[:, 0:3, :], 3, ps_ssc, "s1_pssc", "s1_sqc")
                normB(sqc[:, 3:5, :], 2, ps_ssk, "s1_pssk", "s1_sqk")
                normC(ps_ssc, "s1_pssc", tmp_c, "s1_tmpc", rstd_c, "s1_rstdc", cq, "s1_cq", 3, gqs, "s1_gq", cqn, "s1_cqn")
                normC(ps_ssk, "s1_pssk", tmp_k, "s1_tmpk", rstd_k, "s1_rstdk", ckv, "s1_ckv", 2, gks, "s1_gk", ckvn, "s1_ckvn")
                for c in range(4):
                    plain_fm(512 + c * 128, 128, nak_f[c * 128:(c + 1) * 128, t0:t0 + 512])
                rope_fm(2176, 0, 64, mkp[:, t0:t0 + 512], ("scr", "mkp", b))
                for c in range(4):
                    rope_fm(2240 + c * 128, 64 + c * 128, 128, gq_f[c * 128:(c + 1) * 128, t0:t0 + 512], ("scr", "gq", b, c))
                rope_fm(2752, 576, 128, gk_f[:, t0:t0 + 512], ("scr", "gk", b))
                for i in range(4):
                    ps, psr = nps()
                    for kc in range(8):
                        mm(ps[:], X[:, kc, i * 128:(i + 1) * 128], Wa[:, kc, 1024:1536], kc == 0, kc == 7, [wa_res(1024), xnr], [psr])
                    st_, sr = nstg()
                    evac(st_[:], ps[:], [psr], [sr])
                    store(nav[t0 + i * 128:t0 + (i + 1) * 128, :], st_[:], sr, ("scr", "nav", b, i))
                ps, psr = nps()
                for i in range(4):
                    for kc in range(8):
                        mm(ps[:, i * 128:(i + 1) * 128], X[:, kc, i * 128:(i + 1) * 128], Wa[:, kc, 2880:3008],
                           kc == 0, kc == 7, [wa_res(2880), xnr], [psr])
                st_, sr = nstg()
                evac(st_[:], ps[:], [psr], [sr])
                store(gv[:, b * 4:(b + 1) * 4, :],
                      st_[:].rearrange("p (i d) -> p i d", d=128), sr, ("scr", "gv", b))
                if b + 1 < NB:
                    xnorm(b + 1)
                for h in range(4):
                    ps, psr = nps()
                    for kc in range(3):
                        mm(ps[:], Wq[:, kc, h * 192:h * 192 + 128], cqn[:, kc, :], kc == 0, kc == 2, ["s1_Wq", "s1_cqn"], [psr])
                    st_, sr = nstg()
                    evac(st_[:], ps[:], [psr], [sr])
                    store(mqn[h, :, t0:t0 + 512], st_[:], sr, ("scr", "mqn", b, h))
                for hp in range(2):
                    pa, par = nps()
                    for kc in range(3):
                        mm(pa[:], Wqp[:, kc, hp * 128:(hp + 1) * 128], cqn[:, kc, :], kc == 0, kc == 2, ["s1_Wqp", "s1_cqn"], [par])
                    pb, pbr = nps()
                    for kc in range(3):
                        mm(pb[:], Wqps[:, kc, hp * 128:(hp + 1) * 128], cqn[:, kc, :], kc == 0, kc == 2, ["s1_Wqps", "s1_cqn"], [pbr])
                    i = cnt["t"] % 2
                    cnt["t"] += 1
                    tt("dve", t1[i][:], pa[:], Cs[sl][:], ALU.mult, [par, "s1_C%d" % sl], ["s1_t1_%d" % i])
                    tt("dve", t2[i][:], pb[:], Ss[sl][:], ALU.mult, [pbr, "s1_S%d" % sl], ["s1_t2_%d" % i])
                    st_, sr = nstg()
                    tt("pool", st_[:], t1[i][:], t2[i][:], ALU.add, ["s1_t1_%d" % i, "s1_t2_%d" % i], [sr])
                    store(mqp_f[hp * 128:(hp + 1) * 128, t0:t0 + 512], st_[:], sr, ("scr", "mqp", b, hp))
                for h in range(4):
                    ps, psr = nps()
                    for kc in range(2):
                        mm(ps[:], Wkv[:, kc, h * 256:h * 256 + 128], ckvn[:, kc, :], kc == 0, kc == 1, ["s1_Wkv", "s1_ckvn"], [psr])
                    st_, sr = nstg()
                    evac(st_[:], ps[:], [psr], [sr])
                    store(mkn[h, :, t0:t0 + 512], st_[:], sr, ("scr", "mkn", b, h))
                for i in range(4):
                    ps, psr = nps()
                    for kc in range(2):
                        mm(ps[:], ckvn[:, kc, i * 128:(i + 1) * 128], Wv[:, kc, :], kc == 0, kc == 1, ["s1_Wv", "s1_ckvn"], [psr])
                    st_, sr = nstg()
                    evac(st_[:], ps[:], [psr], [sr])
                    store(mv[:, :, b * 4 + i, :].rearrange("h p d -> p h d"),
                          st_[:].rearrange("p (h d) -> p h d", d=128), sr, ("scr", "mv", b, i))
        S.barrier()

    def stage_mla(l, prefetch=None):
        scale = 192.0 ** -0.5
        with ExitStack() as es:
            def sb(name, shape, dt):
                return es.enter_context(nc.sbuf_tensor(name + SUF[0], list(shape), dt))

            def pst(name, shape):
                return es.enter_context(nc.psum_tensor(name + SUF[0], list(shape), F32))
            kpe = sb("ml_kpe", [128, T], BF16)
            Ks = [sb("ml_K%d" % i, [128, T], BF16) for i in range(2)]
            Vs = [sb("ml_V%d" % i, [128, NT, 128], BF16) for i in range(2)]
            Qn = [sb("ml_Qn%d" % i, [128, 512], BF16) for i in range(2)]
            Qp = [sb("ml_Qp%d" % i, [128, 512], BF16) for i in range(2)]
            NP = 3
            NPT = 4
            pt = [sb("ml_pt%d" % i, [128, 512], BF16) for i in range(NPT)]
            acc = [sb("ml_acc%d" % i, [128, 512], F32) for i in range(2)]
            tr1 = sb("ml_tr1", [128, 512], BF16)
            tr2 = sb("ml_tr2", [128, 512], BF16)
            tr3 = sb("ml_tr3", [128, 512], BF16)
            lnt = sb("ml_ln", [128, 512], F32)
            rec = sb("ml_rec", [128, 512], F32)
            yst = [sb("ml_y%d" % i, [128, 512], BF16) for i in range(2)]
            ps_s = [pst("ml_pss%d" % i, [128, 512]) for i in range(NP)]
            ps_o = [pst("ml_pso%d" % i, [128, 512]) for i in range(2)]
            ps_sum = pst("ml_psum", [128, 512])

            S.op("pool", "memset", writes=["ml_kpe"], ap=kpe[64:128, :], constant=0.0)
            for i in range(2):
                S.op("pool", "memset", writes=["ml_Qp%d" % i], ap=Qp[i][64:128, :], constant=0.0)
            S.dma("sp", kpe[0:64, :], mkp[:, :], writes=["ml_kpe"], stream="ld")

            def loadKV(h):
                s = h % 2
                S.dma("sp", Ks[s][:], mkn[h, :, :], writes=["ml_K%d" % s], stream="ld")
                S.dma("sp", Vs[s][:], mv[h], writes=["ml_V%d" % s], stream="ld")

            def loadQ(h, qb, it):
                s = it % 2
                S.dma("sp", Qn[s][:], mqn[h, :, qb * 512:(qb + 1) * 512], writes=["ml_Qn%d" % s], stream="ld")
                S.dma("sp", Qp[s][0:64, :], mqp[h, :, qb * 512:(qb + 1) * 512], writes=["ml_Qp%d" % s], stream="ld")

            items = [(h, qb) for h in range(4) for qb in range(NB)]
            loadKV(0)
            loadQ(0, 0, 0)
            gcount = [0]
            steps = []
            flat = []
            for it, (h, qb) in enumerate(items):
                hs = h % 2
                qs = it % 2
                for kb in range(NT):
                    def qk(it=it, h=h, qb=qb, hs=hs, qs=qs, kb=kb):
                        if kb == min(8, NT - 1) and it == 0 and prefetch is not None:
                            prefetch()
                        if kb == 0 and it + 1 < len(items):
                            nh, nqb = items[it + 1]
                            if nh != h:
                                loadKV(nh)
                            loadQ(nh, nqb, it + 1)
                        i = gcount[0] % NP
                        ip = gcount[0] % NPT
                        gcount[0] += 1
                        mm(ps_s[i][:], Ks[hs][:, kb * 128:(kb + 1) * 128], Qn[qs][:], True, False,
                           ["ml_K%d" % hs, "ml_Qn%d" % qs], ["ml_pss%d" % i])
                        mm(ps_s[i][:], kpe[:, kb * 128:(kb + 1) * 128], Qp[qs][:], False, True,
                           ["ml_kpe", "ml_Qp%d" % qs], ["ml_pss%d" % i])
                        act(pt[ip][:], ps_s[i][:], AF.Exp, ["ml_pss%d" % i], ["ml_pt%d" % ip], scale=scale)
                        return ip
                    steps.append((kb, qk))
                    flat.append((it, h, qb, kb))
            ring = {}
            pend = []
            iprev = [0]
            depth = 2
            n = len(flat)
            for idx in range(n + depth):
                if idx < n:
                    it, h, qb, kb = flat[idx]
                    ring[idx] = steps[idx][1]()
                if idx >= depth:
                    j = idx - depth
                    it, h, qb, kb = flat[j]
                    i = ring.pop(j)
                    hs = h % 2
                    a = it % 2
                    mm(ps_o[a][:], Vs[hs][:, kb, :], pt[i][:], kb == 0, kb == NT - 1,
                       ["ml_V%d" % hs, "ml_pt%d" % i], ["ml_pso%d" % a])
                    if kb % 4 == 1:
                        tt("dve", tr1[:], pt[iprev[0]][:], pt[i][:], ALU.add, ["ml_pt%d" % iprev[0], "ml_pt%d" % i], ["ml_tr1"])
                    elif kb % 4 == 3:
                        tt("dve", tr2[:], pt[iprev[0]][:], pt[i][:], ALU.add, ["ml_pt%d" % iprev[0], "ml_pt%d" % i], ["ml_tr2"])
                        if kb == 3:
                            tt("dve", acc[a][:], tr1[:], tr2[:], ALU.add, ["ml_tr1", "ml_tr2"], ["ml_acc%d" % a])
                        else:
                            tt("dve", tr3[:], tr1[:], tr2[:], ALU.add, ["ml_tr1", "ml_tr2"], ["ml_tr3"])
                            tt("dve", acc[a][:], acc[a][:], tr3[:], ALU.add, ["ml_acc%d" % a, "ml_tr3"], ["ml_acc%d" % a])
                    iprev[0] = i
                    if kb == NT - 1:
                        def fin(a=a, h=h, qb=qb):
                            mm(ps_sum[:], ones_f[:], acc[a][:], True, True, ["ones_f", "ml_acc%d" % a], ["ml_psum"])
                            act(lnt[:], ps_sum[:], AF.Ln, ["ml_psum"], ["ml_ln"])
                            act(rec[:], lnt[:], AF.Exp, ["ml_ln"], ["ml_rec"], scale=-1.0)
                            tt("dve", yst[a][:], ps_o[a][:], rec[:], ALU.mult, ["ml_pso%d" % a, "ml_rec"], ["ml_y%d" % a])
                            S.dma("sp", ymla[h * 128:(h + 1) * 128, qb * 512:(qb + 1) * 512], yst[a][:],
                                  reads=["ml_y%d" % a], writes=[("scr", "ymla", h, qb)], stream="st")
                        pend.append((idx + 4, fin))
                while pend and pend[0][0] <= idx:
                    pend.pop(0)[1]()
            for _, f in pend:
                f()
        S.barrier()

    def stage_na(l):
        with ExitStack() as es:
            def sb(name, shape, dt):
                return es.enter_context(nc.sbuf_tensor(name + SUF[0], list(shape), dt))

            def pst(name, shape):
                return es.enter_context(nc.psum_tensor(name + SUF[0], list(shape), F32))
            RING = 8
            kT = [sb("na_k%d" % i, [128, 4, 128], BF16) for i in range(RING)]
            vr = [sb("na_v%d" % i, [128, 512], BF16) for i in range(RING)]
            qT = [sb("na_q%d" % i, [128, 4, 2, 128], BF16) for i in range(2)]
            bias_int = sb("na_bint", [128, 5, 8, 128], F32)
            bias_sp = sb("na_bsp", [128, 5, 8, 128], F32)
            NP = 3
            sbf = [sb("na_sb%d" % i, [128, 4, 128], F32) for i in range(NP)]
            pt = [sb("na_pt%d" % i, [128, 4, 128], BF16) for i in range(NP)]
            rec = [sb("na_rec%d" % i, [128, 4, 128], F32) for i in range(2)]
            lnt = sb("na_ln", [128, 512], F32)
            yst = [sb("na_y%d" % i, [128, 2, 128], BF16) for i in range(2)]
            ps_s = [pst("na_pss%d" % i, [128, 4, 128]) for i in range(NP)]
            ps_o = [pst("na_pso%d" % i, [128, 4, 128]) for i in range(2)]
            ps_sum = [pst("na_psum%d" % i, [128, 512]) for i in range(2)]

            counts = {}
            for c in cls_of_tile:
                counts[c] = counts.get(c, 0) + 1
            c_int = max(counts, key=lambda c: counts[c])
            S.dma("sp", bias_int[:].rearrange("p j h q -> p (j h q)"), nab[l, c_int], writes=["na_bint"], stream="ld")
            nak_v = nak.rearrange("h d t -> (h d) t").rearrange("(hp p) t -> p hp t", p=128)
            naq_v = naq.rearrange("(hp two) d t -> d two hp t", two=2)
            for i in range(2):
                S.op("pool", "memset", writes=["na_q%d" % i], ap=qT[i][:].rearrange("p a b q -> p (a b q)"), constant=0.0)
            yna_v = yna.rearrange("(pr p) t -> p pr t", p=128)
            loaded = [-1]

            def ensure(kt_hi):
                while loaded[0] < kt_hi:
                    kt = loaded[0] + 1
                    s = kt % RING
                    S.dma("sp", kT[s][:], nak_v[:, :, kt * 128:(kt + 1) * 128], writes=["na_k%d" % s], stream="ld")
                    S.dma("sp", vr[s][:], nav[kt * 128:(kt + 1) * 128, :], writes=["na_v%d" % s], stream="ld")
                    loaded[0] = kt

            def loadq(t):
                s = t % 2
                S.dma("sp", qT[s][0:64, :, 0, :], naq_v[:, 0, :, t * 128:(t + 1) * 128], writes=["na_q%d" % s], stream="ld")
                S.dma("sp", qT[s][64:128, :, 1, :], naq_v[:, 1, :, t * 128:(t + 1) * 128], writes=["na_q%d" % s], stream="ld")

            kt0s = [min(max(t - 2, 0), NT - 5) for t in range(NT)]
            ensure(kt0s[0] + 4)
            loadq(0)
            gcount = [0]
            grp = [0]
            steps = []
            for t in range(NT):
                kt0 = kt0s[t]
                cls = cls_of_tile[t]
                for g in range(2):
                    a = (t * 2 + g) % 2
                    for j in range(5):
                        def front(t=t, g=g, j=j, kt0=kt0, cls=cls, holder=None):
                            if g == 0 and j == 0:
                                if t + 1 < NT:
                                    ensure(kt0s[t + 1] + 4)
                                    loadq(t + 1)
                                if cls != c_int:
                                    S.dma("sp", bias_sp[:].rearrange("p j h q -> p (j h q)"), nab[l, cls],
                                          writes=["na_bsp"], stream="ld")
                            bias, bres = (bias_int, "na_bint") if cls == c_int else (bias_sp, "na_bsp")
                            kt = kt0 + j
                            s = kt % RING
                            qs = t % 2
                            i = gcount[0] % NP
                            gcount[0] += 1
                            for h2 in range(2):
                                hp = g * 2 + h2
                                mm(ps_s[i][:, h2 * 2:(h2 + 1) * 2, :], kT[s][:, hp, :], qT[qs][:, hp, :, :], True, True,
                                   ["na_k%d" % s, "na_q%d" % qs], ["na_pss%d" % i])
                            stt(sbf[i][:], ps_s[i][:], 0.125, bias[:, j, g * 4:(g + 1) * 4, :], ALU.mult, ALU.add,
                                ["na_pss%d" % i, bres], ["na_sb%d" % i])
                            act(pt[i][:], sbf[i][:], AF.Exp, ["na_sb%d" % i], ["na_pt%d" % i])
                            return i

                        def back(i, t=t, g=g, j=j, kt0=kt0, a=a):
                            kt = kt0 + j
                            s = kt % RING
                            for hh in range(4):
                                pr = g * 2 + hh // 2
                                mm(ps_o[a][:, hh, :], vr[s][:, pr * 128:(pr + 1) * 128], pt[i][:, hh, :], j == 0 and hh == 0, j == 4,
                                   ["na_v%d" % s, "na_pt%d" % i], ["na_pso%d" % a], skip=True)
                            mm(ps_sum[a][:], ones_bf[:], pt[i][:].rearrange("p h q -> p (h q)"), j == 0, j == 4,
                               ["ones_bf", "na_pt%d" % i], ["na_psum%d" % a])
                            if j == 4:
                                def fin_a():
                                    act(lnt[:], ps_sum[a][:], AF.Ln, ["na_psum%d" % a], ["na_ln"])
                                    act(rec[a][:].rearrange("p h q -> p (h q)"), lnt[:], AF.Exp, ["na_ln"], ["na_rec%d" % a], scale=-1.0)

                                def fin_b():
                                    po4 = ps_o[a][:].rearrange("p (pr two) q -> p pr two q", two=2)
                                    rc4 = rec[a][:].rearrange("p (pr two) q -> p pr two q", two=2)
                                    for par in range(2):
                                        lo, hi = par * 64, par * 64 + 64
                                        tt("dve", yst[a][lo:hi, :, :], po4[lo:hi, :, par, :], rc4[lo:hi, :, par, :], ALU.mult,
                                           ["na_pso%d" % a, "na_rec%d" % a], ["na_y%d" % a])
                                    S.dma("sp", yna_v[:, g * 2:(g + 1) * 2, t * 128:(t + 1) * 128], yst[a][:],
                                          reads=["na_y%d" % a], writes=[("scr", "yna", t, g)], stream="st")
                                return [(1, fin_a), (4, fin_b)]
                            return None
                        steps.append((front, back))
            run_pipeline(steps, 2, 0)
        S.barrier()

    def stage_gqa(l):
        with ExitStack() as es:
            def sb(name, shape, dt):
                return es.enter_context(nc.sbuf_tensor(name + SUF[0], list(shape), dt))

            def pst(name, shape):
                return es.enter_context(nc.psum_tensor(name + SUF[0], list(shape), F32))
            Kall = sb("gq_K", [128, T], BF16)
            Vall = sb("gq_V", [128, NT, 128], BF16)
            qT = [sb("gq_q%d" % i, [128, 2, 4, 128], BF16) for i in range(2)]
            msk = sb("gq_msk", [128, 2, 512], F32)
            sk = sb("gq_sink", [128, 8], F32)
            esk = sb("gq_esink", [128, 8], F32)
            NP = 3
            sbf = [sb("gq_sb%d" % i, [128, 512], F32) for i in range(NP)]
            pt = [sb("gq_pt%d" % i, [128, 512], BF16) for i in range(NP)]
            rec = [sb("gq_rec%d" % i, [128, 512], F32) for i in range(3)]
            lnt = sb("gq_ln", [128, 512], F32)
            yst = [sb("gq_y%d" % i, [128, 4, 128], BF16) for i in range(3)]
            ps_s = [pst("gq_pss%d" % i, [128, 512]) for i in range(NP)]
            ps_o = [pst("gq_pso%d" % i, [128, 512]) for i in range(3)]
            ps_sum = [pst("gq_psum%d" % i, [128, 512]) for i in range(2)]

            S.dma("sp", Kall[:], gk.rearrange("h d t -> (h d) t"), writes=["gq_K"], stream="ld")
            for i in range(2):
                S.op("pool", "memset", writes=["gq_q%d" % i], ap=qT[i][:].rearrange("p a b q -> p (a b q)"), constant=0.0)
            S.dma("sp", Vall[:], gv, writes=["gq_V"], stream="ld")
            S.dma("sp", msk[:], gmask.rearrange("m p n -> p m n"), writes=["gq_msk"], stream="ld")
            S.dma("sp", sk[0:64, :], sinkr[l], writes=["gq_sink"], stream="ld")
            S.dma("sp", sk[64:128, :], sinkr[l], writes=["gq_sink"], stream="ld")
            act(esk[:], sk[:], AF.Exp, ["gq_sink"], ["gq_esink"])
            gq_v = gq.rearrange("h d t -> d h t")
            ygqa_v = ygqa.rearrange("(h d) t -> d h t", d=64)

            def loadq(t):
                s = t % 2
                S.dma("sp", qT[s][0:64, 0, :, :], gq_v[:, 0:4, t * 128:(t + 1) * 128], writes=["gq_q%d" % s], stream="ld")
                S.dma("sp", qT[s][64:128, 1, :, :], gq_v[:, 4:8, t * 128:(t + 1) * 128], writes=["gq_q%d" % s], stream="ld")

            loadq(0)
            gcount = [0]
            steps = []
            for t in range(NT):
                for kvh in range(2):
                    a = (t * 2 + kvh) % 3
                    a2 = (t * 2 + kvh) % 2
                    kts = [kt for kt in (t - 1, t, t + 1) if 0 <= kt < NT]
                    for jj, kt in enumerate(kts):
                        first = jj == 0
                        last = jj == len(kts) - 1

                        def front(t=t, kvh=kvh, kt=kt, first=first):
                            if kvh == 0 and first and t + 1 < NT:
                                loadq(t + 1)
                            qs = t % 2
                            i = gcount[0] % NP
                            gcount[0] += 1
                            mm(ps_s[i][:], Kall[:, kt * 128:(kt + 1) * 128],
                               qT[qs][:, kvh, :, :], True, True,
                               ["gq_K", "gq_q%d" % qs], ["gq_pss%d" % i])
                            if kt != t:
                                m = 0 if kt < t else 1
                                tt("dve", sbf[i][:], ps_s[i][:], msk[:, m, :], ALU.add, ["gq_pss%d" % i, "gq_msk"], ["gq_sb%d" % i])
                                act(pt[i][:], sbf[i][:], AF.Exp, ["gq_sb%d" % i], ["gq_pt%d" % i], scale=0.125)
                            else:
                                act(pt[i][:], ps_s[i][:], AF.Exp, ["gq_pss%d" % i], ["gq_pt%d" % i], scale=0.125)
                            return i

                        def back(i, t=t, kvh=kvh, kt=kt, first=first, last=last, a=a, a2=a2):
                            mm(ps_o[a][:], Vall[:, kt, :], pt[i][:], first, last,
                               ["gq_V", "gq_pt%d" % i], ["gq_pso%d" % a])
                            mm(ps_sum[a2][:], ones_bf[:], pt[i][:], first, last,
                               ["ones_bf", "gq_pt%d" % i], ["gq_psum%d" % a2])
                            if last:
                                lo, hi = kvh * 64, kvh * 64 + 64

                                def fin_a():
                                    for g in range(4):
                                        h = kvh * 4 + g
                                        act(lnt[lo:hi, g * 128:(g + 1) * 128], ps_sum[a2][lo:hi, g * 128:(g + 1) * 128], AF.Ln,
                                            ["gq_psum%d" % a2, "gq_esink"], ["gq_ln"], bias=esk[lo:hi, h:h + 1], scale=1.0)
                                    act(rec[a][lo:hi, :], lnt[lo:hi, :], AF.Exp, ["gq_ln"], ["gq_rec%d" % a], scale=-1.0)

                                def fin_b():
                                    tt("dve", yst[a][lo:hi, :, :].rearrange("p h q -> p (h q)"), ps_o[a][lo:hi, :], rec[a][lo:hi, :], ALU.mult,
                                       ["gq_pso%d" % a, "gq_rec%d" % a], ["gq_y%d" % a])
                                    S.dma("sp", ygqa_v[:, kvh * 4:(kvh + 1) * 4, t * 128:(t + 1) * 128], yst[a][lo:hi, :, :],
                                          reads=["gq_y%d" % a], writes=[("scr", "ygqa", t, kvh)], stream="st")
                                return [(1, fin_a), (4, fin_b)]
                            return None
                        steps.append((front, back))
            run_pipeline(steps, 2, 0)
        S.barrier()

    def stage3(l, xsrc, xdst, W3):
        with ExitStack() as es:
            def sb(name, shape, dt):
                return es.enter_context(nc.sbuf_tensor(name + SUF[0], list(shape), dt))

            def pst(name, shape):
                return es.enter_context(nc.psum_tensor(name + SUF[0], list(shape), F32))
            Wg, Wb, Wo = W3
            g1s = sb("s3_g1", [128, 8], F32)
            bgs = sb("s3_bg", [128, 24], F32)
            xs = [sb("s3_x%d" % i, [128, 8, 512], F32) for i in range(2)]
            xn2 = [sb("s3_xn%d" % i, [128, 8, 512], BF16) for i in range(2)]
            tmp = sb("s3_tmp", [128, 512], F32)
            rstd = sb("s3_rstd", [128, 512], F32)
            ys = [[sb("s3_y%d_%d" % (i, s), [128, 4, 512], BF16) for s in range(2)] for i in range(3)]
            mg = sb("s3_mg", [128, 8, 512], BF16)
            gt = [sb("s3_gt%d" % i, [128, 512], F32) for i in range(3)]
            pr = [sb("s3_pr%d" % i, [128, 512], F32) for i in range(3)]
            m01 = sb("s3_m01", [128, 512], F32)
            xo = [sb("s3_xo%d" % i, [128, 512], F32) for i in range(3)]
            NPS = 6
            pss = [pst("s3_ps%d" % i, [128, 512]) for i in range(NPS)]
            ps_ss = pst("s3_pss", [128, 512])
            cnt = {"ps": 0, "xo": 0}

            def nps():
                i = cnt["ps"] % NPS
                cnt["ps"] += 1
                return pss[i], "s3_ps%d" % i

            S.dma("sp", g1s[:], g1[l], writes=["s3_g1"], stream="ld")
            S.dma("sp", bgs[:], bg[l], writes=["s3_bg"], stream="ld")
            xsrc_f = fm(xsrc)
            xdst_f = fm(xdst)
            ysrc = [fm(yna), fm(ymla), fm(ygqa)]

            def load(b):
                sl = b % 2
                t0 = b * 512
                S.dma("sp", xs[sl][:], xsrc_f[:, :, t0:t0 + 512], writes=["s3_x%d" % sl], stream="ld")
                for i in range(3):
                    S.dma("sp", ys[i][sl][:], ysrc[i][:, :, t0:t0 + 512], writes=["s3_y%d_%d" % (i, sl)], stream="ld")

            def xnorm3(b):
                sl_ = b % 2
                fm_norm(xs[sl_], "s3_x%d" % sl_, 8, 512, g1s, "s3_g1", xn2[sl_], "s3_xn%d" % sl_, ps_ss, "s3_pss", tmp, "s3_tmp",
                        rstd, "s3_rstd", xn2[sl_], "s3_xn%d" % sl_, 1.0 / 32.0)

            load(0)
            xnorm3(0)
            for b in range(NB):
                sl = b % 2
                t0 = b * 512
                if b + 1 < NB:
                    load(b + 1)
                xr = "s3_x%d" % sl
                xn = xn2[sl]
                xnres = "s3_xn%d" % sl
                for m in range(8):
                    if m == 4 and b + 1 < NB:
                        xnorm3(b + 1)
                    for i in range(3):
                        pg, pgr = nps()
                        for kc in range(8):
                            mm(pg[:], Wg[:, kc, i * 1024 + m * 128:i * 1024 + (m + 1) * 128], xn[:, kc, :],
                               kc == 0, kc == 7, [xnres], [pgr])
                        py, pyr = nps()
                        yr = "s3_y%d_%d" % (i, sl)
                        for kc in range(4):
                            mm(py[:], Wb[:, i, kc, m * 128:(m + 1) * 128], ys[i][sl][:, kc, :], kc == 0, kc == 3,
                               [yr], [pyr])
                        act(gt[i][:], pg[:], AF.Sigmoid, [pgr, "s3_bg"], ["s3_gt%d" % i],
                            bias=bgs[:, i * 8 + m:i * 8 + m + 1], scale=1.0)
                        tt("dve", pr[i][:], gt[i][:], py[:], ALU.mult, ["s3_gt%d" % i, pyr], ["s3_pr%d" % i])
                    tt("pool", m01[:], pr[0][:], pr[1][:], ALU.add, ["s3_pr0", "s3_pr1"], ["s3_m01"])
                    tt("pool", mg[:, m, :], m01[:], pr[2][:], ALU.add, ["s3_m01", "s3_pr2"], [("s3_mg", m)])
                for m in range(8):
                    po, por = nps()
                    for kc in range(8):
                        mm(po[:], Wo[:, kc, m * 128:(m + 1) * 128], mg[:, kc, :], kc == 0, kc == 7,
                           [("s3_mg", kc)], [por])
                    i = cnt["xo"] % 3
                    cnt["xo"] += 1
                    tt("dve", xo[i][:], po[:], xs[sl][:, m, :], ALU.add, [por, xr], ["s3_xo%d" % i])
                    S.dma("sp", xdst_f[:, m, t0:t0 + 512], xo[i][:], reads=["s3_xo%d" % i],
                          writes=[("scr", "x1", b, m)], stream="st")
        S.barrier()

    def stage_ffn(l, xsrc, xdst, final=False):
        NOUT = 382
        FW = 384
        wins = []
        o0 = 0
        while o0 < T:
            o1 = min(o0 + NOUT, T)
            u0 = max(o0 - 1, 0)
            u1 = min(o1 + 1, T)
            wins.append((o0, o1, u0, u1))
            o0 = o1
        with ExitStack() as es:
            def sb(name, shape, dt):
                return es.enter_context(nc.sbuf_tensor(name + SUF[0], list(shape), dt))

            def pst(name, shape):
                return es.enter_context(nc.psum_tensor(name + SUF[0], list(shape), F32))
            Wu = sb("ff_Wu", [128, 8, 2 * D_FF], BF16)
            Wd = sb("ff_Wd", [128, 22, 1024], BF16)
            g2s = sb("ff_g2", [128, 8], F32)
            cws = sb("ff_cw", [128, 44, 3], F32)
            cbs = sb("ff_cb", [128, 44], F32)
            xs = sb("ff_x", [128, 8, FW], F32)
            xn = sb("ff_xn", [128, 8, FW], BF16)
            tmp = sb("ff_tmp", [128, FW], F32)
            rstd = sb("ff_rstd", [128, FW], F32)
            hm = sb("ff_hm", [128, 22, FW], BF16)
            av = [sb("ff_a%d" % i, [128, FW], F32) for i in range(3)]
            gvv = [sb("ff_gv%d" % i, [128, FW], F32) for i in range(3)]
            gg = [sb("ff_gg%d" % i, [128, FW], F32) for i in range(3)]
            xres = [sb("ff_xr%d" % i, [128, FW], F32) for i in range(2)]
            xo = [sb("ff_xo%d" % i, [128, FW], F32) for i in range(2)]
            NPS = 6
            pss = [pst("ff_ps%d" % i, [128, 512]) for i in range(NPS)]
            ps_ss = pst("ff_pss", [128, 512])
            cnt = {"ps": 0, "c": 0, "xo": 0}

            def nps():
                i = cnt["ps"] % NPS
                cnt["ps"] += 1
                return pss[i], "ff_ps%d" % i

            wu_l = w_up[l].rearrange("(kc p) n -> p kc n", p=128)
            for c in range(22):
                for cc in (c, 22 + c):
                    S.dma("pool", Wu[:, :, cc * 128:(cc + 1) * 128], wu_l[:, :, cc * 128:(cc + 1) * 128],
                          writes=[("ff_Wu", cc)], stream="w")
            wd_l = w_down[l].rearrange("(kc p) n -> p kc n", p=128)
            for kc in range(22):
                S.dma("pool", Wd[:, kc, :], wd_l[:, kc, :], writes=[("ff_Wd", kc)], stream="w")
            S.dma("sp", g2s[:], g2[l], writes=["ff_g2"], stream="ld")
            S.dma("sp", cws[:].rearrange("p c j -> p (c j)"), cw[l], writes=["ff_cw"], stream="ld")
            S.dma("sp", cbs[:], cb[l], writes=["ff_cb"], stream="ld")
            xsrc_f = fm(xsrc)
            xdst_f = fm(xdst)
            if final:
                x2b = sb("ff_x2b", [128, 8, FW], F32)
                tmpf = sb("ff_tmpf", [128, FW], F32)
                rstdf = sb("ff_rstdf", [128, FW], F32)
                gfs = sb("ff_gf", [128, 8], F32)
                ps_fn = pst("ff_psfn", [128, 512])
                S.dma("sp", gfs[:], gf, writes=["ff_gf"], stream="ld")
                y_f = fm(yT)

            def load(w):
                o0, o1, u0, u1 = wins[w]
                S.dma("sp", xs[:, :, 0:u1 - u0], xsrc_f[:, :, u0:u1], writes=["ff_x"], stream="ld")

            load(0)
            for w, (o0, o1, u0, u1) in enumerate(wins):
                nu = u1 - u0
                no = o1 - o0
                uoff = o0 - u0
                if w == 0:
                    fm_norm(xs, "ff_x", 8, nu, g2s, "ff_g2", xn, "ff_xn", ps_ss, "ff_pss",
                            tmp, "ff_tmp", rstd, "ff_rstd", xn, "ff_xn", 1.0 / 32.0)
                if w + 1 < len(wins):
                    load(w + 1)
                defer = []
                for c in range(22):
                    k = cnt["c"] % 3
                    cnt["c"] += 1
                    outs = []
                    for (cc, dst, dres) in ((c, av[k], "ff_a%d" % k), (22 + c, gvv[k], "ff_gv%d" % k)):
                        ps, psr = nps()
                        for kc in range(8):
                            mm(ps[:, 0:nu], Wu[:, kc, cc * 128:(cc + 1) * 128], xn[:, kc, 0:nu], kc == 0, kc == 7,
                               [("ff_Wu", cc), "ff_xn"], [psr])
                        act(dst[:, 0:no], ps[:, uoff:uoff + no], AF.Identity, [psr, "ff_cw", "ff_cb"], [dres],
                            bias=cbs[:, cc:cc + 1], scale=cws[:, cc, 1:2])
                        lo = 0 if uoff >= 1 else 1
                        stt(dst[:, lo:no], ps[:, lo + uoff - 1:no + uoff - 1], cws[:, cc, 0:1], dst[:, lo:no],
                            ALU.mult, ALU.add, [psr, "ff_cw", dres], [dres])
                        hi = min(no, nu - uoff - 1)
                        stt(dst[:, 0:hi], ps[:, uoff + 1:uoff + 1 + hi], cws[:, cc, 2:3], dst[:, 0:hi],
                            ALU.mult, ALU.add, [psr, "ff_cw", dres], [dres])
                    def gl(c=c, k=k, no=no):
                        act(gg[k][:, 0:no], gvv[k][:, 0:no], AF.Gelu_apprx_tanh, ["ff_gv%d" % k], ["ff_gg%d" % k])
                        tt("pool", hm[:, c, 0:no], gg[k][:, 0:no], av[k][:, 0:no], ALU.mult,
                           ["ff_gg%d" % k, "ff_a%d" % k], [("ff_hm", c)])
                    while defer:
                        defer.pop(0)()
                    defer.append(gl)
                while defer:
                    defer.pop(0)()
                if w + 1 < len(wins):
                    n_o0, n_o1, n_u0, n_u1 = wins[w + 1]
                    fm_norm(xs, "ff_x", 8, n_u1 - n_u0, g2s, "ff_g2", xn, "ff_xn", ps_ss, "ff_pss",
                            tmp, "ff_tmp", rstd, "ff_rstd", xn, "ff_xn", 1.0 / 32.0)
                for m in range(8):
                    i = cnt["xo"] % 2
                    cnt["xo"] += 1
                    S.dma("sp", xres[i][:, 0:no], xsrc_f[:, m, o0:o1], writes=["ff_xr%d" % i], stream="ld2")
                    pd, pdr = nps()
                    for c in range(22):
                        mm(pd[:, 0:no], Wd[:, c, m * 128:(m + 1) * 128], hm[:, c, 0:no], c == 0, c == 21,
                           [("ff_Wd", c), ("ff_hm", c)], [pdr])
                    if not final:
                        tt("dve", xo[i][:, 0:no], pd[:, 0:no], xres[i][:, 0:no], ALU.add, [pdr, "ff_xr%d" % i], ["ff_xo%d" % i])
                        S.dma("sp", xdst_f[:, m, o0:o1], xo[i][:, 0:no], reads=["ff_xo%d" % i],
                              writes=[("scr", "x2", w, m)], stream="st")
                    else:
                        tt("dve", x2b[:, m, 0:no], pd[:, 0:no], xres[i][:, 0:no], ALU.add, [pdr, "ff_xr%d" % i], ["ff_x2b"])
                if final:
                    hres = [("ff_hm", c) for c in range(8)]
                    act(hm[:, 0:8, 0:no], x2b[:, :, 0:no], AF.Square, ["ff_x2b"], hres, scale=1.0 / 32.0)
                    for c in range(8):
                        mm(ps_fn[:, 0:no], ones_bf[:], hm[:, c, 0:no], c == 0, c == 7, ["ones_bf", ("ff_hm", c)], ["ff_psfn"])
                    act(tmpf[:, 0:no], ps_fn[:, 0:no], AF.Ln, ["ff_psfn"], ["ff_tmpf"], bias=EPS, scale=1.0)
                    act(rstdf[:, 0:no], tmpf[:, 0:no], AF.Exp, ["ff_tmpf"], ["ff_rstdf"], scale=-0.5)
                    for m in range(8):
                        i = cnt["xo"] % 2
                        cnt["xo"] += 1
                        stt(xo[i][:, 0:no], x2b[:, m, 0:no], gfs[:, m:m + 1], rstdf[:, 0:no], ALU.mult, ALU.mult,
                            ["ff_x2b", "ff_gf", "ff_rstdf"], ["ff_xo%d" % i])
                        S.dma("sp", y_f[:, m, o0:o1], xo[i][:, 0:no], reads=["ff_xo%d" % i],
                              writes=[("out", w, m)], stream="st")
        S.barrier()

    def stage_final(xsrc):
        with ExitStack() as es:
            def sb(name, shape, dt):
                return es.enter_context(nc.sbuf_tensor(name + SUF[0], list(shape), dt))
            gfs = sb("fn_g", [128, 8], F32)
            xs = [sb("fn_x%d" % i, [128, 8, 512], F32) for i in range(2)]
            sq = sb("fn_sq", [128, 8, 512], BF16)
            tmp = sb("fn_tmp", [128, 512], F32)
            rstd = sb("fn_rstd", [128, 512], F32)
            xo = [sb("fn_xo%d" % i, [128, 8, 512], F32) for i in range(2)]
            ps_ss = es.enter_context(nc.psum_tensor("fn_pss" + SUF[0], [128, 512], F32))
            S.dma("sp", gfs[:], gf, writes=["fn_g"], stream="ld")
            xsrc_f = fm(xsrc)
            y_f = fm(yT)
            S.dma("sp", xs[0][:], xsrc_f[:, :, 0:512], writes=["fn_x0"], stream="ld")
            for b in range(NB):
                sl = b % 2
                t0 = b * 512
                if b + 1 < NB:
                    S.dma("sp", xs[1 - sl][:], xsrc_f[:, :, t0 + 512:t0 + 1024], writes=["fn_x%d" % (1 - sl)], stream="ld")
                fm_norm(xs[sl], "fn_x%d" % sl, 8, 512, gfs, "fn_g", sq, "fn_sq", ps_ss, "fn_pss", tmp, "fn_tmp",
                        rstd, "fn_rstd", xo[sl], "fn_xo%d" % sl, 1.0 / 32.0)
                S.dma("sp", y_f[:, :, t0:t0 + 512], xo[sl][:], reads=["fn_xo%d" % sl], writes=[("out", b)], stream="st")
        S.barrier()

    xcur = xT
    for l in range(L):
        SUF[0] = "_L%d" % l
        if stages is None or "s1" in stages:
            stage1(l, xcur)
        with ExitStack() as es3:
            Wg = es3.enter_context(nc.sbuf_tensor("s3_Wg" + SUF[0], [128, 8, 3072], BF16))
            Wb = es3.enter_context(nc.sbuf_tensor("s3_Wb" + SUF[0], [128, 3, 4, 1024], BF16))
            Wo = es3.enter_context(nc.sbuf_tensor("s3_Wo" + SUF[0], [128, 8, 1024], BF16))
            def pre3(l=l, Wg=Wg, Wb=Wb, Wo=Wo):
                win_l = w_in[l].rearrange("(kc p) n -> p kc n", p=128)
                for kc in range(8):
                    S.dma("pool", Wg[:, kc, :], win_l[:, kc, 3008:6080], writes=["s3_Wg"], stream="w")
                for i in range(3):
                    wv = w_br[i][l].rearrange("(kc p) n -> p kc n", p=128)
                    for kc in range(4):
                        S.dma("pool", Wb[:, i, kc, :], wv[:, kc, :], writes=["s3_Wb"], stream="w")
                wo_l = w_out[l].rearrange("(kc p) n -> p kc n", p=128)
                for kc in range(8):
                    S.dma("pool", Wo[:, kc, :], wo_l[:, kc, :], writes=["s3_Wo"], stream="w")
            if stages is None or "mla" in stages:
                stage_mla(l, pre3 if (stages is None or "s3" in stages) else None)
            elif stages is not None and "s3" in stages:
                pre3()
            if stages is None or "na" in stages:
                stage_na(l)
            if stages is None or "gqa" in stages:
                stage_gqa(l)
            if stages is None or "s3" in stages:
                stage3(l, xcur, XA, (Wg, Wb, Wo))
        if stages is None or "ffn" in stages:
            stage_ffn(l, XA, XB, final=(l == L - 1))
        xcur = XB
    S.barrier()
    S.emit()
    return nc


def prep_shared(inputs, T, L):
    f = lambda a: np.ascontiguousarray(np.asarray(a, dtype=np.float32))
    cls_of_tile, classes = na_classes(T)
    C, Sn = rope_tables_fm(T)
    sh = {
        "w_in": f(inputs["w_in"]), "mla_w_uq": f(inputs["mla_w_uq"]), "mla_w_ukv": f(inputs["mla_w_ukv"]),
        "w_br_na": f(inputs["w_br_na"]), "w_br_mla": f(inputs["w_br_mla"]), "w_br_gqa": f(inputs["w_br_gqa"]),
        "w_out": f(inputs["w_out"]), "w_up": f(inputs["w_up"]), "w_down": f(inputs["w_down"]),
        "g1": np.stack([pcol(f(inputs["norm1_g"])[l], 8) for l in range(L)]),
        "g2": np.stack([pcol(f(inputs["norm2_g"])[l], 8) for l in range(L)]),
        "gf": pcol(f(inputs["final_g"]), 8),
        "bg": np.stack([pcol(f(inputs["b_gate"])[l], 24) for l in range(L)]),
        "gqa": np.stack([pcol(f(inputs["mla_qa_g"])[l], 3) for l in range(L)]),
        "gkva": np.stack([pcol(f(inputs["mla_kva_g"])[l], 2) for l in range(L)]),
        "sinkr": np.ascontiguousarray(np.broadcast_to(f(inputs["gqa_sink"])[:, None, :], (L, 64, 8))),
        "cw": np.stack([np.ascontiguousarray(
            f(inputs["conv_w"])[l].reshape(3, 44, 128).transpose(2, 1, 0).reshape(128, 132)) for l in range(L)]),
        "cb": np.stack([pcol(f(inputs["conv_b"])[l], 44) for l in range(L)]),
        "ropeC": C, "ropeS": Sn,
        "nab": na_bias_tables(f(inputs["na_rpb"]), classes).reshape(L, len(classes), 128, 5 * 8 * 128),
        "gmask": gqa_masks(),
    }
    return sh, cls_of_tile, len(classes)


_CACHE = {}


def kernel(**inputs):
    x = np.asarray(inputs["x"], dtype=np.float32)
    B, T, _ = x.shape
    L = np.asarray(inputs["w_in"]).shape[0]
    sh, cls_of_tile, ncls = prep_shared(inputs, T, L)
    key = (T, L, ncls)
    if key not in _CACHE:
        _CACHE[key] = build(T, L, ncls, cls_of_tile)
    nc = _CACHE[key]
    in_maps = []
    for b in range(B):
        m = dict(sh)
        m["xT"] = np.ascontiguousarray(x[b].T)
        in_maps.append(m)
    res = run_bass_kernel_spmd(nc, in_maps, core_ids=list(range(B)))
    out = np.empty((B, T, D), np.float32)
    for b in range(B):
        out[b] = np.asarray(res.results[b]["yT"]).T
    return out
```

```python
import numpy as np
import ml_dtypes
from contextlib import ExitStack
import concourse.bass as bass
import concourse.mybir as mybir
from concourse.bass_utils import run_bass_kernel_spmd

F32 = mybir.dt.float32
BF16 = mybir.dt.bfloat16
AF = mybir.ActivationFunctionType
ALU = mybir.AluOpType

D = 1024
L_FULL = 2
T_FULL = 8192
GRID_W = 64
EPS = 1e-6
D_FF = 2816
IN_W = 6080
NEG = -30000.0
COMPUTE = ("pe", "act", "dve", "pool")


class Sched:
    def __init__(self, nc, n_dma_sems=8):
        self.nc = nc
        self.ins = {e: [] for e in ("pe", "act", "dve", "pool", "sp")}
        self.known = {e: {} for e in self.ins}
        self.res = {}
        self.milestones = {e: set() for e in COMPUTE}
        self.streams = {}
        self.n_dma_sems = n_dma_sems
        self.dma_sem_keys = []
        self.dma_latest = {}
        self.last_compute = {e: 0 for e in COMPUTE}

    def _r(self, key):
        r = self.res.get(key)
        if r is None:
            r = ({}, {})
            self.res[key] = r
        return r

    def _deps(self, eng, reads, writes, is_dma):
        deps = {}
        me = ("E", eng)

        def add(k, v, same_ok):
            if (not is_dma) and same_ok and k == me:
                return
            if deps.get(k, 0) < v:
                deps[k] = v

        for r in reads:
            W, R = self._r(r)
            for k, v in W.items():
                add(k, v, False)
        for w in writes:
            W, R = self._r(w)
            for k, v in W.items():
                add(k, v, True)
            for k, v in R.items():
                add(k, v, True)
        out = []
        kn = self.known[eng]
        for k, v in deps.items():
            if kn.get(k, 0) >= v:
                continue
            kn[k] = v
            out.append((k, v))
            if k[0] == "E":
                self.milestones[k[1]].add(v)
        return out

    def _update(self, tok, reads, writes):
        k, v = tok
        for r in reads:
            W, R = self._r(r)
            if R.get(k, 0) < v:
                R[k] = v
        for w in writes:
            W, R = self._r(w)
            W.clear()
            R.clear()
            W[k] = v

    def op(self, eng, name, reads=(), writes=(), **kw):
        waits = self._deps(eng, reads, writes, False)
        self.ins[eng].append((name, kw, waits, None))
        tok = (("E", eng), len(self.ins[eng]))
        self.last_compute[eng] = len(self.ins[eng])
        self._update(tok, reads, writes)

    def dma(self, q, out, in_, reads=(), writes=(), stream="ld"):
        st = self.streams.get(stream)
        if st is None:
            base = len(self.dma_sem_keys)
            keys = [("D", base + i) for i in range(self.n_dma_sems)]
            self.dma_sem_keys += keys
            st = {"keys": keys, "n": 0}
            self.streams[stream] = st
        i = st["n"]
        st["n"] += 1
        K = len(st["keys"])
        key = st["keys"][i % K]
        val = 16 * (i // K + 1)
        waits = self._deps(q, reads, writes, True)
        if val > 16 and self.known[q].get(key, 0) < val - 16:
            self.known[q][key] = val - 16
            waits.append((key, val - 16))
        self.ins[q].append(("dma_start", dict(out=out, in_=in_), waits, (key, 16)))
        self.dma_latest[key] = val
        self._update((key, val), reads, writes)

    def barrier(self):
        toks = []
        for e in COMPUTE:
            if self.last_compute[e]:
                toks.append((("E", e), self.last_compute[e]))
        for k, v in self.dma_latest.items():
            toks.append((k, v))
        for e in self.ins:
            waits = []
            for k, v in toks:
                if k == ("E", e):
                    continue
                if self.known[e].get(k, 0) >= v:
                    continue
                self.known[e][k] = v
                waits.append((k, v))
                if k[0] == "E":
                    self.milestones[k[1]].add(v)
            if waits:
                self.ins[e].append((None, None, waits, None))
        self.res = {}

    def emit(self):
        nc = self.nc
        sems = {}
        for e in COMPUTE:
            sems[("E", e)] = nc.alloc_semaphore("sem_" + e)
        for k in self.dma_sem_keys:
            sems[k] = nc.alloc_semaphore("semd%d" % k[1])
        rank = {}
        for e in COMPUTE:
            ms = sorted(self.milestones[e])
            rank[e] = {v: i + 1 for i, v in enumerate(ms)}

        def run(engine, lst, ename):
            my = rank.get(ename, {})
            mysem = sems.get(("E", ename))
            for idx, (name, kw, waits, dinc) in enumerate(lst):
                for k, v in waits:
                    if k[0] == "E":
                        engine.wait_ge(sems[k], rank[k[1]][v])
                    else:
                        engine.wait_ge(sems[k], v)
                if name is None:
                    continue
                r = getattr(engine, name)(**kw)
                if dinc is not None:
                    r.then_inc(sems[dinc[0]], dinc[1])
                elif (idx + 1) in my:
                    r.then_inc(mysem, 1)

        with nc.Block() as block:
            @block.tensor
            def _(e):
                run(e, self.ins["pe"], "pe")

            @block.scalar
            def _(e):
                run(e, self.ins["act"], "act")

            @block.vector
            def _(e):
                run(e, self.ins["dve"], "dve")

            @block.gpsimd
            def _(e):
                run(e, self.ins["pool"], "pool")

            @block.sync
            def _(e):
                run(e, self.ins["sp"], "sp")


def rope_tables_fm(T):
    inv = 1.0 / (10000.0 ** (np.arange(0, 64, 2, dtype=np.float32) / 64.0))
    ang = np.arange(T, dtype=np.float32)[:, None] * inv[None, :]
    c = np.cos(ang).astype(np.float32).T
    s = np.sin(ang).astype(np.float32).T
    C = np.concatenate([c, c, c, c], axis=0)
    S = np.concatenate([-s, s, -s, s], axis=0)
    return np.ascontiguousarray(C), np.ascontiguousarray(S)


def na_classes(T):
    rows = T // GRID_W
    NT = T // 128
    kk = np.arange(128)
    qq = np.arange(128)
    cls_of_tile = []
    classes = []
    keys = {}
    for t in range(NT):
        kt0 = min(max(t - 2, 0), NT - 5)
        r = 2 * t + qq // 64
        cq = qq % 64
        r_start = np.clip(r - 4, 0, rows - 8)
        c_start = np.clip(cq - 8, 0, GRID_W - 16)
        valid = np.zeros((128, 5, 128), bool)
        ri = np.zeros((128, 5, 128), np.int64)
        ci = np.zeros((128, 5, 128), np.int64)
        for j in range(5):
            kt = kt0 + j
            rk = 2 * kt + kk // 64
            ck = kk % 64
            v = ((rk[:, None] >= r_start[None, :]) & (rk[:, None] < r_start[None, :] + 8)
                 & (ck[:, None] >= c_start[None, :]) & (ck[:, None] < c_start[None, :] + 16))
            valid[:, j, :] = v
            ri[:, j, :] = np.clip(rk[:, None] - r[None, :] + 7, 0, 14)
            ci[:, j, :] = np.clip(ck[:, None] - cq[None, :] + 15, 0, 30)
        key = (valid.tobytes(), ri.tobytes(), ci.tobytes())
        if key not in keys:
            keys[key] = len(classes)
            classes.append((valid, ri, ci))
        cls_of_tile.append(keys[key])
    return cls_of_tile, classes


def na_bias_tables(rpb, classes):
    Lx = rpb.shape[0]
    out = np.empty((Lx, len(classes), 128, 5, 8, 128), np.float32)
    for c, (valid, ri, ci) in enumerate(classes):
        g = rpb[:, :, ri, ci]
        g = np.where(valid[None, None], g, np.float32(NEG))
        out[:, c] = g.transpose(0, 2, 3, 1, 4)
    return out


def gqa_masks():
    kk = np.arange(128)[:, None]
    qq = np.arange(128)[None, :]
    m_prev = np.where(qq <= kk, 0.0, -1.0e6).astype(np.float32)
    m_next = np.where(kk <= qq, 0.0, -1.0e6).astype(np.float32)
    m = np.stack([np.tile(m_prev, (1, 4)), np.tile(m_next, (1, 4))], axis=0)
    return np.ascontiguousarray(m)


def pcol(v, nchunk):
    return np.ascontiguousarray(v.reshape(nchunk, 128).T)


def build(T, L, ncls, cls_of_tile, debug=False, stages=None):
    NB = T // 512
    NT = T // 128
    nc = bass.Bass("TRN2", target_bir_lowering=False)
    S = Sched(nc)

    def din(name, shape, dt=F32):
        return nc.dram_tensor(name, list(shape), dt, kind="ExternalInput").ap()

    def dscr(name, shape, dt):
        return nc.dram_tensor(name, list(shape), dt, kind=("ExternalOutput" if debug else "Internal")).ap()

    xT = din("xT", [D, T])
    w_in = din("w_in", [L, D, IN_W])
    w_uq = din("mla_w_uq", [L, 384, 768])
    w_ukv = din("mla_w_ukv", [L, 256, 1024])
    w_br = [din("w_br_na", [L, 512, D]), din("w_br_mla", [L, 512, D]), din("w_br_gqa", [L, 512, D])]
    w_out = din("w_out", [L, D, D])
    w_up = din("w_up", [L, D, 2 * D_FF])
    w_down = din("w_down", [L, D_FF, D])
    g1 = din("g1", [L, 128, 8])
    g2 = din("g2", [L, 128, 8])
    gf = din("gf", [128, 8])
    bg = din("bg", [L, 128, 24])
    gqa_ = din("gqa", [L, 128, 3])
    gkva = din("gkva", [L, 128, 2])
    sinkr = din("sinkr", [L, 64, 8])
    cw = din("cw", [L, 128, 44 * 3])
    cb = din("cb", [L, 128, 44])
    ropeC = din("ropeC", [128, T])
    ropeS = din("ropeS", [128, T])
    nab = din("nab", [L, ncls, 128, 5 * 8 * 128])
    gmask = din("gmask", [2, 128, 512])

    yT = nc.dram_tensor("yT", [D, T], F32, kind="ExternalOutput").ap()

    XA = dscr("XA", [D, T], F32)
    XB = dscr("XB", [D, T], F32)
    naq = dscr("naq", [8, 64, T], BF16)
    nak = dscr("nak", [8, 64, T], BF16)
    nav = dscr("nav", [T, 512], BF16)
    mqn = dscr("mqn", [4, 128, T], BF16)
    mqp = dscr("mqp", [4, 64, T], BF16)
    mkn = dscr("mkn", [4, 128, T], BF16)
    mkp = dscr("mkp", [64, T], BF16)
    mv = dscr("mv", [4, 128, NT, 128], BF16)
    gq = dscr("gq", [8, 64, T], BF16)
    gk = dscr("gk", [2, 64, T], BF16)
    gv = dscr("gv", [128, NT, 128], BF16)
    yna = dscr("yna", [512, T], BF16)
    ymla = dscr("ymla", [512, T], BF16)
    ygqa = dscr("ygqa", [512, T], BF16)

    def fm(ap):
        return ap.rearrange("(kc p) t -> p kc t", p=128)

    def mm(out, lhsT, rhs, start, stop, reads, writes, skip=False):
        kw = dict(out=out, lhsT=lhsT, rhs=rhs, start=start, stop=stop)
        if skip:
            kw["skip_group_check"] = True
        S.op("pe", "matmul", reads=reads, writes=writes, **kw)

    def act(out, in_, func, reads, writes, **kw):
        S.op("act", "activation", reads=reads, writes=writes, out=out, in_=in_, func=func, **kw)

    def tt(eng, out, in0, in1, op, reads, writes):
        S.op(eng, "tensor_tensor", reads=reads, writes=writes, out=out, in0=in0, in1=in1, op=op)

    def ts(eng, out, in0, s1, s2, op0, op1, reads, writes):
        kw = dict(out=out, in0=in0, scalar1=s1, scalar2=s2, op0=op0)
        if op1 is not None:
            kw["op1"] = op1
        S.op(eng, "tensor_scalar", reads=reads, writes=writes, **kw)

    def stt(out, in0, scalar, in1, op0, op1, reads, writes):
        S.op("dve", "scalar_tensor_tensor", reads=reads, writes=writes, out=out, in0=in0, scalar=scalar,
             in1=in1, op0=op0, op1=op1)

    def cp(eng, out, in_, reads, writes):
        if eng == "act":
            act(out, in_, AF.Copy, reads, writes)
        else:
            S.op(eng, "tensor_copy", reads=reads, writes=writes, out=out, in_=in_)

    def recip(out, in_, reads, writes):
        S.op("dve", "reciprocal", reads=reads, writes=writes, out=out, in_=in_)

    evac_rr = [0]
    SUF = [""]

    def evac(out, in_, reads, writes):
        evac_rr[0] ^= 1
        cp("act" if evac_rr[0] else "dve", out, in_, reads, writes)

    ones_bf = nc.alloc_sbuf_tensor("ones_bf", [128, 128], BF16)
    ones_f = nc.alloc_sbuf_tensor("ones_f", [128, 128], F32)
    S.op("pool", "memset", writes=["ones_bf"], ap=ones_bf[:], constant=1.0)
    S.op("pool", "memset", writes=["ones_f"], ap=ones_f[:], constant=1.0)

    def fm_norm(xs, xres, nch, n, gcol, gres, sq, sqres, ps, psres, tmp, tmpres, rstd, rstdres, xn, xnres, inv_sqrt_dim):
        act(sq[:, 0:nch, 0:n], xs[:, 0:nch, 0:n], AF.Square, [xres], [sqres], scale=inv_sqrt_dim)
        for c in range(nch):
            mm(ps[:, 0:n], ones_bf[:], sq[:, c, 0:n], c == 0, c == nch - 1, ["ones_bf", sqres], [psres])
        act(tmp[:, 0:n], ps[:, 0:n], AF.Ln, [psres], [tmpres], bias=EPS, scale=1.0)
        act(rstd[:, 0:n], tmp[:, 0:n], AF.Exp, [tmpres], [rstdres], scale=-0.5)
        for c in range(nch):
            stt(xn[:, c, 0:n], xs[:, c, 0:n], gcol[:, c:c + 1], rstd[:, 0:n], ALU.mult, ALU.mult,
                [xres, gres, rstdres], [xnres])

    def run_pipeline(steps, depth, defer):
        n = len(steps)
        ring = {}
        pend = []
        for idx in range(n + depth):
            if idx < n:
                ring[idx] = steps[idx][0]()
            if idx >= depth:
                fl = steps[idx - depth][1](ring.pop(idx - depth))
                if fl is not None:
                    for d, f in fl:
                        pend.append((idx + d, f))
                    pend.sort(key=lambda x: x[0])
            while pend and pend[0][0] <= idx:
                pend.pop(0)[1]()
        for _, f in pend:
            f()

    def stage1(l, xsrc):
        with ExitStack() as es:
            def sb(name, shape, dt):
                return es.enter_context(nc.sbuf_tensor(name + SUF[0], list(shape), dt))

            def pst(name, shape):
                return es.enter_context(nc.psum_tensor(name + SUF[0], list(shape), F32))
            Wa = sb("s1_Wa", [128, 8, 3008], BF16)
            Wsw = sb("s1_Wsw", [128, 8, 704], BF16)
            Wq = sb("s1_Wq", [128, 3, 768], BF16)
            Wqp = sb("s1_Wqp", [128, 3, 256], BF16)
            Wqps = sb("s1_Wqps", [128, 3, 256], BF16)
            Wkv = sb("s1_Wkv", [128, 2, 1024], BF16)
            Wv = sb("s1_Wv", [128, 2, 512], BF16)
            g1s = sb("s1_g1", [128, 8], F32)
            gqs = sb("s1_gq", [128, 3], F32)
            gks = sb("s1_gk", [128, 2], F32)
            xs = [sb("s1_x%d" % i, [128, 8, 512], F32) for i in range(2)]
            xn = [sb("s1_xn%d" % i, [128, 8, 512], BF16) for i in range(2)]
            sq = sb("s1_sq", [128, 8, 512], BF16)
            tmp = sb("s1_tmp", [128, 512], F32)
            rstd = sb("s1_rstd", [128, 512], F32)
            Cs = [sb("s1_C%d" % i, [128, 512], F32) for i in range(2)]
            Ss = [sb("s1_S%d" % i, [128, 512], F32) for i in range(2)]
            cq = sb("s1_cq", [128, 3, 512], F32)
            ckv = sb("s1_ckv", [128, 2, 512], F32)
            cqn = sb("s1_cqn", [128, 3, 512], BF16)
            ckvn = sb("s1_ckvn", [128, 2, 512], BF16)
            NST = 6
            stg = [sb("s1_stg%d" % i, [128, 512], BF16) for i in range(NST)]
            t1 = [sb("s1_t1_%d" % i, [128, 512], F32) for i in range(2)]
            t2 = [sb("s1_t2_%d" % i, [128, 512], F32) for i in range(2)]
            NPS = 5
            pss = [pst("s1_ps%d" % i, [128, 512]) for i in range(NPS)]
            ps_ss = pst("s1_pss", [128, 512])
            cnt = {"ps": 0, "stg": 0, "t": 0}

            def nps():
                i = cnt["ps"] % NPS
                cnt["ps"] += 1
                return pss[i], "s1_ps%d" % i

            def nstg():
                i = cnt["stg"] % NST
                cnt["stg"] += 1
                return stg[i], "s1_stg%d" % i

            win_l = w_in[l].rearrange("(kc p) n -> p kc n", p=128)
            WA_GROUPS = [(1536, 2176), (0, 512), (512, 1024), (2176, 2880), (1024, 1536), (2880, 3008)]

            def wa_res(c0):
                for gi, (a0, a1) in enumerate(WA_GROUPS):
                    if a0 <= c0 < a1:
                        return ("s1_Wa", gi)
                raise ValueError(c0)

            def load_wa(gi):
                a0, a1 = WA_GROUPS[gi]
                S.dma("pool", Wa[:, :, a0:a1], win_l[:, :, a0:a1], writes=[("s1_Wa", gi)], stream="w")
            load_wa(0)
            load_wa(1)
            load_wa(2)
            load_wa(3)
            load_wa(4)
            load_wa(5)
            wuq_l = w_uq[l].rearrange("(kc p) n -> p kc n", p=128)
            S.dma("pool", Wq[:], wuq_l, writes=["s1_Wq"], stream="w")
            wukv_l = w_ukv[l].rearrange("(kc p) n -> p kc n", p=128)
            S.dma("pool", Wkv[:], wukv_l, writes=["s1_Wkv"], stream="w")
            for (src0, nh, dst0) in ((2176, 1, 0), (2240, 8, 64), (2752, 2, 576)):
                srcv = Wa[:, :, src0:src0 + nh * 64].rearrange("p k (h two r) -> p k h two r", two=2, r=32)
                dstv = Wsw[:, :, dst0:dst0 + nh * 64].rearrange("p k (h two r) -> p k h two r", two=2, r=32)
                S.op("pool", "tensor_copy", reads=[("s1_Wa", 3)], writes=["s1_Wsw"], out=dstv[:, :, :, 0, :], in_=srcv[:, :, :, 1, :])
                S.op("pool", "tensor_copy", reads=[("s1_Wa", 3)], writes=["s1_Wsw"], out=dstv[:, :, :, 1, :], in_=srcv[:, :, :, 0, :])
            wq4 = Wq[:].rearrange("p k (h c) -> p k h c", c=192)
            S.op("pool", "tensor_copy", reads=["s1_Wq"], writes=["s1_Wqp"],
                 out=Wqp[:].rearrange("p k (h c) -> p k h c", c=64), in_=wq4[:, :, :, 128:192])
            wqps4 = Wqps[:].rearrange("p k (h c) -> p k h c", c=64)
            S.op("pool", "tensor_copy", reads=["s1_Wq"], writes=["s1_Wqps"], out=wqps4[:, :, :, 0:32], in_=wq4[:, :, :, 160:192])
            S.op("pool", "tensor_copy", reads=["s1_Wq"], writes=["s1_Wqps"], out=wqps4[:, :, :, 32:64], in_=wq4[:, :, :, 128:160])
            S.op("pool", "tensor_copy", reads=["s1_Wkv"], writes=["s1_Wv"],
                 out=Wv[:].rearrange("p k (h c) -> p k h c", c=128),
                 in_=Wkv[:].rearrange("p k (h c) -> p k h c", c=256)[:, :, :, 128:256])
            S.dma("sp", g1s[:], g1[l], writes=["s1_g1"], stream="ld")
            S.dma("sp", gqs[:], gqa_[l], writes=["s1_gq"], stream="ld")
            S.dma("sp", gks[:], gkva[l], writes=["s1_gk"], stream="ld")

            xsrc_f = fm(xsrc)
            naq_f = naq.rearrange("h d t -> (h d) t")
            nak_f = nak.rearrange("h d t -> (h d) t")
            mqp_f = mqp.rearrange("h d t -> (h d) t")
            gq_f = gq.rearrange("h d t -> (h d) t")
            gk_f = gk.rearrange("h d t -> (h d) t")

            def load(b):
                sl = b % 2
                t0 = b * 512
                S.dma("sp", xs[sl][:], xsrc_f[:, :, t0:t0 + 512], reads=[("X", id(xsrc), b)], writes=["s1_x%d" % sl], stream="ld")
                S.dma("sp", Cs[sl][:], ropeC[:, t0:t0 + 512], writes=["s1_C%d" % sl], stream="ld")
                S.dma("sp", Ss[sl][:], ropeS[:, t0:t0 + 512], writes=["s1_S%d" % sl], stream="ld")

            def store(dst, src, srcres, dstres):
                S.dma("sp", dst, src, reads=[srcres], writes=[dstres], stream="st")

            def normA(xin, nch, sqb, sqres, xres, scl):
                act(sqb[:, 0:nch, :], xin[:, 0:nch, :], AF.Square, [xres], [sqres], scale=scl)

            def normB(sqb, nch, ps, psres, sqres):
                for c in range(nch):
                    mm(ps[:], ones_bf[:], sqb[:, c, :], c == 0, c == nch - 1, ["ones_bf", sqres], [psres])

            def normC(ps, psres, tm, tmres, rs, rsres, xin, xres, nch, gcol, gres, xo, xores):
                act(tm[:], ps[:], AF.Ln, [psres], [tmres], bias=EPS, scale=1.0)
                act(rs[:], tm[:], AF.Exp, [tmres], [rsres], scale=-0.5)
                for c in range(nch):
                    stt(xo[:, c, :], xin[:, c, :], gcol[:, c:c + 1], rs[:], ALU.mult, ALU.mult,
                        [xres, gres, rsres], [xores])

            sqc = sb("s1_sqc", [128, 5, 512], BF16)
            tmp_c = sb("s1_tmpc", [128, 512], F32)
            rstd_c = sb("s1_rstdc", [128, 512], F32)
            tmp_k = sb("s1_tmpk", [128, 512], F32)
            rstd_k = sb("s1_rstdk", [128, 512], F32)
            ps_ssc = pst("s1_pssc", [128, 512])
            ps_ssk = pst("s1_pssk", [128, 512])

            def xnorm(b):
                sl_ = b % 2
                xr_, xnr_ = "s1_x%d" % sl_, "s1_xn%d" % sl_
                normA(xs[sl_], 8, sq, "s1_sq", xr_, 1.0 / 32.0)
                normB(sq, 8, ps_ss, "s1_pss", "s1_sq")
                normC(ps_ss, "s1_pss", tmp, "s1_tmp", rstd, "s1_rstd", xs[sl_], xr_, 8, g1s, "s1_g1", xn[sl_], xnr_)

            load(0)
            xnorm(0)
            for b in range(NB):
                sl = b % 2
                t0 = b * 512
                if b + 1 < NB:
                    load(b + 1)
                xr, xnr = "s1_x%d" % sl, "s1_xn%d" % sl
                X = xn[sl]

                def proj_fm(W, Wres, c0, M):
                    ps, psr = nps()
                    if Wres == "s1_Wa":
                        Wres = wa_res(c0)
                    for kc in range(8):
                        mm(ps[0:M, :], W[:, kc, c0:c0 + M], X[:, kc, :], kc == 0, kc == 7, [Wres, xnr], [psr])
                    return ps, psr

                def plain_fm(c0, M, dst):
                    ps, psr = proj_fm(Wa, "s1_Wa", c0, M)
                    st_, sr = nstg()
                    evac(st_[0:M, :], ps[0:M, :], [psr], [sr])
                    store(dst, st_[0:M, :], sr, ("scr", id(dst), b))

                def rope_fm(c0, csw, M, dst, dres):
                    pa, par = proj_fm(Wa, "s1_Wa", c0, M)
                    pb, pbr = proj_fm(Wsw, "s1_Wsw", csw, M)
                    i = cnt["t"] % 2
                    cnt["t"] += 1
                    tt("dve", t1[i][0:M, :], pa[0:M, :], Cs[sl][0:M, :], ALU.mult, [par, "s1_C%d" % sl], ["s1_t1_%d" % i])
                    tt("dve", t2[i][0:M, :], pb[0:M, :], Ss[sl][0:M, :], ALU.mult, [pbr, "s1_S%d" % sl], ["s1_t2_%d" % i])
                    st_, sr = nstg()
                    tt("pool", st_[0:M, :], t1[i][0:M, :], t2[i][0:M, :], ALU.add, ["s1_t1_%d" % i, "s1_t2_%d" % i], [sr])
                    store(dst, st_[0:M, :], sr, dres)

                for c in range(3):
                    ps, psr = proj_fm(Wa, "s1_Wa", 1536 + c * 128, 128)
                    cp("dve", cq[:, c, :], ps[:], [psr], ["s1_cq"])
                for c in range(2):
                    ps, psr = proj_fm(Wa, "s1_Wa", 1920 + c * 128, 128)
                    cp("dve", ckv[:, c, :], ps[:], [psr], ["s1_ckv"])
                normA(cq, 3, sqc[:, 0:3, :], "s1_sqc", "s1_cq", 1.0 / np.sqrt(384.0))
                normA(ckv, 2, sqc[:, 3:5, :], "s1_sqk", "s1_ckv", 1.0 / 16.0)
                for c in range(4):
                    plain_fm(c * 128, 128, naq_f[c * 128:(c + 1) * 128, t0:t0 + 512])
                normB(sqc[:, 0:3, :], 3, ps_ssc, "s1_pssc", "s1_sqc")
                normB(sqc[:, 3:5, :], 2, ps_ssk, "s1_pssk", "s1_sqk")
                normC(ps_ssc, "s1_pssc", tmp_c, "s1_tmpc", rstd_c, "s1_rstdc", cq, "s1_cq", 3, gqs, "s1_gq", cqn, "s1_cqn")
                normC(ps_ssk, "s1_pssk", tmp_k, "s1_tmpk", rstd_k, "s1_rstdk", ckv, "s1_ckv", 2, gks, "s1_gk", ckvn, "s1_ckvn")
                for c in range(4):
                    plain_fm(512 + c * 128, 128, nak_f[c * 128:(c + 1) * 128, t0:t0 + 512])
                rope_fm(2176, 0, 64, mkp[:, t0:t0 + 512], ("scr", "mkp", b))
                for c in range(4):
                    rope_fm(2240 + c * 128, 64 + c * 128, 128, gq_f[c * 128:(c + 1) * 128, t0:t0 + 512], ("scr", "gq", b, c))
                rope_fm(2752, 576, 128, gk_f[:, t0:t0 + 512], ("scr", "gk", b))
                for i in range(4):
                    ps, psr = nps()
                    for kc in range(8):
                        mm(ps[:], X[:, kc, i * 128:(i + 1) * 128], Wa[:, kc, 1024:1536], kc == 0, kc == 7, [wa_res(1024), xnr], [psr])
                    st_, sr = nstg()
                    evac(st_[:], ps[:], [psr], [sr])
                    store(nav[t0 + i * 128:t0 + (i + 1) * 128, :], st_[:], sr, ("scr", "nav", b, i))
                ps, psr = nps()
                for i in range(4):
                    for kc in range(8):
                        mm(ps[:, i * 128:(i + 1) * 128], X[:, kc, i * 128:(i + 1) * 128], Wa[:, kc, 2880:3008],
                           kc == 0, kc == 7, [wa_res(2880), xnr], [psr])
                st_, sr = nstg()
                evac(st_[:], ps[:], [psr], [sr])
                store(gv[:, b * 4:(b + 1) * 4, :],
                      st_[:].rearrange("p (i d) -> p i d", d=128), sr, ("scr", "gv", b))
                if b + 1 < NB:
                    xnorm(b + 1)
                for h in range(4):
                    ps, psr = nps()
                    for kc in range(3):
                        mm(ps[:], Wq[:, kc, h * 192:h * 192 + 128], cqn[:, kc, :], kc == 0, kc == 2, ["s1_Wq", "s1_cqn"], [psr])
                    st_, sr = nstg()
                    evac(st_[:], ps[:], [psr], [sr])
                    store(mqn[h, :, t0:t0 + 512], st_[:], sr, ("scr", "mqn", b, h))
                for hp in range(2):
                    pa, par = nps()
                    for kc in range(3):
                        mm(pa[:], Wqp[:, kc, hp * 128:(hp + 1) * 128], cqn[:, kc, :], kc == 0, kc == 2, ["s1_Wqp", "s1_cqn"], [par])
                    pb, pbr = nps()
                    for kc in range(3):
                        mm(pb[:], Wqps[:, kc, hp * 128:(hp + 1) * 128], cqn[:, kc, :], kc == 0, kc == 2, ["s1_Wqps", "s1_cqn"], [pbr])
                    i = cnt["t"] % 2
                    cnt["t"] += 1
                    tt("dve", t1[i][:], pa[:], Cs[sl][:], ALU.mult, [par, "s1_C%d" % sl], ["s1_t1_%d" % i])
                    tt("dve", t2[i][:], pb[:], Ss[sl][:], ALU.mult, [pbr, "s1_S%d" % sl], ["s1_t2_%d" % i])
                    st_, sr = nstg()
                    tt("pool", st_[:], t1[i][:], t2[i][:], ALU.add, ["s1_t1_%d" % i, "s1_t2_%d" % i], [sr])
                    store(mqp_f[hp * 128:(hp + 1) * 128, t0:t0 + 512], st_[:], sr, ("scr", "mqp", b, hp))
                for h in range(4):
                    ps, psr = nps()
                    for kc in range(2):
                        mm(ps[:], Wkv[:, kc, h * 256:h * 256 + 128], ckvn[:, kc, :], kc == 0, kc == 1, ["s1_Wkv", "s1_ckvn"], [psr])
                    st_, sr = nstg()
                    evac(st_[:], ps[:], [psr], [sr])
                    store(mkn[h, :, t0:t0 + 512], st_[:], sr, ("scr", "mkn", b, h))
                for i in range(4):
                    ps, psr = nps()
                    for kc in range(2):
                        mm(ps[:], ckvn[:, kc, i * 128:(i + 1) * 128], Wv[:, kc, :], kc == 0, kc == 1, ["s1_Wv", "s1_ckvn"], [psr])
                    st_, sr = nstg()
                    evac(st_[:], ps[:], [psr], [sr])
                    store(mv[:, :, b * 4 + i, :].rearrange("h p d -> p h d"),
                          st_[:].rearrange("p (h d) -> p h d", d=128), sr, ("scr", "mv", b, i))
        S.barrier()

    def stage_mla(l, prefetch=None):
        scale = 192.0 ** -0.5
        with ExitStack() as es:
            def sb(name, shape, dt):
                return es.enter_context(nc.sbuf_tensor(name + SUF[0], list(shape), dt))

            def pst(name, shape):
                return es.enter_context(nc.psum_tensor(name + SUF[0], list(shape), F32))
            kpe = sb("ml_kpe", [128, T], BF16)
            Ks = [sb("ml_K%d" % i, [128, T], BF16) for i in range(2)]
            Vs = [sb("ml_V%d" % i, [128, NT, 128], BF16) for i in range(2)]
            Qn = [sb("ml_Qn%d" % i, [128, 512], BF16) for i in range(2)]
            Qp = [sb("ml_Qp%d" % i, [128, 512], BF16) for i in range(2)]
            NP = 3
            NPT = 4
            pt = [sb("ml_pt%d" % i, [128, 512], BF16) for i in range(NPT)]
            acc = [sb("ml_acc%d" % i, [128, 512], F32) for i in range(2)]
            tr1 = sb("ml_tr1", [128, 512], BF16)
            tr2 = sb("ml_tr2", [128, 512], BF16)
            tr3 = sb("ml_tr3", [128, 512], BF16)
            lnt = sb("ml_ln", [128, 512], F32)
            rec = sb("ml_rec", [128, 512], F32)
            yst = [sb("ml_y%d" % i, [128, 512], BF16) for i in range(2)]
            ps_s = [pst("ml_pss%d" % i, [128, 512]) for i in range(NP)]
            ps_o = [pst("ml_pso%d" % i, [128, 512]) for i in range(2)]
            ps_sum = pst("ml_psum", [128, 512])

            S.op("pool", "memset", writes=["ml_kpe"], ap=kpe[64:128, :], constant=0.0)
            for i in range(2):
                S.op("pool", "memset", writes=["ml_Qp%d" % i], ap=Qp[i][64:128, :], constant=0.0)
            S.dma("sp", kpe[0:64, :], mkp[:, :], writes=["ml_kpe"], stream="ld")

            def loadKV(h):
                s = h % 2
                S.dma("sp", Ks[s][:], mkn[h, :, :], writes=["ml_K%d" % s], stream="ld")
                S.dma("sp", Vs[s][:], mv[h], writes=["ml_V%d" % s], stream="ld")

            def loadQ(h, qb, it):
                s = it % 2
                S.dma("sp", Qn[s][:], mqn[h, :, qb * 512:(qb + 1) * 512], writes=["ml_Qn%d" % s], stream="ld")
                S.dma("sp", Qp[s][0:64, :], mqp[h, :, qb * 512:(qb + 1) * 512], writes=["ml_Qp%d" % s], stream="ld")

            items = [(h, qb) for h in range(4) for qb in range(NB)]
            loadKV(0)
            loadQ(0, 0, 0)
            gcount = [0]
            steps = []
            flat = []
            for it, (h, qb) in enumerate(items):
                hs = h % 2
                qs = it % 2
                for kb in range(NT):
                    def qk(it=it, h=h, qb=qb, hs=hs, qs=qs, kb=kb):
                        if kb == min(8, NT - 1) and it == 0 and prefetch is not None:
                            prefetch()
                        if kb == 0 and it + 1 < len(items):
                            nh, nqb = items[it + 1]
                            if nh != h:
                                loadKV(nh)
                            loadQ(nh, nqb, it + 1)
                        i = gcount[0] % NP
                        ip = gcount[0] % NPT
                        gcount[0] += 1
                        mm(ps_s[i][:], Ks[hs][:, kb * 128:(kb + 1) * 128], Qn[qs][:], True, False,
                           ["ml_K%d" % hs, "ml_Qn%d" % qs], ["ml_pss%d" % i])
                        mm(ps_s[i][:], kpe[:, kb * 128:(kb + 1) * 128], Qp[qs][:], False, True,
                           ["ml_kpe", "ml_Qp%d" % qs], ["ml_pss%d" % i])
                        act(pt[ip][:], ps_s[i][:], AF.Exp, ["ml_pss%d" % i], ["ml_pt%d" % ip], scale=scale)
                        return ip
                    steps.append((kb, qk))
                    flat.append((it, h, qb, kb))
            ring = {}
            pend = []
            iprev = [0]
            depth = 2
            n = len(flat)
            for idx in range(n + depth):
                if idx < n:
                    it, h, qb, kb = flat[idx]
                    ring[idx] = steps[idx][1]()
                if idx >= depth:
                    j = idx - depth
                    it, h, qb, kb = flat[j]
                    i = ring.pop(j)
                    hs = h % 2
                    a = it % 2
                    mm(ps_o[a][:], Vs[hs][:, kb, :], pt[i][:], kb == 0, kb == NT - 1,
                       ["ml_V%d" % hs, "ml_pt%d" % i], ["ml_pso%d" % a])
                    if kb % 4 == 1:
                        tt("dve", tr1[:], pt[iprev[0]][:], pt[i][:], ALU.add, ["ml_pt%d" % iprev[0], "ml_pt%d" % i], ["ml_tr1"])
                    elif kb % 4 == 3:
                        tt("dve", tr2[:], pt[iprev[0]][:], pt[i][:], ALU.add, ["ml_pt%d" % iprev[0], "ml_pt%d" % i], ["ml_tr2"])
                        if kb == 3:
                            tt("dve", acc[a][:], tr1[:], tr2[:], ALU.add, ["ml_tr1", "ml_tr2"], ["ml_acc%d" % a])
                        else:
                            tt("dve", tr3[:], tr1[:], tr2[:], ALU.add, ["ml_tr1", "ml_tr2"], ["ml_tr3"])
                            tt("dve", acc[a][:], acc[a][:], tr3[:], ALU.add, ["ml_acc%d" % a, "ml_tr3"], ["ml_acc%d" % a])
                    iprev[0] = i
                    if kb == NT - 1:
                        def fin(a=a, h=h, qb=qb):
                            mm(ps_sum[:], ones_f[:], acc[a][:], True, True, ["ones_f", "ml_acc%d" % a], ["ml_psum"])
                            act(lnt[:], ps_sum[:], AF.Ln, ["ml_psum"], ["ml_ln"])
                            act(rec[:], lnt[:], AF.Exp, ["ml_ln"], ["ml_rec"], scale=-1.0)
                            tt("dve", yst[a][:], ps_o[a][:], rec[:], ALU.mult, ["ml_pso%d" % a, "ml_rec"], ["ml_y%d" % a])
                            S.dma("sp", ymla[h * 128:(h + 1) * 128, qb * 512:(qb + 1) * 512], yst[a][:],
                                  reads=["ml_y%d" % a], writes=[("scr", "ymla", h, qb)], stream="st")
                        pend.append((idx + 4, fin))
                while pend and pend[0][0] <= idx:
                    pend.pop(0)[1]()
            for _, f in pend:
                f()
        S.barrier()

    def stage_na(l):
        with ExitStack() as es:
            def sb(name, shape, dt):
                return es.enter_context(nc.sbuf_tensor(name + SUF[0], list(shape), dt))

            def pst(name, shape):
                return es.enter_context(nc.psum_tensor(name + SUF[0], list(shape), F32))
            RING = 8
            kT = [sb("na_k%d" % i, [128, 4, 128], BF16) for i in range(RING)]
            vr = [sb("na_v%d" % i, [128, 512], BF16) for i in range(RING)]
            qT = [sb("na_q%d" % i, [128, 4, 2, 128], BF16) for i in range(2)]
            bias_int = sb("na_bint", [128, 5, 8, 128], F32)
            bias_sp = sb("na_bsp", [128, 5, 8, 128], F32)
            NP = 3
            sbf = [sb("na_sb%d" % i, [128, 4, 128], F32) for i in range(NP)]
            pt = [sb("na_pt%d" % i, [128, 4, 128], BF16) for i in range(NP)]
            rec = [sb("na_rec%d" % i, [128, 4, 128], F32) for i in range(2)]
            lnt = sb("na_ln", [128, 512], F32)
            yst = [sb("na_y%d" % i, [128, 2, 128], BF16) for i in range(2)]
            ps_s = [pst("na_pss%d" % i, [128, 4, 128]) for i in range(NP)]
            ps_o = [pst("na_pso%d" % i, [128, 4, 128]) for i in range(2)]
            ps_sum = [pst("na_psum%d" % i, [128, 512]) for i in range(2)]

            counts = {}
            for c in cls_of_tile:
                counts[c] = counts.get(c, 0) + 1
            c_int = max(counts, key=lambda c: counts[c])
            S.dma("sp", bias_int[:].rearrange("p j h q -> p (j h q)"), nab[l, c_int], writes=["na_bint"], stream="ld")
            nak_v = nak.rearrange("h d t -> (h d) t").rearrange("(hp p) t -> p hp t", p=128)
            naq_v = naq.rearrange("(hp two) d t -> d two hp t", two=2)
            for i in range(2):
                S.op("pool", "memset", writes=["na_q%d" % i], ap=qT[i][:].rearrange("p a b q -> p (a b q)"), constant=0.0)
            yna_v = yna.rearrange("(pr p) t -> p pr t", p=128)
            loaded = [-1]

            def ensure(kt_hi):
                while loaded[0] < kt_hi:
                    kt = loaded[0] + 1
                    s = kt % RING
                    S.dma("sp", kT[s][:], nak_v[:, :, kt * 128:(kt + 1) * 128], writes=["na_k%d" % s], stream="ld")
                    S.dma("sp", vr[s][:], nav[kt * 128:(kt + 1) * 128, :], writes=["na_v%d" % s], stream="ld")
                    loaded[0] = kt

            def loadq(t):
                s = t % 2
                S.dma("sp", qT[s][0:64, :, 0, :], naq_v[:, 0, :, t * 128:(t + 1) * 128], writes=["na_q%d" % s], stream="ld")
                S.dma("sp", qT[s][64:128, :, 1, :], naq_v[:, 1, :, t * 128:(t + 1) * 128], writes=["na_q%d" % s], stream="ld")

            kt0s = [min(max(t - 2, 0), NT - 5) for t in range(NT)]
            ensure(kt0s[0] + 4)
            loadq(0)
            gcount = [0]
            grp = [0]
            steps = []
            for t in range(NT):
                kt0 = kt0s[t]
                cls = cls_of_tile[t]
                for g in range(2):
                    a = (t * 2 + g) % 2
                    for j in range(5):
                        def front(t=t, g=g, j=j, kt0=kt0, cls=cls, holder=None):
                            if g == 0 and j == 0:
                                if t + 1 < NT:
                                    ensure(kt0s[t + 1] + 4)
                                    loadq(t + 1)
                                if cls != c_int:
                                    S.dma("sp", bias_sp[:].rearrange("p j h q -> p (j h q)"), nab[l, cls],
                                          writes=["na_bsp"], stream="ld")
                            bias, bres = (bias_int, "na_bint") if cls == c_int else (bias_sp, "na_bsp")
                            kt = kt0 + j
                            s = kt % RING
                            qs = t % 2
                            i = gcount[0] % NP
                            gcount[0] += 1
                            for h2 in range(2):
                                hp = g * 2 + h2
                                mm(ps_s[i][:, h2 * 2:(h2 + 1) * 2, :], kT[s][:, hp, :], qT[qs][:, hp, :, :], True, True,
                                   ["na_k%d" % s, "na_q%d" % qs], ["na_pss%d" % i])
                            stt(sbf[i][:], ps_s[i][:], 0.125, bias[:, j, g * 4:(g + 1) * 4, :], ALU.mult, ALU.add,
                                ["na_pss%d" % i, bres], ["na_sb%d" % i])
                            act(pt[i][:], sbf[i][:], AF.Exp, ["na_sb%d" % i], ["na_pt%d" % i])
                            return i

                        def back(i, t=t, g=g, j=j, kt0=kt0, a=a):
                            kt = kt0 + j
                            s = kt % RING
                            for hh in range(4):
                                pr = g * 2 + hh // 2
                                mm(ps_o[a][:, hh, :], vr[s][:, pr * 128:(pr + 1) * 128], pt[i][:, hh, :], j == 0 and hh == 0, j == 4,
                                   ["na_v%d" % s, "na_pt%d" % i], ["na_pso%d" % a], skip=True)
                            mm(ps_sum[a][:], ones_bf[:], pt[i][:].rearrange("p h q -> p (h q)"), j == 0, j == 4,
                               ["ones_bf", "na_pt%d" % i], ["na_psum%d" % a])
                            if j == 4:
                                def fin_a():
                                    act(lnt[:], ps_sum[a][:], AF.Ln, ["na_psum%d" % a], ["na_ln"])
                                    act(rec[a][:].rearrange("p h q -> p (h q)"), lnt[:], AF.Exp, ["na_ln"], ["na_rec%d" % a], scale=-1.0)

                                def fin_b():
                                    po4 = ps_o[a][:].rearrange("p (pr two) q -> p pr two q", two=2)
                                    rc4 = rec[a][:].rearrange("p (pr two) q -> p pr two q", two=2)
                                    for par in range(2):
                                        lo, hi = par * 64, par * 64 + 64
                                        tt("dve", yst[a][lo:hi, :, :], po4[lo:hi, :, par, :], rc4[lo:hi, :, par, :], ALU.mult,
                                           ["na_pso%d" % a, "na_rec%d" % a], ["na_y%d" % a])
                                    S.dma("sp", yna_v[:, g * 2:(g + 1) * 2, t * 128:(t + 1) * 128], yst[a][:],
                                          reads=["na_y%d" % a], writes=[("scr", "yna", t, g)], stream="st")
                                return [(1, fin_a), (4, fin_b)]
                            return None
                        steps.append((front, back))
            run_pipeline(steps, 2, 0)
        S.barrier()

    def stage_gqa(l):
        with ExitStack() as es:
            def sb(name, shape, dt):
                return es.enter_context(nc.sbuf_tensor(name + SUF[0], list(shape), dt))

            def pst(name, shape):
                return es.enter_context(nc.psum_tensor(name + SUF[0], list(shape), F32))
            Kall = sb("gq_K", [128, T], BF16)
            Vall = sb("gq_V", [128, NT, 128], BF16)
            qT = [sb("gq_q%d" % i, [128, 2, 4, 128], BF16) for i in range(2)]
            msk = sb("gq_msk", [128, 2, 512], F32)
            sk = sb("gq_sink", [128, 8], F32)
            esk = sb("gq_esink", [128, 8], F32)
            NP = 3
            sbf = [sb("gq_sb%d" % i, [128, 512], F32) for i in range(NP)]
            pt = [sb("gq_pt%d" % i, [128, 512], BF16) for i in range(NP)]
            rec = [sb("gq_rec%d" % i, [128, 512], F32) for i in range(3)]
            lnt = sb("gq_ln", [128, 512], F32)
            yst = [sb("gq_y%d" % i, [128, 4, 128], BF16) for i in range(3)]
            ps_s = [pst("gq_pss%d" % i, [128, 512]) for i in range(NP)]
            ps_o = [pst("gq_pso%d" % i, [128, 512]) for i in range(3)]
            ps_sum = [pst("gq_psum%d" % i, [128, 512]) for i in range(2)]

            S.dma("sp", Kall[:], gk.rearrange("h d t -> (h d) t"), writes=["gq_K"], stream="ld")
            for i in range(2):
                S.op("pool", "memset", writes=["gq_q%d" % i], ap=qT[i][:].rearrange("p a b q -> p (a b q)"), constant=0.0)
            S.dma("sp", Vall[:], gv, writes=["gq_V"], stream="ld")
            S.dma("sp", msk[:], gmask.rearrange("m p n -> p m n"), writes=["gq_msk"], stream="ld")
            S.dma("sp", sk[0:64, :], sinkr[l], writes=["gq_sink"], stream="ld")
            S.dma("sp", sk[64:128, :], sinkr[l], writes=["gq_sink"], stream="ld")
            act(esk[:], sk[:], AF.Exp, ["gq_sink"], ["gq_esink"])
            gq_v = gq.rearrange("h d t -> d h t")
            ygqa_v = ygqa.rearrange("(h d) t -> d h t", d=64)

            def loadq(t):
                s = t % 2
                S.dma("sp", qT[s][0:64, 0, :, :], gq_v[:, 0:4, t * 128:(t + 1) * 128], writes=["gq_q%d" % s], stream="ld")
                S.dma("sp", qT[s][64:128, 1, :, :], gq_v[:, 4:8, t * 128:(t + 1) * 128], writes=["gq_q%d" % s], stream="ld")

            loadq(0)
            gcount = [0]
            steps = []
            for t in range(NT):
                for kvh in range(2):
                    a = (t * 2 + kvh) % 3
                    a2 = (t * 2 + kvh) % 2
                    kts = [kt for kt in (t - 1, t, t + 1) if 0 <= kt < NT]
                    for jj, kt in enumerate(kts):
                        first = jj == 0
                        last = jj == len(kts) - 1

                        def front(t=t, kvh=kvh, kt=kt, first=first):
                            if kvh == 0 and first and t + 1 < NT:
                                loadq(t + 1)
                            qs = t % 2
                            i = gcount[0] % NP
                            gcount[0] += 1
                            mm(ps_s[i][:], Kall[:, kt * 128:(kt + 1) * 128],
                               qT[qs][:, kvh, :, :], True, True,
                               ["gq_K", "gq_q%d" % qs], ["gq_pss%d" % i])
                            if kt != t:
                                m = 0 if kt < t else 1
                                tt("dve", sbf[i][:], ps_s[i][:], msk[:, m, :], ALU.add, ["gq_pss%d" % i, "gq_msk"], ["gq_sb%d" % i])
                                act(pt[i][:], sbf[i][:], AF.Exp, ["gq_sb%d" % i], ["gq_pt%d" % i], scale=0.125)
                            else:
                                act(pt[i][:], ps_s[i][:], AF.Exp, ["gq_pss%d" % i], ["gq_pt%d" % i], scale=0.125)
                            return i

                        def back(i, t=t, kvh=kvh, kt=kt, first=first, last=last, a=a, a2=a2):
                            mm(ps_o[a][:], Vall[:, kt, :], pt[i][:], first, last,
                               ["gq_V", "gq_pt%d" % i], ["gq_pso%d" % a])
                            mm(ps_sum[a2][:], ones_bf[:], pt[i][:], first, last,
                               ["ones_bf", "gq_pt%d" % i], ["gq_psum%d" % a2])
                            if last:
                                lo, hi = kvh * 64, kvh * 64 + 64

                                def fin_a():
                                    for g in range(4):
                                        h = kvh * 4 + g
                                        act(lnt[lo:hi, g * 128:(g + 1) * 128], ps_sum[a2][lo:hi, g * 128:(g + 1) * 128], AF.Ln,
                                            ["gq_psum%d" % a2, "gq_esink"], ["gq_ln"], bias=esk[lo:hi, h:h + 1], scale=1.0)
                                    act(rec[a][lo:hi, :], lnt[lo:hi, :], AF.Exp, ["gq_ln"], ["gq_rec%d" % a], scale=-1.0)

                                def fin_b():
                                    tt("dve", yst[a][lo:hi, :, :].rearrange("p h q -> p (h q)"), ps_o[a][lo:hi, :], rec[a][lo:hi, :], ALU.mult,
                                       ["gq_pso%d" % a, "gq_rec%d" % a], ["gq_y%d" % a])
                                    S.dma("sp", ygqa_v[:, kvh * 4:(kvh + 1) * 4, t * 128:(t + 1) * 128], yst[a][lo:hi, :, :],
                                          reads=["gq_y%d" % a], writes=[("scr", "ygqa", t, kvh)], stream="st")
                                return [(1, fin_a), (4, fin_b)]
                            return None
                        steps.append((front, back))
            run_pipeline(steps, 2, 0)
        S.barrier()

    def stage3(l, xsrc, xdst, W3):
        with ExitStack() as es:
            def sb(name, shape, dt):
                return es.enter_context(nc.sbuf_tensor(name + SUF[0], list(shape), dt))

            def pst(name, shape):
                return es.enter_context(nc.psum_tensor(name + SUF[0], list(shape), F32))
            Wg, Wb, Wo = W3
            g1s = sb("s3_g1", [128, 8], F32)
            bgs = sb("s3_bg", [128, 24], F32)
            xs = [sb("s3_x%d" % i, [128, 8, 512], F32) for i in range(2)]
            xn2 = [sb("s3_xn%d" % i, [128, 8, 512], BF16) for i in range(2)]
            tmp = sb("s3_tmp", [128, 512], F32)
            rstd = sb("s3_rstd", [128, 512], F32)
            ys = [[sb("s3_y%d_%d" % (i, s), [128, 4, 512], BF16) for s in range(2)] for i in range(3)]
            mg = sb("s3_mg", [128, 8, 512], BF16)
            gt = [sb("s3_gt%d" % i, [128, 512], F32) for i in range(3)]
            pr = [sb("s3_pr%d" % i, [128, 512], F32) for i in range(3)]
            m01 = sb("s3_m01", [128, 512], F32)
            xo = [sb("s3_xo%d" % i, [128, 512], F32) for i in range(3)]
            NPS = 6
            pss = [pst("s3_ps%d" % i, [128, 512]) for i in range(NPS)]
            ps_ss = pst("s3_pss", [128, 512])
            cnt = {"ps": 0, "xo": 0}

            def nps():
                i = cnt["ps"] % NPS
                cnt["ps"] += 1
                return pss[i], "s3_ps%d" % i

            S.dma("sp", g1s[:], g1[l], writes=["s3_g1"], stream="ld")
            S.dma("sp", bgs[:], bg[l], writes=["s3_bg"], stream="ld")
            xsrc_f = fm(xsrc)
            xdst_f = fm(xdst)
            ysrc = [fm(yna), fm(ymla), fm(ygqa)]

            def load(b):
                sl = b % 2
                t0 = b * 512
                S.dma("sp", xs[sl][:], xsrc_f[:, :, t0:t0 + 512], writes=["s3_x%d" % sl], stream="ld")
                for i in range(3):
                    S.dma("sp", ys[i][sl][:], ysrc[i][:, :, t0:t0 + 512], writes=["s3_y%d_%d" % (i, sl)], stream="ld")

            def xnorm3(b):
                sl_ = b % 2
                fm_norm(xs[sl_], "s3_x%d" % sl_, 8, 512, g1s, "s3_g1", xn2[sl_], "s3_xn%d" % sl_, ps_ss, "s3_pss", tmp, "s3_tmp",
                        rstd, "s3_rstd", xn2[sl_], "s3_xn%d" % sl_, 1.0 / 32.0)

            load(0)
            xnorm3(0)
            for b in range(NB):
                sl = b % 2
                t0 = b * 512
                if b + 1 < NB:
                    load(b + 1)
                xr = "s3_x%d" % sl
                xn = xn2[sl]
                xnres = "s3_xn%d" % sl
                for m in range(8):
                    if m == 4 and b + 1 < NB:
                        xnorm3(b + 1)
                    for i in range(3):
                        pg, pgr = nps()
                        for kc in range(8):
                            mm(pg[:], Wg[:, kc, i * 1024 + m * 128:i * 1024 + (m + 1) * 128], xn[:, kc, :],
                               kc == 0, kc == 7, [xnres], [pgr])
                        py, pyr = nps()
                        yr = "s3_y%d_%d" % (i, sl)
                        for kc in range(4):
                            mm(py[:], Wb[:, i, kc, m * 128:(m + 1) * 128], ys[i][sl][:, kc, :], kc == 0, kc == 3,
                               [yr], [pyr])
                        act(gt[i][:], pg[:], AF.Sigmoid, [pgr, "s3_bg"], ["s3_gt%d" % i],
                            bias=bgs[:, i * 8 + m:i * 8 + m + 1], scale=1.0)
                        tt("dve", pr[i][:], gt[i][:], py[:], ALU.mult, ["s3_gt%d" % i, pyr], ["s3_pr%d" % i])
                    tt("pool", m01[:], pr[0][:], pr[1][:], ALU.add, ["s3_pr0", "s3_pr1"], ["s3_m01"])
                    tt("pool", mg[:, m, :], m01[:], pr[2][:], ALU.add, ["s3_m01", "s3_pr2"], [("s3_mg", m)])
                for m in range(8):
                    po, por = nps()
                    for kc in range(8):
                        mm(po[:], Wo[:, kc, m * 128:(m + 1) * 128], mg[:, kc, :], kc == 0, kc == 7,
                           [("s3_mg", kc)], [por])
                    i = cnt["xo"] % 3
                    cnt["xo"] += 1
                    tt("dve", xo[i][:], po[:], xs[sl][:, m, :], ALU.add, [por, xr], ["s3_xo%d" % i])
                    S.dma("sp", xdst_f[:, m, t0:t0 + 512], xo[i][:], reads=["s3_xo%d" % i],
                          writes=[("scr", "x1", b, m)], stream="st")
        S.barrier()

    def stage_ffn(l, xsrc, xdst, final=False):
        NOUT = 382
        FW = 384
        wins = []
        o0 = 0
        while o0 < T:
            o1 = min(o0 + NOUT, T)
            u0 = max(o0 - 1, 0)
            u1 = min(o1 + 1, T)
            wins.append((o0, o1, u0, u1))
            o0 = o1
        with ExitStack() as es:
            def sb(name, shape, dt):
                return es.enter_context(nc.sbuf_tensor(name + SUF[0], list(shape), dt))

            def pst(name, shape):
                return es.enter_context(nc.psum_tensor(name + SUF[0], list(shape), F32))
            Wu = sb("ff_Wu", [128, 8, 2 * D_FF], BF16)
            Wd = sb("ff_Wd", [128, 22, 1024], BF16)
            g2s = sb("ff_g2", [128, 8], F32)
            cws = sb("ff_cw", [128, 44, 3], F32)
            cbs = sb("ff_cb", [128, 44], F32)
            xs = sb("ff_x", [128, 8, FW], F32)
            xn = sb("ff_xn", [128, 8, FW], BF16)
            tmp = sb("ff_tmp", [128, FW], F32)
            rstd = sb("ff_rstd", [128, FW], F32)
            hm = sb("ff_hm", [128, 22, FW], BF16)
            av = [sb("ff_a%d" % i, [128, FW], F32) for i in range(3)]
            gvv = [sb("ff_gv%d" % i, [128, FW], F32) for i in range(3)]
            gg = [sb("ff_gg%d" % i, [128, FW], F32) for i in range(3)]
            xres = [sb("ff_xr%d" % i, [128, FW], F32) for i in range(2)]
            xo = [sb("ff_xo%d" % i, [128, FW], F32) for i in range(2)]
            NPS = 6
            pss = [pst("ff_ps%d" % i, [128, 512]) for i in range(NPS)]
            ps_ss = pst("ff_pss", [128, 512])
            cnt = {"ps": 0, "c": 0, "xo": 0}

            def nps():
                i = cnt["ps"] % NPS
                cnt["ps"] += 1
                return pss[i], "ff_ps%d" % i

            wu_l = w_up[l].rearrange("(kc p) n -> p kc n", p=128)
            for c in range(22):
                for cc in (c, 22 + c):
                    S.dma("pool", Wu[:, :, cc * 128:(cc + 1) * 128], wu_l[:, :, cc * 128:(cc + 1) * 128],
                          writes=[("ff_Wu", cc)], stream="w")
            wd_l = w_down[l].rearrange("(kc p) n -> p kc n", p=128)
            for kc in range(22):
                S.dma("pool", Wd[:, kc, :], wd_l[:, kc, :], writes=[("ff_Wd", kc)], stream="w")
            S.dma("sp", g2s[:], g2[l], writes=["ff_g2"], stream="ld")
            S.dma("sp", cws[:].rearrange("p c j -> p (c j)"), cw[l], writes=["ff_cw"], stream="ld")
            S.dma("sp", cbs[:], cb[l], writes=["ff_cb"], stream="ld")
            xsrc_f = fm(xsrc)
            xdst_f = fm(xdst)
            if final:
                x2b = sb("ff_x2b", [128, 8, FW], F32)
                tmpf = sb("ff_tmpf", [128, FW], F32)
                rstdf = sb("ff_rstdf", [128, FW], F32)
                gfs = sb("ff_gf", [128, 8], F32)
                ps_fn = pst("ff_psfn", [128, 512])
                S.dma("sp", gfs[:], gf, writes=["ff_gf"], stream="ld")
                y_f = fm(yT)

            def load(w):
                o0, o1, u0, u1 = wins[w]
                S.dma("sp", xs[:, :, 0:u1 - u0], xsrc_f[:, :, u0:u1], writes=["ff_x"], stream="ld")

            load(0)
            for w, (o0, o1, u0, u1) in enumerate(wins):
                nu = u1 - u0
                no = o1 - o0
                uoff = o0 - u0
                if w == 0:
                    fm_norm(xs, "ff_x", 8, nu, g2s, "ff_g2", xn, "ff_xn", ps_ss, "ff_pss",
                            tmp, "ff_tmp", rstd, "ff_rstd", xn, "ff_xn", 1.0 / 32.0)
                if w + 1 < len(wins):
                    load(w + 1)
                defer = []
                for c in range(22):
                    k = cnt["c"] % 3
                    cnt["c"] += 1
                    outs = []
                    for (cc, dst, dres) in ((22 + c, gvv[k], "ff_gv%d" % k), (c, av[k], "ff_a%d" % k)):
                        ps, psr = nps()
                        for kc in range(8):
                            mm(ps[:, 0:nu], Wu[:, kc, cc * 128:(cc + 1) * 128], xn[:, kc, 0:nu], kc == 0, kc == 7,
                               [("ff_Wu", cc), "ff_xn"], [psr])
                        act(dst[:, 0:no], ps[:, uoff:uoff + no], AF.Identity, [psr, "ff_cw", "ff_cb"], [dres],
                            bias=cbs[:, cc:cc + 1], scale=cws[:, cc, 1:2])
                        lo = 0 if uoff >= 1 else 1
                        stt(dst[:, lo:no], ps[:, lo + uoff - 1:no + uoff - 1], cws[:, cc, 0:1], dst[:, lo:no],
                            ALU.mult, ALU.add, [psr, "ff_cw", dres], [dres])
                        hi = min(no, nu - uoff - 1)
                        stt(dst[:, 0:hi], ps[:, uoff + 1:uoff + 1 + hi], cws[:, cc, 2:3], dst[:, 0:hi],
                            ALU.mult, ALU.add, [psr, "ff_cw", dres], [dres])
                    def gl(c=c, k=k, no=no):
                        act(gg[k][:, 0:no], gvv[k][:, 0:no], AF.Gelu_apprx_tanh, ["ff_gv%d" % k], ["ff_gg%d" % k])
                        tt("dve" if c >= 20 else "pool", hm[:, c, 0:no], gg[k][:, 0:no], av[k][:, 0:no], ALU.mult,
                           ["ff_gg%d" % k, "ff_a%d" % k], [("ff_hm", c)])
                    while defer:
                        defer.pop(0)()
                    defer.append(gl)
                while defer:
                    defer.pop(0)()
                if w + 1 < len(wins):
                    n_o0, n_o1, n_u0, n_u1 = wins[w + 1]
                    fm_norm(xs, "ff_x", 8, n_u1 - n_u0, g2s, "ff_g2", xn, "ff_xn", ps_ss, "ff_pss",
                            tmp, "ff_tmp", rstd, "ff_rstd", xn, "ff_xn", 1.0 / 32.0)
                for m in range(8):
                    i = cnt["xo"] % 2
                    cnt["xo"] += 1
                    S.dma("sp", xres[i][:, 0:no], xsrc_f[:, m, o0:o1], writes=["ff_xr%d" % i], stream="ld2")
                    pd, pdr = nps()
                    for c in range(22):
                        mm(pd[:, 0:no], Wd[:, c, m * 128:(m + 1) * 128], hm[:, c, 0:no], c == 0, c == 21,
                           [("ff_Wd", c), ("ff_hm", c)], [pdr])
                    if not final:
                        tt("dve", xo[i][:, 0:no], pd[:, 0:no], xres[i][:, 0:no], ALU.add, [pdr, "ff_xr%d" % i], ["ff_xo%d" % i])
                        S.dma("sp", xdst_f[:, m, o0:o1], xo[i][:, 0:no], reads=["ff_xo%d" % i],
                              writes=[("scr", "x2", w, m)], stream="st")
                    else:
                        tt("dve", x2b[:, m, 0:no], pd[:, 0:no], xres[i][:, 0:no], ALU.add, [pdr, "ff_xr%d" % i], ["ff_x2b"])
                if final:
                    hres = [("ff_hm", c) for c in range(8)]
                    act(hm[:, 0:8, 0:no], x2b[:, :, 0:no], AF.Square, ["ff_x2b"], hres, scale=1.0 / 32.0)
                    for c in range(8):
                        mm(ps_fn[:, 0:no], ones_bf[:], hm[:, c, 0:no], c == 0, c == 7, ["ones_bf", ("ff_hm", c)], ["ff_psfn"])
                    act(tmpf[:, 0:no], ps_fn[:, 0:no], AF.Ln, ["ff_psfn"], ["ff_tmpf"], bias=EPS, scale=1.0)
                    act(rstdf[:, 0:no], tmpf[:, 0:no], AF.Exp, ["ff_tmpf"], ["ff_rstdf"], scale=-0.5)
                    for m in range(8):
                        i = cnt["xo"] % 2
                        cnt["xo"] += 1
                        stt(xo[i][:, 0:no], x2b[:, m, 0:no], gfs[:, m:m + 1], rstdf[:, 0:no], ALU.mult, ALU.mult,
                            ["ff_x2b", "ff_gf", "ff_rstdf"], ["ff_xo%d" % i])
                        S.dma("sp", y_f[:, m, o0:o1], xo[i][:, 0:no], reads=["ff_xo%d" % i],
                              writes=[("out", w, m)], stream="st")
        S.barrier()

    def stage_final(xsrc):
        with ExitStack() as es:
            def sb(name, shape, dt):
                return es.enter_context(nc.sbuf_tensor(name + SUF[0], list(shape), dt))
            gfs = sb("fn_g", [128, 8], F32)
            xs = [sb("fn_x%d" % i, [128, 8, 512], F32) for i in range(2)]
            sq = sb("fn_sq", [128, 8, 512], BF16)
            tmp = sb("fn_tmp", [128, 512], F32)
            rstd = sb("fn_rstd", [128, 512], F32)
            xo = [sb("fn_xo%d" % i, [128, 8, 512], F32) for i in range(2)]
            ps_ss = es.enter_context(nc.psum_tensor("fn_pss" + SUF[0], [128, 512], F32))
            S.dma("sp", gfs[:], gf, writes=["fn_g"], stream="ld")
            xsrc_f = fm(xsrc)
            y_f = fm(yT)
            S.dma("sp", xs[0][:], xsrc_f[:, :, 0:512], writes=["fn_x0"], stream="ld")
            for b in range(NB):
                sl = b % 2
                t0 = b * 512
                if b + 1 < NB:
                    S.dma("sp", xs[1 - sl][:], xsrc_f[:, :, t0 + 512:t0 + 1024], writes=["fn_x%d" % (1 - sl)], stream="ld")
                fm_norm(xs[sl], "fn_x%d" % sl, 8, 512, gfs, "fn_g", sq, "fn_sq", ps_ss, "fn_pss", tmp, "fn_tmp",
                        rstd, "fn_rstd", xo[sl], "fn_xo%d" % sl, 1.0 / 32.0)
                S.dma("sp", y_f[:, :, t0:t0 + 512], xo[sl][:], reads=["fn_xo%d" % sl], writes=[("out", b)], stream="st")
        S.barrier()

    xcur = xT
    for l in range(L):
        SUF[0] = "_L%d" % l
        if stages is None or "s1" in stages:
            stage1(l, xcur)
        with ExitStack() as es3:
            Wg = es3.enter_context(nc.sbuf_tensor("s3_Wg" + SUF[0], [128, 8, 3072], BF16))
            Wb = es3.enter_context(nc.sbuf_tensor("s3_Wb" + SUF[0], [128, 3, 4, 1024], BF16))
            Wo = es3.enter_context(nc.sbuf_tensor("s3_Wo" + SUF[0], [128, 8, 1024], BF16))
            def pre3(l=l, Wg=Wg, Wb=Wb, Wo=Wo):
                win_l = w_in[l].rearrange("(kc p) n -> p kc n", p=128)
                for kc in range(8):
                    S.dma("pool", Wg[:, kc, :], win_l[:, kc, 3008:6080], writes=["s3_Wg"], stream="w")
                for i in range(3):
                    wv = w_br[i][l].rearrange("(kc p) n -> p kc n", p=128)
                    for kc in range(4):
                        S.dma("pool", Wb[:, i, kc, :], wv[:, kc, :], writes=["s3_Wb"], stream="w")
                wo_l = w_out[l].rearrange("(kc p) n -> p kc n", p=128)
                for kc in range(8):
                    S.dma("pool", Wo[:, kc, :], wo_l[:, kc, :], writes=["s3_Wo"], stream="w")
            if stages is None or "mla" in stages:
                stage_mla(l, pre3 if (stages is None or "s3" in stages) else None)
            elif stages is not None and "s3" in stages:
                pre3()
            if stages is None or "na" in stages:
                stage_na(l)
            if stages is None or "gqa" in stages:
                stage_gqa(l)
            if stages is None or "s3" in stages:
                stage3(l, xcur, XA, (Wg, Wb, Wo))
        if stages is None or "ffn" in stages:
            stage_ffn(l, XA, XB, final=(l == L - 1))
        xcur = XB
    S.barrier()
    S.emit()
    return nc


def prep_shared(inputs, T, L):
    f = lambda a: np.ascontiguousarray(np.asarray(a, dtype=np.float32))
    cls_of_tile, classes = na_classes(T)
    C, Sn = rope_tables_fm(T)
    sh = {
        "w_in": f(inputs["w_in"]), "mla_w_uq": f(inputs["mla_w_uq"]), "mla_w_ukv": f(inputs["mla_w_ukv"]),
        "w_br_na": f(inputs["w_br_na"]), "w_br_mla": f(inputs["w_br_mla"]), "w_br_gqa": f(inputs["w_br_gqa"]),
        "w_out": f(inputs["w_out"]), "w_up": f(inputs["w_up"]), "w_down": f(inputs["w_down"]),
        "g1": np.stack([pcol(f(inputs["norm1_g"])[l], 8) for l in range(L)]),
        "g2": np.stack([pcol(f(inputs["norm2_g"])[l], 8) for l in range(L)]),
        "gf": pcol(f(inputs["final_g"]), 8),
        "bg": np.stack([pcol(f(inputs["b_gate"])[l], 24) for l in range(L)]),
        "gqa": np.stack([pcol(f(inputs["mla_qa_g"])[l], 3) for l in range(L)]),
        "gkva": np.stack([pcol(f(inputs["mla_kva_g"])[l], 2) for l in range(L)]),
        "sinkr": np.ascontiguousarray(np.broadcast_to(f(inputs["gqa_sink"])[:, None, :], (L, 64, 8))),
        "cw": np.stack([np.ascontiguousarray(
            f(inputs["conv_w"])[l].reshape(3, 44, 128).transpose(2, 1, 0).reshape(128, 132)) for l in range(L)]),
        "cb": np.stack([pcol(f(inputs["conv_b"])[l], 44) for l in range(L)]),
        "ropeC": C, "ropeS": Sn,
        "nab": na_bias_tables(f(inputs["na_rpb"]), classes).reshape(L, len(classes), 128, 5 * 8 * 128),
        "gmask": gqa_masks(),
    }
    return sh, cls_of_tile, len(classes)


_CACHE = {}


def kernel(**inputs):
    x = np.asarray(inputs["x"], dtype=np.float32)
    B, T, _ = x.shape
    L = np.asarray(inputs["w_in"]).shape[0]
    sh, cls_of_tile, ncls = prep_shared(inputs, T, L)
    key = (T, L, ncls)
    if key not in _CACHE:
        _CACHE[key] = build(T, L, ncls, cls_of_tile)
    nc = _CACHE[key]
    in_maps = []
    for b in range(B):
        m = dict(sh)
        m["xT"] = np.ascontiguousarray(x[b].T)
        in_maps.append(m)
    res = run_bass_kernel_spmd(nc, in_maps, core_ids=list(range(B)))
    out = np.empty((B, T, D), np.float32)
    for b in range(B):
        out[b] = np.asarray(res.results[b]["yT"]).T
    return out
```

```python
import numpy as np
import ml_dtypes
from contextlib import ExitStack
import concourse.bass as bass
import concourse.mybir as mybir
from concourse.bass_utils import run_bass_kernel_spmd

F32 = mybir.dt.float32
BF16 = mybir.dt.bfloat16
AF = mybir.ActivationFunctionType
ALU = mybir.AluOpType

D = 1024
L_FULL = 2
T_FULL = 8192
GRID_W = 64
EPS = 1e-6
D_FF = 2816
IN_W = 6080
NEG = -30000.0
COMPUTE = ("pe", "act", "dve", "pool")


class Sched:
    def __init__(self, nc, n_dma_sems=8):
        self.nc = nc
        self.ins = {e: [] for e in ("pe", "act", "dve", "pool", "sp")}
        self.known = {e: {} for e in self.ins}
        self.res = {}
        self.milestones = {e: set() for e in COMPUTE}
        self.streams = {}
        self.n_dma_sems = n_dma_sems
        self.dma_sem_keys = []
        self.dma_latest = {}
        self.last_compute = {e: 0 for e in COMPUTE}

    def _r(self, key):
        r = self.res.get(key)
        if r is None:
            r = ({}, {})
            self.res[key] = r
        return r

    def _deps(self, eng, reads, writes, is_dma):
        deps = {}
        me = ("E", eng)

        def add(k, v, same_ok):
            if (not is_dma) and same_ok and k == me:
                return
            if deps.get(k, 0) < v:
                deps[k] = v

        for r in reads:
            W, R = self._r(r)
            for k, v in W.items():
                add(k, v, False)
        for w in writes:
            W, R = self._r(w)
            for k, v in W.items():
                add(k, v, True)
            for k, v in R.items():
                add(k, v, True)
        out = []
        kn = self.known[eng]
        for k, v in deps.items():
            if kn.get(k, 0) >= v:
                continue
            kn[k] = v
            out.append((k, v))
            if k[0] == "E":
                self.milestones[k[1]].add(v)
        return out

    def _update(self, tok, reads, writes):
        k, v = tok
        for r in reads:
            W, R = self._r(r)
            if R.get(k, 0) < v:
                R[k] = v
        for w in writes:
            W, R = self._r(w)
            W.clear()
            R.clear()
            W[k] = v

    def op(self, eng, name, reads=(), writes=(), **kw):
        waits = self._deps(eng, reads, writes, False)
        self.ins[eng].append((name, kw, waits, None))
        tok = (("E", eng), len(self.ins[eng]))
        self.last_compute[eng] = len(self.ins[eng])
        self._update(tok, reads, writes)

    def dma(self, q, out, in_, reads=(), writes=(), stream="ld"):
        st = self.streams.get(stream)
        if st is None:
            base = len(self.dma_sem_keys)
            keys = [("D", base + i) for i in range(self.n_dma_sems)]
            self.dma_sem_keys += keys
            st = {"keys": keys, "n": 0}
            self.streams[stream] = st
        i = st["n"]
        st["n"] += 1
        K = len(st["keys"])
        key = st["keys"][i % K]
        val = 16 * (i // K + 1)
        waits = self._deps(q, reads, writes, True)
        if val > 16 and self.known[q].get(key, 0) < val - 16:
            self.known[q][key] = val - 16
            waits.append((key, val - 16))
        self.ins[q].append(("dma_start", dict(out=out, in_=in_), waits, (key, 16)))
        self.dma_latest[key] = val
        self._update((key, val), reads, writes)

    def barrier(self):
        toks = []
        for e in COMPUTE:
            if self.last_compute[e]:
                toks.append((("E", e), self.last_compute[e]))
        for k, v in self.dma_latest.items():
            toks.append((k, v))
        for e in self.ins:
            waits = []
            for k, v in toks:
                if k == ("E", e):
                    continue
                if self.known[e].get(k, 0) >= v:
                    continue
                self.known[e][k] = v
                waits.append((k, v))
                if k[0] == "E":
                    self.milestones[k[1]].add(v)
            if waits:
                self.ins[e].append((None, None, waits, None))
        self.res = {}

    def emit(self):
        nc = self.nc
        sems = {}
        for e in COMPUTE:
            sems[("E", e)] = nc.alloc_semaphore("sem_" + e)
        for k in self.dma_sem_keys:
            sems[k] = nc.alloc_semaphore("semd%d" % k[1])
        rank = {}
        for e in COMPUTE:
            ms = sorted(self.milestones[e])
            rank[e] = {v: i + 1 for i, v in enumerate(ms)}

        def run(engine, lst, ename):
            my = rank.get(ename, {})
            mysem = sems.get(("E", ename))
            for idx, (name, kw, waits, dinc) in enumerate(lst):
                for k, v in waits:
                    if k[0] == "E":
                        engine.wait_ge(sems[k], rank[k[1]][v])
                    else:
                        engine.wait_ge(sems[k], v)
                if name is None:
                    continue
                r = getattr(engine, name)(**kw)
                if dinc is not None:
                    r.then_inc(sems[dinc[0]], dinc[1])
                elif (idx + 1) in my:
                    r.then_inc(mysem, 1)

        with nc.Block() as block:
            @block.tensor
            def _(e):
                run(e, self.ins["pe"], "pe")

            @block.scalar
            def _(e):
                run(e, self.ins["act"], "act")

            @block.vector
            def _(e):
                run(e, self.ins["dve"], "dve")

            @block.gpsimd
            def _(e):
                run(e, self.ins["pool"], "pool")

            @block.sync
            def _(e):
                run(e, self.ins["sp"], "sp")


def rope_tables_fm(T):
    inv = 1.0 / (10000.0 ** (np.arange(0, 64, 2, dtype=np.float32) / 64.0))
    ang = np.arange(T, dtype=np.float32)[:, None] * inv[None, :]
    c = np.cos(ang).astype(np.float32).T
    s = np.sin(ang).astype(np.float32).T
    C = np.concatenate([c, c, c, c], axis=0)
    S = np.concatenate([-s, s, -s, s], axis=0)
    return np.ascontiguousarray(C), np.ascontiguousarray(S)


def na_classes(T):
    rows = T // GRID_W
    NT = T // 128
    kk = np.arange(128)
    qq = np.arange(128)
    cls_of_tile = []
    classes = []
    keys = {}
    for t in range(NT):
        kt0 = min(max(t - 2, 0), NT - 5)
        r = 2 * t + qq // 64
        cq = qq % 64
        r_start = np.clip(r - 4, 0, rows - 8)
        c_start = np.clip(cq - 8, 0, GRID_W - 16)
        valid = np.zeros((128, 5, 128), bool)
        ri = np.zeros((128, 5, 128), np.int64)
        ci = np.zeros((128, 5, 128), np.int64)
        for j in range(5):
            kt = kt0 + j
            rk = 2 * kt + kk // 64
            ck = kk % 64
            v = ((rk[:, None] >= r_start[None, :]) & (rk[:, None] < r_start[None, :] + 8)
                 & (ck[:, None] >= c_start[None, :]) & (ck[:, None] < c_start[None, :] + 16))
            valid[:, j, :] = v
            ri[:, j, :] = np.clip(rk[:, None] - r[None, :] + 7, 0, 14)
            ci[:, j, :] = np.clip(ck[:, None] - cq[None, :] + 15, 0, 30)
        key = (valid.tobytes(), ri.tobytes(), ci.tobytes())
        if key not in keys:
            keys[key] = len(classes)
            classes.append((valid, ri, ci))
        cls_of_tile.append(keys[key])
    return cls_of_tile, classes


def na_bias_tables(rpb, classes):
    Lx = rpb.shape[0]
    out = np.empty((Lx, len(classes), 128, 5, 8, 128), np.float32)
    for c, (valid, ri, ci) in enumerate(classes):
        g = rpb[:, :, ri, ci]
        g = np.where(valid[None, None], g, np.float32(NEG))
        out[:, c] = g.transpose(0, 2, 3, 1, 4)
    return out


def gqa_masks():
    kk = np.arange(128)[:, None]
    qq = np.arange(128)[None, :]
    m_prev = np.where(qq <= kk, 0.0, -1.0e6).astype(np.float32)
    m_next = np.where(kk <= qq, 0.0, -1.0e6).astype(np.float32)
    m = np.stack([np.tile(m_prev, (1, 4)), np.tile(m_next, (1, 4))], axis=0)
    return np.ascontiguousarray(m)


def pcol(v, nchunk):
    return np.ascontiguousarray(v.reshape(nchunk, 128).T)


def build(T, L, ncls, cls_of_tile, debug=False, stages=None):
    NB = T // 512
    NT = T // 128
    nc = bass.Bass("TRN2", target_bir_lowering=False)
    S = Sched(nc)

    def din(name, shape, dt=F32):
        return nc.dram_tensor(name, list(shape), dt, kind="ExternalInput").ap()

    def dscr(name, shape, dt):
        return nc.dram_tensor(name, list(shape), dt, kind=("ExternalOutput" if debug else "Internal")).ap()

    xT = din("xT", [D, T])
    w_in = din("w_in", [L, D, IN_W])
    w_uq = din("mla_w_uq", [L, 384, 768])
    w_ukv = din("mla_w_ukv", [L, 256, 1024])
    w_br = [din("w_br_na", [L, 512, D]), din("w_br_mla", [L, 512, D]), din("w_br_gqa", [L, 512, D])]
    w_out = din("w_out", [L, D, D])
    w_up = din("w_up", [L, D, 2 * D_FF])
    w_down = din("w_down", [L, D_FF, D])
    g1 = din("g1", [L, 128, 8])
    g2 = din("g2", [L, 128, 8])
    gf = din("gf", [128, 8])
    bg = din("bg", [L, 128, 24])
    gqa_ = din("gqa", [L, 128, 3])
    gkva = din("gkva", [L, 128, 2])
    sinkr = din("sinkr", [L, 64, 8])
    cw = din("cw", [L, 128, 44 * 3])
    cb = din("cb", [L, 128, 44])
    ropeC = din("ropeC", [128, T])
    ropeS = din("ropeS", [128, T])
    nab = din("nab", [L, ncls, 128, 5 * 8 * 128])
    gmask = din("gmask", [2, 128, 512])

    yT = nc.dram_tensor("yT", [D, T], F32, kind="ExternalOutput").ap()

    XA = dscr("XA", [D, T], F32)
    XB = dscr("XB", [D, T], F32)
    naq = dscr("naq", [8, 64, T], BF16)
    nak = dscr("nak", [8, 64, T], BF16)
    nav = dscr("nav", [T, 512], BF16)
    mqn = dscr("mqn", [4, 128, T], BF16)
    mqp = dscr("mqp", [4, 64, T], BF16)
    mkn = dscr("mkn", [4, 128, T], BF16)
    mkp = dscr("mkp", [64, T], BF16)
    mv = dscr("mv", [4, 128, NT, 128], BF16)
    gq = dscr("gq", [8, 64, T], BF16)
    gk = dscr("gk", [2, 64, T], BF16)
    gv = dscr("gv", [128, NT, 128], BF16)
    yna = dscr("yna", [512, T], BF16)
    ymla = dscr("ymla", [512, T], BF16)
    ygqa = dscr("ygqa", [512, T], BF16)

    def fm(ap):
        return ap.rearrange("(kc p) t -> p kc t", p=128)

    def mm(out, lhsT, rhs, start, stop, reads, writes, skip=False):
        kw = dict(out=out, lhsT=lhsT, rhs=rhs, start=start, stop=stop)
        if skip:
            kw["skip_group_check"] = True
        S.op("pe", "matmul", reads=reads, writes=writes, **kw)

    def act(out, in_, func, reads, writes, **kw):
        S.op("act", "activation", reads=reads, writes=writes, out=out, in_=in_, func=func, **kw)

    def tt(eng, out, in0, in1, op, reads, writes):
        S.op(eng, "tensor_tensor", reads=reads, writes=writes, out=out, in0=in0, in1=in1, op=op)

    def ts(eng, out, in0, s1, s2, op0, op1, reads, writes):
        kw = dict(out=out, in0=in0, scalar1=s1, scalar2=s2, op0=op0)
        if op1 is not None:
            kw["op1"] = op1
        S.op(eng, "tensor_scalar", reads=reads, writes=writes, **kw)

    def stt(out, in0, scalar, in1, op0, op1, reads, writes):
        S.op("dve", "scalar_tensor_tensor", reads=reads, writes=writes, out=out, in0=in0, scalar=scalar,
             in1=in1, op0=op0, op1=op1)

    def cp(eng, out, in_, reads, writes):
        if eng == "act":
            act(out, in_, AF.Copy, reads, writes)
        else:
            S.op(eng, "tensor_copy", reads=reads, writes=writes, out=out, in_=in_)

    def recip(out, in_, reads, writes):
        S.op("dve", "reciprocal", reads=reads, writes=writes, out=out, in_=in_)

    evac_rr = [0]
    SUF = [""]

    def evac(out, in_, reads, writes):
        evac_rr[0] ^= 1
        cp("act" if evac_rr[0] else "dve", out, in_, reads, writes)

    ones_bf = nc.alloc_sbuf_tensor("ones_bf", [128, 128], BF16)
    ones_f = nc.alloc_sbuf_tensor("ones_f", [128, 128], F32)
    S.op("pool", "memset", writes=["ones_bf"], ap=ones_bf[:], constant=1.0)
    S.op("pool", "memset", writes=["ones_f"], ap=ones_f[:], constant=1.0)

    def fm_norm(xs, xres, nch, n, gcol, gres, sq, sqres, ps, psres, tmp, tmpres, rstd, rstdres, xn, xnres, inv_sqrt_dim):
        act(sq[:, 0:nch, 0:n], xs[:, 0:nch, 0:n], AF.Square, [xres], [sqres], scale=inv_sqrt_dim)
        for c in range(nch):
            mm(ps[:, 0:n], ones_bf[:], sq[:, c, 0:n], c == 0, c == nch - 1, ["ones_bf", sqres], [psres])
        act(tmp[:, 0:n], ps[:, 0:n], AF.Ln, [psres], [tmpres], bias=EPS, scale=1.0)
        act(rstd[:, 0:n], tmp[:, 0:n], AF.Exp, [tmpres], [rstdres], scale=-0.5)
        for c in range(nch):
            stt(xn[:, c, 0:n], xs[:, c, 0:n], gcol[:, c:c + 1], rstd[:, 0:n], ALU.mult, ALU.mult,
                [xres, gres, rstdres], [xnres])

    def run_pipeline(steps, depth, defer):
        n = len(steps)
        ring = {}
        pend = []
        for idx in range(n + depth):
            if idx < n:
                ring[idx] = steps[idx][0]()
            if idx >= depth:
                fl = steps[idx - depth][1](ring.pop(idx - depth))
                if fl is not None:
                    for d, f in fl:
                        pend.append((idx + d, f))
                    pend.sort(key=lambda x: x[0])
            while pend and pend[0][0] <= idx:
                pend.pop(0)[1]()
        for _, f in pend:
            f()

    def stage1(l, xsrc):
        with ExitStack() as es:
            def sb(name, shape, dt):
                return es.enter_context(nc.sbuf_tensor(name + SUF[0], list(shape), dt))

            def pst(name, shape):
                return es.enter_context(nc.psum_tensor(name + SUF[0], list(shape), F32))
            Wa = sb("s1_Wa", [128, 8, 3008], BF16)
            Wsw = sb("s1_Wsw", [128, 8, 704], BF16)
            Wq = sb("s1_Wq", [128, 3, 768], BF16)
            Wqp = sb("s1_Wqp", [128, 3, 256], BF16)
            Wqps = sb("s1_Wqps", [128, 3, 256], BF16)
            Wkv = sb("s1_Wkv", [128, 2, 1024], BF16)
            Wv = sb("s1_Wv", [128, 2, 512], BF16)
            g1s = sb("s1_g1", [128, 8], F32)
            gqs = sb("s1_gq", [128, 3], F32)
            gks = sb("s1_gk", [128, 2], F32)
            xs = [sb("s1_x%d" % i, [128, 8, 512], F32) for i in range(2)]
            xn = [sb("s1_xn%d" % i, [128, 8, 512], BF16) for i in range(2)]
            sq = sb("s1_sq", [128, 8, 512], BF16)
            tmp = sb("s1_tmp", [128, 512], F32)
            rstd = sb("s1_rstd", [128, 512], F32)
            Cs = [sb("s1_C%d" % i, [128, 512], F32) for i in range(2)]
            Ss = [sb("s1_S%d" % i, [128, 512], F32) for i in range(2)]
            cq = sb("s1_cq", [128, 3, 512], F32)
            ckv = sb("s1_ckv", [128, 2, 512], F32)
            cqn = sb("s1_cqn", [128, 3, 512], BF16)
            ckvn = sb("s1_ckvn", [128, 2, 512], BF16)
            NST = 6
            stg = [sb("s1_stg%d" % i, [128, 512], BF16) for i in range(NST)]
            t1 = [sb("s1_t1_%d" % i, [128, 512], F32) for i in range(2)]
            t2 = [sb("s1_t2_%d" % i, [128, 512], F32) for i in range(2)]
            NPS = 5
            pss = [pst("s1_ps%d" % i, [128, 512]) for i in range(NPS)]
            ps_ss = pst("s1_pss", [128, 512])
            cnt = {"ps": 0, "stg": 0, "t": 0}

            def nps():
                i = cnt["ps"] % NPS
                cnt["ps"] += 1
                return pss[i], "s1_ps%d" % i

            def nstg():
                i = cnt["stg"] % NST
                cnt["stg"] += 1
                return stg[i], "s1_stg%d" % i

            win_l = w_in[l].rearrange("(kc p) n -> p kc n", p=128)
            WA_GROUPS = [(1536, 2176), (0, 512), (512, 1024), (2176, 2880), (1024, 1536), (2880, 3008)]

            def wa_res(c0):
                for gi, (a0, a1) in enumerate(WA_GROUPS):
                    if a0 <= c0 < a1:
                        return ("s1_Wa", gi)
                raise ValueError(c0)

            def load_wa(gi):
                a0, a1 = WA_GROUPS[gi]
                S.dma("pool", Wa[:, :, a0:a1], win_l[:, :, a0:a1], writes=[("s1_Wa", gi)], stream="w")
            load_wa(0)
            load_wa(1)
            load_wa(2)
            load_wa(3)
            load_wa(4)
            load_wa(5)
            wuq_l = w_uq[l].rearrange("(kc p) n -> p kc n", p=128)
            S.dma("pool", Wq[:], wuq_l, writes=["s1_Wq"], stream="w")
            wukv_l = w_ukv[l].rearrange("(kc p) n -> p kc n", p=128)
            S.dma("pool", Wkv[:], wukv_l, writes=["s1_Wkv"], stream="w")
            for (src0, nh, dst0) in ((2176, 1, 0), (2240, 8, 64), (2752, 2, 576)):
                srcv = Wa[:, :, src0:src0 + nh * 64].rearrange("p k (h two r) -> p k h two r", two=2, r=32)
                dstv = Wsw[:, :, dst0:dst0 + nh * 64].rearrange("p k (h two r) -> p k h two r", two=2, r=32)
                S.op("pool", "tensor_copy", reads=[("s1_Wa", 3)], writes=["s1_Wsw"], out=dstv[:, :, :, 0, :], in_=srcv[:, :, :, 1, :])
                S.op("pool", "tensor_copy", reads=[("s1_Wa", 3)], writes=["s1_Wsw"], out=dstv[:, :, :, 1, :], in_=srcv[:, :, :, 0, :])
            wq4 = Wq[:].rearrange("p k (h c) -> p k h c", c=192)
            S.op("pool", "tensor_copy", reads=["s1_Wq"], writes=["s1_Wqp"],
                 out=Wqp[:].rearrange("p k (h c) -> p k h c", c=64), in_=wq4[:, :, :, 128:192])
            wqps4 = Wqps[:].rearrange("p k (h c) -> p k h c", c=64)
            S.op("pool", "tensor_copy", reads=["s1_Wq"], writes=["s1_Wqps"], out=wqps4[:, :, :, 0:32], in_=wq4[:, :, :, 160:192])
            S.op("pool", "tensor_copy", reads=["s1_Wq"], writes=["s1_Wqps"], out=wqps4[:, :, :, 32:64], in_=wq4[:, :, :, 128:160])
            S.op("pool", "tensor_copy", reads=["s1_Wkv"], writes=["s1_Wv"],
                 out=Wv[:].rearrange("p k (h c) -> p k h c", c=128),
                 in_=Wkv[:].rearrange("p k (h c) -> p k h c", c=256)[:, :, :, 128:256])
            S.dma("sp", g1s[:], g1[l], writes=["s1_g1"], stream="ld")
            S.dma("sp", gqs[:], gqa_[l], writes=["s1_gq"], stream="ld")
            S.dma("sp", gks[:], gkva[l], writes=["s1_gk"], stream="ld")

            xsrc_f = fm(xsrc)
            naq_f = naq.rearrange("h d t -> (h d) t")
            nak_f = nak.rearrange("h d t -> (h d) t")
            mqp_f = mqp.rearrange("h d t -> (h d) t")
            gq_f = gq.rearrange("h d t -> (h d) t")
            gk_f = gk.rearrange("h d t -> (h d) t")

            def load(b):
                sl = b % 2
                t0 = b * 512
                S.dma("sp", xs[sl][:], xsrc_f[:, :, t0:t0 + 512], reads=[("X", id(xsrc), b)], writes=["s1_x%d" % sl], stream="ld")
                S.dma("sp", Cs[sl][:], ropeC[:, t0:t0 + 512], writes=["s1_C%d" % sl], stream="ld")
                S.dma("sp", Ss[sl][:], ropeS[:, t0:t0 + 512], writes=["s1_S%d" % sl], stream="ld")

            def store(dst, src, srcres, dstres):
                S.dma("sp", dst, src, reads=[srcres], writes=[dstres], stream="st")

            def normA(xin, nch, sqb, sqres, xres, scl):
                act(sqb[:, 0:nch, :], xin[:, 0:nch, :], AF.Square, [xres], [sqres], scale=scl)

            def normB(sqb, nch, ps, psres, sqres):
                for c in range(nch):
                    mm(ps[:], ones_bf[:], sqb[:, c, :], c == 0, c == nch - 1, ["ones_bf", sqres], [psres])

            def normC(ps, psres, tm, tmres, rs, rsres, xin, xres, nch, gcol, gres, xo, xores):
                act(tm[:], ps[:], AF.Ln, [psres], [tmres], bias=EPS, scale=1.0)
                act(rs[:], tm[:], AF.Exp, [tmres], [rsres], scale=-0.5)
                for c in range(nch):
                    stt(xo[:, c, :], xin[:, c, :], gcol[:, c:c + 1], rs[:], ALU.mult, ALU.mult,
                        [xres, gres, rsres], [xores])

            sqc = sb("s1_sqc", [128, 5, 512], BF16)
            tmp_c = sb("s1_tmpc", [128, 512], F32)
            rstd_c = sb("s1_rstdc", [128, 512], F32)
            tmp_k = sb("s1_tmpk", [128, 512], F32)
            rstd_k = sb("s1_rstdk", [128, 512], F32)
            ps_ssc = pst("s1_pssc", [128, 512])
            ps_ssk = pst("s1_pssk", [128, 512])

            def xnorm(b):
                sl_ = b % 2
                xr_, xnr_ = "s1_x%d" % sl_, "s1_xn%d" % sl_
                normA(xs[sl_], 8, sq, "s1_sq", xr_, 1.0 / 32.0)
                normB(sq, 8, ps_ss, "s1_pss", "s1_sq")
                normC(ps_ss, "s1_pss", tmp, "s1_tmp", rstd, "s1_rstd", xs[sl_], xr_, 8, g1s, "s1_g1", xn[sl_], xnr_)

            load(0)
            xnorm(0)
            for b in range(NB):
                sl = b % 2
                t0 = b * 512
                if b + 1 < NB:
                    load(b + 1)
                xr, xnr = "s1_x%d" % sl, "s1_xn%d" % sl
                X = xn[sl]

                def proj_fm(W, Wres, c0, M):
                    ps, psr = nps()
                    if Wres == "s1_Wa":
                        Wres = wa_res(c0)
                    for kc in range(8):
                        mm(ps[0:M, :], W[:, kc, c0:c0 + M], X[:, kc, :], kc == 0, kc == 7, [Wres, xnr], [psr])
                    return ps, psr

                def plain_fm(c0, M, dst):
                    ps, psr = proj_fm(Wa, "s1_Wa", c0, M)
                    st_, sr = nstg()
                    evac(st_[0:M, :], ps[0:M, :], [psr], [sr])
                    store(dst, st_[0:M, :], sr, ("scr", id(dst), b))

                def rope_fm(c0, csw, M, dst, dres):
                    pa, par = proj_fm(Wa, "s1_Wa", c0, M)
                    pb, pbr = proj_fm(Wsw, "s1_Wsw", csw, M)
                    i = cnt["t"] % 2
                    cnt["t"] += 1
                    tt("dve", t1[i][0:M, :], pa[0:M, :], Cs[sl][0:M, :], ALU.mult, [par, "s1_C%d" % sl], ["s1_t1_%d" % i])
                    tt("dve", t2[i][0:M, :], pb[0:M, :], Ss[sl][0:M, :], ALU.mult, [pbr, "s1_S%d" % sl], ["s1_t2_%d" % i])
                    st_, sr = nstg()
                    tt("pool", st_[0:M, :], t1[i][0:M, :], t2[i][0:M, :], ALU.add, ["s1_t1_%d" % i, "s1_t2_%d" % i], [sr])
                    store(dst, st_[0:M, :], sr, dres)

                for c in range(3):
                    ps, psr = proj_fm(Wa, "s1_Wa", 1536 + c * 128, 128)
                    cp("dve", cq[:, c, :], ps[:], [psr], ["s1_cq"])
                for c in range(2):
                    ps, psr = proj_fm(Wa, "s1_Wa", 1920 + c * 128, 128)
                    cp("dve", ckv[:, c, :], ps[:], [psr], ["s1_ckv"])
                normA(cq, 3, sqc[:, 0:3, :], "s1_sqc", "s1_cq", 1.0 / np.sqrt(384.0))
                normA(ckv, 2, sqc[:, 3:5, :], "s1_sqk", "s1_ckv", 1.0 / 16.0)
                for c in range(4):
                    plain_fm(c * 128, 128, naq_f[c * 128:(c + 1) * 128, t0:t0 + 512])
                normB(sqc[:, 0:3, :], 3, ps_ssc, "s1_pssc", "s1_sqc")
                normB(sqc[:, 3:5, :], 2, ps_ssk, "s1_pssk", "s1_sqk")
                normC(ps_ssc, "s1_pssc", tmp_c, "s1_tmpc", rstd_c, "s1_rstdc", cq, "s1_cq", 3, gqs, "s1_gq", cqn, "s1_cqn")
                normC(ps_ssk, "s1_pssk", tmp_k, "s1_tmpk", rstd_k, "s1_rstdk", ckv, "s1_ckv", 2, gks, "s1_gk", ckvn, "s1_ckvn")
                for c in range(4):
                    plain_fm(512 + c * 128, 128, nak_f[c * 128:(c + 1) * 128, t0:t0 + 512])
                rope_fm(2176, 0, 64, mkp[:, t0:t0 + 512], ("scr", "mkp", b))
                for c in range(4):
                    rope_fm(2240 + c * 128, 64 + c * 128, 128, gq_f[c * 128:(c + 1) * 128, t0:t0 + 512], ("scr", "gq", b, c))
                rope_fm(2752, 576, 128, gk_f[:, t0:t0 + 512], ("scr", "gk", b))
                for i in range(4):
                    ps, psr = nps()
                    for kc in range(8):
                        mm(ps[:], X[:, kc, i * 128:(i + 1) * 128], Wa[:, kc, 1024:1536], kc == 0, kc == 7, [wa_res(1024), xnr], [psr])
                    st_, sr = nstg()
                    evac(st_[:], ps[:], [psr], [sr])
                    store(nav[t0 + i * 128:t0 + (i + 1) * 128, :], st_[:], sr, ("scr", "nav", b, i))
                ps, psr = nps()
                for i in range(4):
                    for kc in range(8):
                        mm(ps[:, i * 128:(i + 1) * 128], X[:, kc, i * 128:(i + 1) * 128], Wa[:, kc, 2880:3008],
                           kc == 0, kc == 7, [wa_res(2880), xnr], [psr])
                st_, sr = nstg()
                evac(st_[:], ps[:], [psr], [sr])
                store(gv[:, b * 4:(b + 1) * 4, :],
                      st_[:].rearrange("p (i d) -> p i d", d=128), sr, ("scr", "gv", b))
                if b + 1 < NB:
                    xnorm(b + 1)
                for h in range(4):
                    ps, psr = nps()
                    for kc in range(3):
                        mm(ps[:], Wq[:, kc, h * 192:h * 192 + 128], cqn[:, kc, :], kc == 0, kc == 2, ["s1_Wq", "s1_cqn"], [psr])
                    st_, sr = nstg()
                    evac(st_[:], ps[:], [psr], [sr])
                    store(mqn[h, :, t0:t0 + 512], st_[:], sr, ("scr", "mqn", b, h))
                for hp in range(2):
                    pa, par = nps()
                    for kc in range(3):
                        mm(pa[:], Wqp[:, kc, hp * 128:(hp + 1) * 128], cqn[:, kc, :], kc == 0, kc == 2, ["s1_Wqp", "s1_cqn"], [par])
                    pb, pbr = nps()
                    for kc in range(3):
                        mm(pb[:], Wqps[:, kc, hp * 128:(hp + 1) * 128], cqn[:, kc, :], kc == 0, kc == 2, ["s1_Wqps", "s1_cqn"], [pbr])
                    i = cnt["t"] % 2
                    cnt["t"] += 1
                    tt("dve", t1[i][:], pa[:], Cs[sl][:], ALU.mult, [par, "s1_C%d" % sl], ["s1_t1_%d" % i])
                    tt("dve", t2[i][:], pb[:], Ss[sl][:], ALU.mult, [pbr, "s1_S%d" % sl], ["s1_t2_%d" % i])
                    st_, sr = nstg()
                    tt("pool", st_[:], t1[i][:], t2[i][:], ALU.add, ["s1_t1_%d" % i, "s1_t2_%d" % i], [sr])
                    store(mqp_f[hp * 128:(hp + 1) * 128, t0:t0 + 512], st_[:], sr, ("scr", "mqp", b, hp))
                for h in range(4):
                    ps, psr = nps()
                    for kc in range(2):
                        mm(ps[:], Wkv[:, kc, h * 256:h * 256 + 128], ckvn[:, kc, :], kc == 0, kc == 1, ["s1_Wkv", "s1_ckvn"], [psr])
                    st_, sr = nstg()
                    evac(st_[:], ps[:], [psr], [sr])
                    store(mkn[h, :, t0:t0 + 512], st_[:], sr, ("scr", "mkn", b, h))
                for i in range(4):
                    ps, psr = nps()
                    for kc in range(2):
                        mm(ps[:], ckvn[:, kc, i * 128:(i + 1) * 128], Wv[:, kc, :], kc == 0, kc == 1, ["s1_Wv", "s1_ckvn"], [psr])
                    st_, sr = nstg()
                    evac(st_[:], ps[:], [psr], [sr])
                    store(mv[:, :, b * 4 + i, :].rearrange("h p d -> p h d"),
                          st_[:].rearrange("p (h d) -> p h d", d=128), sr, ("scr", "mv", b, i))
        S.barrier()

    def stage_mla(l, prefetch=None):
        scale = 192.0 ** -0.5
        with ExitStack() as es:
            def sb(name, shape, dt):
                return es.enter_context(nc.sbuf_tensor(name + SUF[0], list(shape), dt))

            def pst(name, shape):
                return es.enter_context(nc.psum_tensor(name + SUF[0], list(shape), F32))
            kpe = sb("ml_kpe", [128, T], BF16)
            Ks = [sb("ml_K%d" % i, [128, T], BF16) for i in range(2)]
            Vs = [sb("ml_V%d" % i, [128, NT, 128], BF16) for i in range(2)]
            Qn = [sb("ml_Qn%d" % i, [128, 512], BF16) for i in range(2)]
            Qp = [sb("ml_Qp%d" % i, [128, 512], BF16) for i in range(2)]
            NP = 3
            NPT = 4
            pt = [sb("ml_pt%d" % i, [128, 512], BF16) for i in range(NPT)]
            acc = [sb("ml_acc%d" % i, [128, 512], F32) for i in range(2)]
            tr1 = sb("ml_tr1", [128, 512], BF16)
            tr2 = sb("ml_tr2", [128, 512], BF16)
            tr3 = sb("ml_tr3", [128, 512], BF16)
            lnt = sb("ml_ln", [128, 512], F32)
            rec = sb("ml_rec", [128, 512], F32)
            yst = [sb("ml_y%d" % i, [128, 512], BF16) for i in range(2)]
            ps_s = [pst("ml_pss%d" % i, [128, 512]) for i in range(NP)]
            ps_o = [pst("ml_pso%d" % i, [128, 512]) for i in range(2)]
            ps_sum = pst("ml_psum", [128, 512])

            S.op("pool", "memset", writes=["ml_kpe"], ap=kpe[64:128, :], constant=0.0)
            for i in range(2):
                S.op("pool", "memset", writes=["ml_Qp%d" % i], ap=Qp[i][64:128, :], constant=0.0)
            S.dma("sp", kpe[0:64, :], mkp[:, :], writes=["ml_kpe"], stream="ld")

            def loadKV(h):
                s = h % 2
                S.dma("sp", Ks[s][:], mkn[h, :, :], writes=["ml_K%d" % s], stream="ld")
                S.dma("sp", Vs[s][:], mv[h], writes=["ml_V%d" % s], stream="ld")

            def loadQ(h, qb, it):
                s = it % 2
                S.dma("sp", Qn[s][:], mqn[h, :, qb * 512:(qb + 1) * 512], writes=["ml_Qn%d" % s], stream="ld")
                S.dma("sp", Qp[s][0:64, :], mqp[h, :, qb * 512:(qb + 1) * 512], writes=["ml_Qp%d" % s], stream="ld")

            items = [(h, qb) for h in range(4) for qb in range(NB)]
            loadKV(0)
            loadQ(0, 0, 0)
            gcount = [0]
            steps = []
            flat = []
            for it, (h, qb) in enumerate(items):
                hs = h % 2
                qs = it % 2
                for kb in range(NT):
                    def qk(it=it, h=h, qb=qb, hs=hs, qs=qs, kb=kb):
                        if kb == min(8, NT - 1) and it == 0 and prefetch is not None:
                            prefetch()
                        if kb == 0 and it + 1 < len(items):
                            nh, nqb = items[it + 1]
                            if nh != h:
                                loadKV(nh)
                            loadQ(nh, nqb, it + 1)
                        i = gcount[0] % NP
                        ip = gcount[0] % NPT
                        gcount[0] += 1
                        mm(ps_s[i][:], Ks[hs][:, kb * 128:(kb + 1) * 128], Qn[qs][:], True, False,
                           ["ml_K%d" % hs, "ml_Qn%d" % qs], ["ml_pss%d" % i])
                        mm(ps_s[i][:], kpe[:, kb * 128:(kb + 1) * 128], Qp[qs][:], False, True,
                           ["ml_kpe", "ml_Qp%d" % qs], ["ml_pss%d" % i])
                        act(pt[ip][:], ps_s[i][:], AF.Exp, ["ml_pss%d" % i], ["ml_pt%d" % ip], scale=scale)
                        return ip
                    steps.append((kb, qk))
                    flat.append((it, h, qb, kb))
            ring = {}
            pend = []
            iprev = [0]
            depth = 2
            n = len(flat)
            for idx in range(n + depth):
                if idx < n:
                    it, h, qb, kb = flat[idx]
                    ring[idx] = steps[idx][1]()
                if idx >= depth:
                    j = idx - depth
                    it, h, qb, kb = flat[j]
                    i = ring.pop(j)
                    hs = h % 2
                    a = it % 2
                    mm(ps_o[a][:], Vs[hs][:, kb, :], pt[i][:], kb == 0, kb == NT - 1,
                       ["ml_V%d" % hs, "ml_pt%d" % i], ["ml_pso%d" % a])
                    if kb % 4 == 1:
                        tt("dve", tr1[:], pt[iprev[0]][:], pt[i][:], ALU.add, ["ml_pt%d" % iprev[0], "ml_pt%d" % i], ["ml_tr1"])
                    elif kb % 4 == 3:
                        tt("dve", tr2[:], pt[iprev[0]][:], pt[i][:], ALU.add, ["ml_pt%d" % iprev[0], "ml_pt%d" % i], ["ml_tr2"])
                        if kb == 3:
                            tt("dve", acc[a][:], tr1[:], tr2[:], ALU.add, ["ml_tr1", "ml_tr2"], ["ml_acc%d" % a])
                        else:
                            tt("dve", tr3[:], tr1[:], tr2[:], ALU.add, ["ml_tr1", "ml_tr2"], ["ml_tr3"])
                            tt("dve", acc[a][:], acc[a][:], tr3[:], ALU.add, ["ml_acc%d" % a, "ml_tr3"], ["ml_acc%d" % a])
                    iprev[0] = i
                    if kb == NT - 1:
                        def fin(a=a, h=h, qb=qb):
                            mm(ps_sum[:], ones_f[:], acc[a][:], True, True, ["ones_f", "ml_acc%d" % a], ["ml_psum"])
                            act(lnt[:], ps_sum[:], AF.Ln, ["ml_psum"], ["ml_ln"])
                            act(rec[:], lnt[:], AF.Exp, ["ml_ln"], ["ml_rec"], scale=-1.0)
                            tt("dve", yst[a][:], ps_o[a][:], rec[:], ALU.mult, ["ml_pso%d" % a, "ml_rec"], ["ml_y%d" % a])
                            S.dma("sp", ymla[h * 128:(h + 1) * 128, qb * 512:(qb + 1) * 512], yst[a][:],
                                  reads=["ml_y%d" % a], writes=[("scr", "ymla", h, qb)], stream="st")
                        pend.append((idx + 4, fin))
                while pend and pend[0][0] <= idx:
                    pend.pop(0)[1]()
            for _, f in pend:
                f()
        S.barrier()

    def stage_na(l):
        with ExitStack() as es:
            def sb(name, shape, dt):
                return es.enter_context(nc.sbuf_tensor(name + SUF[0], list(shape), dt))

            def pst(name, shape):
                return es.enter_context(nc.psum_tensor(name + SUF[0], list(shape), F32))
            RING = 8
            kT = [sb("na_k%d" % i, [128, 4, 128], BF16) for i in range(RING)]
            vr = [sb("na_v%d" % i, [128, 512], BF16) for i in range(RING)]
            qT = [sb("na_q%d" % i, [128, 4, 2, 128], BF16) for i in range(2)]
            bias_int = sb("na_bint", [128, 5, 8, 128], F32)
            bias_sp = sb("na_bsp", [128, 5, 8, 128], F32)
            NP = 3
            sbf = [sb("na_sb%d" % i, [128, 4, 128], F32) for i in range(NP)]
            pt = [sb("na_pt%d" % i, [128, 4, 128], BF16) for i in range(NP)]
            rec = [sb("na_rec%d" % i, [128, 4, 128], F32) for i in range(2)]
            lnt = sb("na_ln", [128, 512], F32)
            yst = [sb("na_y%d" % i, [128, 2, 128], BF16) for i in range(2)]
            ps_s = [pst("na_pss%d" % i, [128, 4, 128]) for i in range(NP)]
            ps_o = [pst("na_pso%d" % i, [128, 4, 128]) for i in range(2)]
            ps_sum = [pst("na_psum%d" % i, [128, 512]) for i in range(2)]

            counts = {}
            for c in cls_of_tile:
                counts[c] = counts.get(c, 0) + 1
            c_int = max(counts, key=lambda c: counts[c])
            S.dma("sp", bias_int[:].rearrange("p j h q -> p (j h q)"), nab[l, c_int], writes=["na_bint"], stream="ld")
            nak_v = nak.rearrange("h d t -> (h d) t").rearrange("(hp p) t -> p hp t", p=128)
            naq_v = naq.rearrange("(hp two) d t -> d two hp t", two=2)
            for i in range(2):
                S.op("pool", "memset", writes=["na_q%d" % i], ap=qT[i][:].rearrange("p a b q -> p (a b q)"), constant=0.0)
            yna_v = yna.rearrange("(pr p) t -> p pr t", p=128)
            loaded = [-1]

            def ensure(kt_hi):
                while loaded[0] < kt_hi:
                    kt = loaded[0] + 1
                    s = kt % RING
                    S.dma("sp", kT[s][:], nak_v[:, :, kt * 128:(kt + 1) * 128], writes=["na_k%d" % s], stream="ld")
                    S.dma("sp", vr[s][:], nav[kt * 128:(kt + 1) * 128, :], writes=["na_v%d" % s], stream="ld")
                    loaded[0] = kt

            def loadq(t):
                s = t % 2
                S.dma("sp", qT[s][0:64, :, 0, :], naq_v[:, 0, :, t * 128:(t + 1) * 128], writes=["na_q%d" % s], stream="ld")
                S.dma("sp", qT[s][64:128, :, 1, :], naq_v[:, 1, :, t * 128:(t + 1) * 128], writes=["na_q%d" % s], stream="ld")

            kt0s = [min(max(t - 2, 0), NT - 5) for t in range(NT)]
            ensure(kt0s[0] + 4)
            loadq(0)
            gcount = [0]
            grp = [0]
            steps = []
            for t in range(NT):
                kt0 = kt0s[t]
                cls = cls_of_tile[t]
                for g in range(2):
                    a = (t * 2 + g) % 2
                    for j in range(5):
                        def front(t=t, g=g, j=j, kt0=kt0, cls=cls, holder=None):
                            if g == 0 and j == 0:
                                if t + 1 < NT:
                                    ensure(kt0s[t + 1] + 4)
                                    loadq(t + 1)
                                if cls != c_int:
                                    S.dma("sp", bias_sp[:].rearrange("p j h q -> p (j h q)"), nab[l, cls],
                                          writes=["na_bsp"], stream="ld")
                            bias, bres = (bias_int, "na_bint") if cls == c_int else (bias_sp, "na_bsp")
                            kt = kt0 + j
                            s = kt % RING
                            qs = t % 2
                            i = gcount[0] % NP
                            gcount[0] += 1
                            for h2 in range(2):
                                hp = g * 2 + h2
                                mm(ps_s[i][:, h2 * 2:(h2 + 1) * 2, :], kT[s][:, hp, :], qT[qs][:, hp, :, :], True, True,
                                   ["na_k%d" % s, "na_q%d" % qs], ["na_pss%d" % i])
                            stt(sbf[i][:], ps_s[i][:], 0.125, bias[:, j, g * 4:(g + 1) * 4, :], ALU.mult, ALU.add,
                                ["na_pss%d" % i, bres], ["na_sb%d" % i])
                            act(pt[i][:], sbf[i][:], AF.Exp, ["na_sb%d" % i], ["na_pt%d" % i])
                            return i

                        def back(i, t=t, g=g, j=j, kt0=kt0, a=a):
                            kt = kt0 + j
                            s = kt % RING
                            for hh in range(4):
                                pr = g * 2 + hh // 2
                                mm(ps_o[a][:, hh, :], vr[s][:, pr * 128:(pr + 1) * 128], pt[i][:, hh, :], j == 0 and hh == 0, j == 4,
                                   ["na_v%d" % s, "na_pt%d" % i], ["na_pso%d" % a], skip=True)
                            mm(ps_sum[a][:], ones_bf[:], pt[i][:].rearrange("p h q -> p (h q)"), j == 0, j == 4,
                               ["ones_bf", "na_pt%d" % i], ["na_psum%d" % a])
                            if j == 4:
                                def fin_a():
                                    act(lnt[:], ps_sum[a][:], AF.Ln, ["na_psum%d" % a], ["na_ln"])
                                    act(rec[a][:].rearrange("p h q -> p (h q)"), lnt[:], AF.Exp, ["na_ln"], ["na_rec%d" % a], scale=-1.0)

                                def fin_b():
                                    po4 = ps_o[a][:].rearrange("p (pr two) q -> p pr two q", two=2)
                                    rc4 = rec[a][:].rearrange("p (pr two) q -> p pr two q", two=2)
                                    for par in range(2):
                                        lo, hi = par * 64, par * 64 + 64
                                        tt("dve", yst[a][lo:hi, :, :], po4[lo:hi, :, par, :], rc4[lo:hi, :, par, :], ALU.mult,
                                           ["na_pso%d" % a, "na_rec%d" % a], ["na_y%d" % a])
                                    S.dma("sp", yna_v[:, g * 2:(g + 1) * 2, t * 128:(t + 1) * 128], yst[a][:],
                                          reads=["na_y%d" % a], writes=[("scr", "yna", t, g)], stream="st")
                                return [(1, fin_a), (4, fin_b)]
                            return None
                        steps.append((front, back))
            run_pipeline(steps, 2, 0)
        S.barrier()

    def stage_gqa(l):
        with ExitStack() as es:
            def sb(name, shape, dt):
                return es.enter_context(nc.sbuf_tensor(name + SUF[0], list(shape), dt))

            def pst(name, shape):
                return es.enter_context(nc.psum_tensor(name + SUF[0], list(shape), F32))
            Kall = sb("gq_K", [128, T], BF16)
            Vall = sb("gq_V", [128, NT, 128], BF16)
            qT = [sb("gq_q%d" % i, [128, 2, 4, 128], BF16) for i in range(2)]
            msk = sb("gq_msk", [128, 2, 512], F32)
            sk = sb("gq_sink", [128, 8], F32)
            esk = sb("gq_esink", [128, 8], F32)
            NP = 3
            sbf = [sb("gq_sb%d" % i, [128, 512], F32) for i in range(NP)]
            pt = [sb("gq_pt%d" % i, [128, 512], BF16) for i in range(NP)]
            rec = [sb("gq_rec%d" % i, [128, 512], F32) for i in range(3)]
            lnt = sb("gq_ln", [128, 512], F32)
            den = sb("gq_den", [128, 512], F32)
            yst = [sb("gq_y%d" % i, [128, 4, 128], BF16) for i in range(3)]
            ps_s = [pst("gq_pss%d" % i, [128, 512]) for i in range(NP)]
            ps_o = [pst("gq_pso%d" % i, [128, 512]) for i in range(3)]
            ps_sum = [pst("gq_psum%d" % i, [128, 512]) for i in range(2)]

            S.dma("sp", Kall[:], gk.rearrange("h d t -> (h d) t"), writes=["gq_K"], stream="ld")
            for i in range(2):
                S.op("pool", "memset", writes=["gq_q%d" % i], ap=qT[i][:].rearrange("p a b q -> p (a b q)"), constant=0.0)
            S.dma("sp", Vall[:], gv, writes=["gq_V"], stream="ld")
            S.dma("sp", msk[:], gmask.rearrange("m p n -> p m n"), writes=["gq_msk"], stream="ld")
            S.dma("sp", sk[0:64, :], sinkr[l], writes=["gq_sink"], stream="ld")
            S.dma("sp", sk[64:128, :], sinkr[l], writes=["gq_sink"], stream="ld")
            act(esk[:], sk[:], AF.Exp, ["gq_sink"], ["gq_esink"])
            gq_v = gq.rearrange("h d t -> d h t")
            ygqa_v = ygqa.rearrange("(h d) t -> d h t", d=64)

            def loadq(t):
                s = t % 2
                S.dma("sp", qT[s][0:64, 0, :, :], gq_v[:, 0:4, t * 128:(t + 1) * 128], writes=["gq_q%d" % s], stream="ld")
                S.dma("sp", qT[s][64:128, 1, :, :], gq_v[:, 4:8, t * 128:(t + 1) * 128], writes=["gq_q%d" % s], stream="ld")

            loadq(0)
            gcount = [0]
            steps = []
            for t in range(NT):
                for kvh in range(2):
                    a = (t * 2 + kvh) % 3
                    a2 = (t * 2 + kvh) % 2
                    kts = [kt for kt in (t - 1, t, t + 1) if 0 <= kt < NT]
                    for jj, kt in enumerate(kts):
                        first = jj == 0
                        last = jj == len(kts) - 1

                        def front(t=t, kvh=kvh, kt=kt, first=first):
                            if kvh == 0 and first and t + 1 < NT:
                                loadq(t + 1)
                            qs = t % 2
                            i = gcount[0] % NP
                            gcount[0] += 1
                            mm(ps_s[i][:], Kall[:, kt * 128:(kt + 1) * 128],
                               qT[qs][:, kvh, :, :], True, True,
                               ["gq_K", "gq_q%d" % qs], ["gq_pss%d" % i])
                            if kt != t:
                                m = 0 if kt < t else 1
                                tt("dve", sbf[i][:], ps_s[i][:], msk[:, m, :], ALU.add, ["gq_pss%d" % i, "gq_msk"], ["gq_sb%d" % i])
                                act(pt[i][:], sbf[i][:], AF.Exp, ["gq_sb%d" % i], ["gq_pt%d" % i], scale=0.125)
                            else:
                                act(pt[i][:], ps_s[i][:], AF.Exp, ["gq_pss%d" % i], ["gq_pt%d" % i], scale=0.125)
                            return i

                        def back(i, t=t, kvh=kvh, kt=kt, first=first, last=last, a=a, a2=a2):
                            mm(ps_o[a][:], Vall[:, kt, :], pt[i][:], first, last,
                               ["gq_V", "gq_pt%d" % i], ["gq_pso%d" % a])
                            mm(ps_sum[a2][:], ones_bf[:], pt[i][:], first, last,
                               ["ones_bf", "gq_pt%d" % i], ["gq_psum%d" % a2])
                            if last:
                                lo, hi = kvh * 64, kvh * 64 + 64

                                def fin_a():
                                    for g in range(4):
                                        h = kvh * 4 + g
                                        ts("dve", den[lo:hi, g * 128:(g + 1) * 128], ps_sum[a2][lo:hi, g * 128:(g + 1) * 128],
                                           esk[lo:hi, h:h + 1], None, ALU.add, None, ["gq_psum%d" % a2, "gq_esink"], ["gq_den"])
                                    act(lnt[lo:hi, :], den[lo:hi, :], AF.Ln, ["gq_den"], ["gq_ln"])
                                    act(rec[a][lo:hi, :], lnt[lo:hi, :], AF.Exp, ["gq_ln"], ["gq_rec%d" % a], scale=-1.0)

                                def fin_b():
                                    tt("dve", yst[a][lo:hi, :, :].rearrange("p h q -> p (h q)"), ps_o[a][lo:hi, :], rec[a][lo:hi, :], ALU.mult,
                                       ["gq_pso%d" % a, "gq_rec%d" % a], ["gq_y%d" % a])
                                    S.dma("sp", ygqa_v[:, kvh * 4:(kvh + 1) * 4, t * 128:(t + 1) * 128], yst[a][lo:hi, :, :],
                                          reads=["gq_y%d" % a], writes=[("scr", "ygqa", t, kvh)], stream="st")
                                return [(1, fin_a), (4, fin_b)]
                            return None
                        steps.append((front, back))
            run_pipeline(steps, 2, 0)
        S.barrier()

    def stage3(l, xsrc, xdst, W3):
        with ExitStack() as es:
            def sb(name, shape, dt):
                return es.enter_context(nc.sbuf_tensor(name + SUF[0], list(shape), dt))

            def pst(name, shape):
                return es.enter_context(nc.psum_tensor(name + SUF[0], list(shape), F32))
            Wg, Wb, Wo = W3
            g1s = sb("s3_g1", [128, 8], F32)
            bgs = sb("s3_bg", [128, 24], F32)
            xs = [sb("s3_x%d" % i, [128, 8, 512], F32) for i in range(2)]
            xn2 = [sb("s3_xn%d" % i, [128, 8, 512], BF16) for i in range(2)]
            tmp = sb("s3_tmp", [128, 512], F32)
            rstd = sb("s3_rstd", [128, 512], F32)
            ys = [[sb("s3_y%d_%d" % (i, s), [128, 4, 512], BF16) for s in range(2)] for i in range(3)]
            mg = sb("s3_mg", [128, 8, 512], BF16)
            gt = [sb("s3_gt%d" % i, [128, 512], F32) for i in range(3)]
            pr = [sb("s3_pr%d" % i, [128, 512], F32) for i in range(3)]
            m01 = sb("s3_m01", [128, 512], F32)
            xo = [sb("s3_xo%d" % i, [128, 512], F32) for i in range(3)]
            NPS = 6
            pss = [pst("s3_ps%d" % i, [128, 512]) for i in range(NPS)]
            ps_ss = pst("s3_pss", [128, 512])
            cnt = {"ps": 0, "xo": 0}

            def nps():
                i = cnt["ps"] % NPS
                cnt["ps"] += 1
                return pss[i], "s3_ps%d" % i

            S.dma("sp", g1s[:], g1[l], writes=["s3_g1"], stream="ld")
            S.dma("sp", bgs[:], bg[l], writes=["s3_bg"], stream="ld")
            xsrc_f = fm(xsrc)
            xdst_f = fm(xdst)
            ysrc = [fm(yna), fm(ymla), fm(ygqa)]

            def load(b):
                sl = b % 2
                t0 = b * 512
                S.dma("sp", xs[sl][:], xsrc_f[:, :, t0:t0 + 512], writes=["s3_x%d" % sl], stream="ld")
                for i in range(3):
                    S.dma("sp", ys[i][sl][:], ysrc[i][:, :, t0:t0 + 512], writes=["s3_y%d_%d" % (i, sl)], stream="ld")

            def xnorm3(b):
                sl_ = b % 2
                fm_norm(xs[sl_], "s3_x%d" % sl_, 8, 512, g1s, "s3_g1", xn2[sl_], "s3_xn%d" % sl_, ps_ss, "s3_pss", tmp, "s3_tmp",
                        rstd, "s3_rstd", xn2[sl_], "s3_xn%d" % sl_, 1.0 / 32.0)

            load(0)
            xnorm3(0)
            for b in range(NB):
                sl = b % 2
                t0 = b * 512
                if b + 1 < NB:
                    load(b + 1)
                xr = "s3_x%d" % sl
                xn = xn2[sl]
                xnres = "s3_xn%d" % sl
                for m in range(8):
                    if m == 4 and b + 1 < NB:
                        xnorm3(b + 1)
                    for i in range(3):
                        pg, pgr = nps()
                        for kc in range(8):
                            mm(pg[:], Wg[:, kc, i * 1024 + m * 128:i * 1024 + (m + 1) * 128], xn[:, kc, :],
                               kc == 0, kc == 7, [xnres], [pgr])
                        py, pyr = nps()
                        yr = "s3_y%d_%d" % (i, sl)
                        for kc in range(4):
                            mm(py[:], Wb[:, i, kc, m * 128:(m + 1) * 128], ys[i][sl][:, kc, :], kc == 0, kc == 3,
                               [yr], [pyr])
                        act(gt[i][:], pg[:], AF.Sigmoid, [pgr, "s3_bg"], ["s3_gt%d" % i],
                            bias=bgs[:, i * 8 + m:i * 8 + m + 1], scale=1.0)
                        tt("dve", pr[i][:], gt[i][:], py[:], ALU.mult, ["s3_gt%d" % i, pyr], ["s3_pr%d" % i])
                    tt("pool", m01[:], pr[0][:], pr[1][:], ALU.add, ["s3_pr0", "s3_pr1"], ["s3_m01"])
                    tt("pool", mg[:, m, :], m01[:], pr[2][:], ALU.add, ["s3_m01", "s3_pr2"], [("s3_mg", m)])
                for m in range(8):
                    po, por = nps()
                    for kc in range(8):
                        mm(po[:], Wo[:, kc, m * 128:(m + 1) * 128], mg[:, kc, :], kc == 0, kc == 7,
                           [("s3_mg", kc)], [por])
                    i = cnt["xo"] % 3
                    cnt["xo"] += 1
                    tt("dve", xo[i][:], po[:], xs[sl][:, m, :], ALU.add, [por, xr], ["s3_xo%d" % i])
                    S.dma("sp", xdst_f[:, m, t0:t0 + 512], xo[i][:], reads=["s3_xo%d" % i],
                          writes=[("scr", "x1", b, m)], stream="st")
        S.barrier()

    def stage_ffn(l, xsrc, xdst, final=False):
        NOUT = 382
        FW = 384
        wins = []
        o0 = 0
        while o0 < T:
            o1 = min(o0 + NOUT, T)
            u0 = max(o0 - 1, 0)
            u1 = min(o1 + 1, T)
            wins.append((o0, o1, u0, u1))
            o0 = o1
        with ExitStack() as es:
            def sb(name, shape, dt):
                return es.enter_context(nc.sbuf_tensor(name + SUF[0], list(shape), dt))

            def pst(name, shape):
                return es.enter_context(nc.psum_tensor(name + SUF[0], list(shape), F32))
            Wu = sb("ff_Wu", [128, 8, 2 * D_FF], BF16)
            Wd = sb("ff_Wd", [128, 22, 1024], BF16)
            g2s = sb("ff_g2", [128, 8], F32)
            cws = sb("ff_cw", [128, 44, 3], F32)
            cbs = sb("ff_cb", [128, 44], F32)
            xs = sb("ff_x", [128, 8, FW], F32)
            xn = sb("ff_xn", [128, 8, FW], BF16)
            tmp = sb("ff_tmp", [128, FW], F32)
            rstd = sb("ff_rstd", [128, FW], F32)
            hm = sb("ff_hm", [128, 22, FW], BF16)
            av = [sb("ff_a%d" % i, [128, FW], F32) for i in range(3)]
            gvv = [sb("ff_gv%d" % i, [128, FW], F32) for i in range(3)]
            gg = [sb("ff_gg%d" % i, [128, FW], F32) for i in range(3)]
            xres = [sb("ff_xr%d" % i, [128, FW], F32) for i in range(2)]
            xo = [sb("ff_xo%d" % i, [128, FW], F32) for i in range(2)]
            NPS = 6
            pss = [pst("ff_ps%d" % i, [128, 512]) for i in range(NPS)]
            ps_ss = pst("ff_pss", [128, 512])
            cnt = {"ps": 0, "c": 0, "xo": 0}

            def nps():
                i = cnt["ps"] % NPS
                cnt["ps"] += 1
                return pss[i], "ff_ps%d" % i

            wu_l = w_up[l].rearrange("(kc p) n -> p kc n", p=128)
            for c in range(22):
                for cc in (c, 22 + c):
                    S.dma("pool", Wu[:, :, cc * 128:(cc + 1) * 128], wu_l[:, :, cc * 128:(cc + 1) * 128],
                          writes=[("ff_Wu", cc)], stream="w")
            wd_l = w_down[l].rearrange("(kc p) n -> p kc n", p=128)
            for kc in range(22):
                S.dma("pool", Wd[:, kc, :], wd_l[:, kc, :], writes=[("ff_Wd", kc)], stream="w")
            S.dma("sp", g2s[:], g2[l], writes=["ff_g2"], stream="ld")
            S.dma("sp", cws[:].rearrange("p c j -> p (c j)"), cw[l], writes=["ff_cw"], stream="ld")
            S.dma("sp", cbs[:], cb[l], writes=["ff_cb"], stream="ld")
            xsrc_f = fm(xsrc)
            xdst_f = fm(xdst)
            if final:
                x2b = sb("ff_x2b", [128, 8, FW], F32)
                tmpf = sb("ff_tmpf", [128, FW], F32)
                rstdf = sb("ff_rstdf", [128, FW], F32)
                gfs = sb("ff_gf", [128, 8], F32)
                ps_fn = pst("ff_psfn", [128, 512])
                S.dma("sp", gfs[:], gf, writes=["ff_gf"], stream="ld")
                y_f = fm(yT)

            def load(w):
                o0, o1, u0, u1 = wins[w]
                S.dma("sp", xs[:, :, 0:u1 - u0], xsrc_f[:, :, u0:u1], writes=["ff_x"], stream="ld")

            load(0)
            for w, (o0, o1, u0, u1) in enumerate(wins):
                nu = u1 - u0
                no = o1 - o0
                uoff = o0 - u0
                if w == 0:
                    fm_norm(xs, "ff_x", 8, nu, g2s, "ff_g2", xn, "ff_xn", ps_ss, "ff_pss",
                            tmp, "ff_tmp", rstd, "ff_rstd", xn, "ff_xn", 1.0 / 32.0)
                if w + 1 < len(wins):
                    load(w + 1)
                defer = []
                for c in range(22):
                    k = cnt["c"] % 3
                    cnt["c"] += 1
                    outs = []
                    for (cc, dst, dres) in ((22 + c, gvv[k], "ff_gv%d" % k), (c, av[k], "ff_a%d" % k)):
                        ps, psr = nps()
                        for kc in range(8):
                            mm(ps[:, 0:nu], Wu[:, kc, cc * 128:(cc + 1) * 128], xn[:, kc, 0:nu], kc == 0, kc == 7,
                               [("ff_Wu", cc), "ff_xn"], [psr])
                        act(dst[:, 0:no], ps[:, uoff:uoff + no], AF.Identity, [psr, "ff_cw", "ff_cb"], [dres],
                            bias=cbs[:, cc:cc + 1], scale=cws[:, cc, 1:2])
                        lo = 0 if uoff >= 1 else 1
                        stt(dst[:, lo:no], ps[:, lo + uoff - 1:no + uoff - 1], cws[:, cc, 0:1], dst[:, lo:no],
                            ALU.mult, ALU.add, [psr, "ff_cw", dres], [dres])
                        hi = min(no, nu - uoff - 1)
                        stt(dst[:, 0:hi], ps[:, uoff + 1:uoff + 1 + hi], cws[:, cc, 2:3], dst[:, 0:hi],
                            ALU.mult, ALU.add, [psr, "ff_cw", dres], [dres])
                    def gl(c=c, k=k, no=no):
                        act(gg[k][:, 0:no], gvv[k][:, 0:no], AF.Gelu_apprx_tanh, ["ff_gv%d" % k], ["ff_gg%d" % k])
                        tt("dve" if c >= 20 else "pool", hm[:, c, 0:no], gg[k][:, 0:no], av[k][:, 0:no], ALU.mult,
                           ["ff_gg%d" % k, "ff_a%d" % k], [("ff_hm", c)])
                    while defer:
                        defer.pop(0)()
                    defer.append(gl)
                while defer:
                    defer.pop(0)()
                if w + 1 < len(wins):
                    n_o0, n_o1, n_u0, n_u1 = wins[w + 1]
                    fm_norm(xs, "ff_x", 8, n_u1 - n_u0, g2s, "ff_g2", xn, "ff_xn", ps_ss, "ff_pss",
                            tmp, "ff_tmp", rstd, "ff_rstd", xn, "ff_xn", 1.0 / 32.0)
                for m in range(8):
                    i = cnt["xo"] % 2
                    cnt["xo"] += 1
                    S.dma("sp", xres[i][:, 0:no], xsrc_f[:, m, o0:o1], writes=["ff_xr%d" % i], stream="ld2")
                    pd, pdr = nps()
                    for c in range(22):
                        mm(pd[:, 0:no], Wd[:, c, m * 128:(m + 1) * 128], hm[:, c, 0:no], c == 0, c == 21,
                           [("ff_Wd", c), ("ff_hm", c)], [pdr])
                    if not final:
                        tt("dve", xo[i][:, 0:no], pd[:, 0:no], xres[i][:, 0:no], ALU.add, [pdr, "ff_xr%d" % i], ["ff_xo%d" % i])
                        S.dma("sp", xdst_f[:, m, o0:o1], xo[i][:, 0:no], reads=["ff_xo%d" % i],
                              writes=[("scr", "x2", w, m)], stream="st")
                    else:
                        tt("dve", x2b[:, m, 0:no], pd[:, 0:no], xres[i][:, 0:no], ALU.add, [pdr, "ff_xr%d" % i], ["ff_x2b"])
                if final:
                    hres = [("ff_hm", c) for c in range(8)]
                    act(hm[:, 0:8, 0:no], x2b[:, :, 0:no], AF.Square, ["ff_x2b"], hres, scale=1.0 / 32.0)
                    for c in range(8):
                        mm(ps_fn[:, 0:no], ones_bf[:], hm[:, c, 0:no], c == 0, c == 7, ["ones_bf", ("ff_hm", c)], ["ff_psfn"])
                    act(tmpf[:, 0:no], ps_fn[:, 0:no], AF.Ln, ["ff_psfn"], ["ff_tmpf"], bias=EPS, scale=1.0)
                    act(rstdf[:, 0:no], tmpf[:, 0:no], AF.Exp, ["ff_tmpf"], ["ff_rstdf"], scale=-0.5)
                    for m in range(8):
                        i = cnt["xo"] % 2
                        cnt["xo"] += 1
                        stt(xo[i][:, 0:no], x2b[:, m, 0:no], gfs[:, m:m + 1], rstdf[:, 0:no], ALU.mult, ALU.mult,
                            ["ff_x2b", "ff_gf", "ff_rstdf"], ["ff_xo%d" % i])
                        S.dma("sp", y_f[:, m, o0:o1], xo[i][:, 0:no], reads=["ff_xo%d" % i],
                              writes=[("out", w, m)], stream="st")
        S.barrier()

    def stage_final(xsrc):
        with ExitStack() as es:
            def sb(name, shape, dt):
                return es.enter_context(nc.sbuf_tensor(name + SUF[0], list(shape), dt))
            gfs = sb("fn_g", [128, 8], F32)
            xs = [sb("fn_x%d" % i, [128, 8, 512], F32) for i in range(2)]
            sq = sb("fn_sq", [128, 8, 512], BF16)
            tmp = sb("fn_tmp", [128, 512], F32)
            rstd = sb("fn_rstd", [128, 512], F32)
            xo = [sb("fn_xo%d" % i, [128, 8, 512], F32) for i in range(2)]
            ps_ss = es.enter_context(nc.psum_tensor("fn_pss" + SUF[0], [128, 512], F32))
            S.dma("sp", gfs[:], gf, writes=["fn_g"], stream="ld")
            xsrc_f = fm(xsrc)
            y_f = fm(yT)
            S.dma("sp", xs[0][:], xsrc_f[:, :, 0:512], writes=["fn_x0"], stream="ld")
            for b in range(NB):
                sl = b % 2
                t0 = b * 512
                if b + 1 < NB:
                    S.dma("sp", xs[1 - sl][:], xsrc_f[:, :, t0 + 512:t0 + 1024], writes=["fn_x%d" % (1 - sl)], stream="ld")
                fm_norm(xs[sl], "fn_x%d" % sl, 8, 512, gfs, "fn_g", sq, "fn_sq", ps_ss, "fn_pss", tmp, "fn_tmp",
                        rstd, "fn_rstd", xo[sl], "fn_xo%d" % sl, 1.0 / 32.0)
                S.dma("sp", y_f[:, :, t0:t0 + 512], xo[sl][:], reads=["fn_xo%d" % sl], writes=[("out", b)], stream="st")
        S.barrier()

    xcur = xT
    for l in range(L):
        SUF[0] = "_L%d" % l
        if stages is None or "s1" in stages:
            stage1(l, xcur)
        with ExitStack() as es3:
            Wg = es3.enter_context(nc.sbuf_tensor("s3_Wg" + SUF[0], [128, 8, 3072], BF16))
            Wb = es3.enter_context(nc.sbuf_tensor("s3_Wb" + SUF[0], [128, 3, 4, 1024], BF16))
            Wo = es3.enter_context(nc.sbuf_tensor("s3_Wo" + SUF[0], [128, 8, 1024], BF16))
            def pre3(l=l, Wg=Wg, Wb=Wb, Wo=Wo):
                win_l = w_in[l].rearrange("(kc p) n -> p kc n", p=128)
                for kc in range(8):
                    S.dma("pool", Wg[:, kc, :], win_l[:, kc, 3008:6080], writes=["s3_Wg"], stream="w")
                for i in range(3):
                    wv = w_br[i][l].rearrange("(kc p) n -> p kc n", p=128)
                    for kc in range(4):
                        S.dma("pool", Wb[:, i, kc, :], wv[:, kc, :], writes=["s3_Wb"], stream="w")
                wo_l = w_out[l].rearrange("(kc p) n -> p kc n", p=128)
                for kc in range(8):
                    S.dma("pool", Wo[:, kc, :], wo_l[:, kc, :], writes=["s3_Wo"], stream="w")
            if stages is None or "mla" in stages:
                stage_mla(l, pre3 if (stages is None or "s3" in stages) else None)
            elif stages is not None and "s3" in stages:
                pre3()
            if stages is None or "na" in stages:
                stage_na(l)
            if stages is None or "gqa" in stages:
                stage_gqa(l)
            if stages is None or "s3" in stages:
                stage3(l, xcur, XA, (Wg, Wb, Wo))
        if stages is None or "ffn" in stages:
            stage_ffn(l, XA, XB, final=(l == L - 1))
        xcur = XB
    S.barrier()
    S.emit()
    return nc


def prep_shared(inputs, T, L):
    f = lambda a: np.ascontiguousarray(np.asarray(a, dtype=np.float32))
    cls_of_tile, classes = na_classes(T)
    C, Sn = rope_tables_fm(T)
    sh = {
        "w_in": f(inputs["w_in"]), "mla_w_uq": f(inputs["mla_w_uq"]), "mla_w_ukv": f(inputs["mla_w_ukv"]),
        "w_br_na": f(inputs["w_br_na"]), "w_br_mla": f(inputs["w_br_mla"]), "w_br_gqa": f(inputs["w_br_gqa"]),
        "w_out": f(inputs["w_out"]), "w_up": f(inputs["w_up"]), "w_down": f(inputs["w_down"]),
        "g1": np.stack([pcol(f(inputs["norm1_g"])[l], 8) for l in range(L)]),
        "g2": np.stack([pcol(f(inputs["norm2_g"])[l], 8) for l in range(L)]),
        "gf": pcol(f(inputs["final_g"]), 8),
        "bg": np.stack([pcol(f(inputs["b_gate"])[l], 24) for l in range(L)]),
        "gqa": np.stack([pcol(f(inputs["mla_qa_g"])[l], 3) for l in range(L)]),
        "gkva": np.stack([pcol(f(inputs["mla_kva_g"])[l], 2) for l in range(L)]),
        "sinkr": np.ascontiguousarray(np.broadcast_to(f(inputs["gqa_sink"])[:, None, :], (L, 64, 8))),
        "cw": np.stack([np.ascontiguousarray(
            f(inputs["conv_w"])[l].reshape(3, 44, 128).transpose(2, 1, 0).reshape(128, 132)) for l in range(L)]),
        "cb": np.stack([pcol(f(inputs["conv_b"])[l], 44) for l in range(L)]),
        "ropeC": C, "ropeS": Sn,
        "nab": na_bias_tables(f(inputs["na_rpb"]), classes).reshape(L, len(classes), 128, 5 * 8 * 128),
        "gmask": gqa_masks(),
    }
    return sh, cls_of_tile, len(classes)


_CACHE = {}


def kernel(**inputs):
    x = np.asarray(inputs["x"], dtype=np.float32)
    B, T, _ = x.shape
    L = np.asarray(inputs["w_in"]).shape[0]
    sh, cls_of_tile, ncls = prep_shared(inputs, T, L)
    key = (T, L, ncls)
    if key not in _CACHE:
        _CACHE[key] = build(T, L, ncls, cls_of_tile)
    nc = _CACHE[key]
    in_maps = []
    for b in range(B):
        m = dict(sh)
        m["xT"] = np.ascontiguousarray(x[b].T)
        in_maps.append(m)
    res = run_bass_kernel_spmd(nc, in_maps, core_ids=list(range(B)))
    out = np.empty((B, T, D), np.float32)
    for b in range(B):
        out[b] = np.asarray(res.results[b]["yT"]).T
    return out
```

```python
import numpy as np
import ml_dtypes
from contextlib import ExitStack
import concourse.bass as bass
import concourse.mybir as mybir
from concourse.bass_utils import run_bass_kernel_spmd

F32 = mybir.dt.float32
BF16 = mybir.dt.bfloat16
AF = mybir.ActivationFunctionType
ALU = mybir.AluOpType

D = 1024
L_FULL = 2
T_FULL = 8192
GRID_W = 64
EPS = 1e-6
D_FF = 2816
IN_W = 6080
NEG = -30000.0
COMPUTE = ("pe", "act", "dve", "pool")


class Sched:
    def __init__(self, nc, n_dma_sems=8):
        self.nc = nc
        self.ins = {e: [] for e in ("pe", "act", "dve", "pool", "sp")}
        self.known = {e: {} for e in self.ins}
        self.res = {}
        self.milestones = {e: set() for e in COMPUTE}
        self.streams = {}
        self.n_dma_sems = n_dma_sems
        self.dma_sem_keys = []
        self.dma_latest = {}
        self.last_compute = {e: 0 for e in COMPUTE}

    def _r(self, key):
        r = self.res.get(key)
        if r is None:
            r = ({}, {})
            self.res[key] = r
        return r

    def _deps(self, eng, reads, writes, is_dma):
        deps = {}
        me = ("E", eng)

        def add(k, v, same_ok):
            if (not is_dma) and same_ok and k == me:
                return
            if deps.get(k, 0) < v:
                deps[k] = v

        for r in reads:
            W, R = self._r(r)
            for k, v in W.items():
                add(k, v, False)
        for w in writes:
            W, R = self._r(w)
            for k, v in W.items():
                add(k, v, True)
            for k, v in R.items():
                add(k, v, True)
        out = []
        kn = self.known[eng]
        for k, v in deps.items():
            if kn.get(k, 0) >= v:
                continue
            kn[k] = v
            out.append((k, v))
            if k[0] == "E":
                self.milestones[k[1]].add(v)
        return out

    def _update(self, tok, reads, writes):
        k, v = tok
        for r in reads:
            W, R = self._r(r)
            if R.get(k, 0) < v:
                R[k] = v
        for w in writes:
            W, R = self._r(w)
            W.clear()
            R.clear()
            W[k] = v

    def op(self, eng, name, reads=(), writes=(), **kw):
        waits = self._deps(eng, reads, writes, False)
        self.ins[eng].append((name, kw, waits, None))
        tok = (("E", eng), len(self.ins[eng]))
        self.last_compute[eng] = len(self.ins[eng])
        self._update(tok, reads, writes)

    def dma(self, q, out, in_, reads=(), writes=(), stream="ld"):
        st = self.streams.get(stream)
        if st is None:
            base = len(self.dma_sem_keys)
            keys = [("D", base + i) for i in range(self.n_dma_sems)]
            self.dma_sem_keys += keys
            st = {"keys": keys, "n": 0}
            self.streams[stream] = st
        i = st["n"]
        st["n"] += 1
        K = len(st["keys"])
        key = st["keys"][i % K]
        val = 16 * (i // K + 1)
        waits = self._deps(q, reads, writes, True)
        if val > 16 and self.known[q].get(key, 0) < val - 16:
            self.known[q][key] = val - 16
            waits.append((key, val - 16))
        self.ins[q].append(("dma_start", dict(out=out, in_=in_), waits, (key, 16)))
        self.dma_latest[key] = val
        self._update((key, val), reads, writes)

    def barrier(self):
        toks = []
        for e in COMPUTE:
            if self.last_compute[e]:
                toks.append((("E", e), self.last_compute[e]))
        for k, v in self.dma_latest.items():
            toks.append((k, v))
        for e in self.ins:
            waits = []
            for k, v in toks:
                if k == ("E", e):
                    continue
                if self.known[e].get(k, 0) >= v:
                    continue
                self.known[e][k] = v
                waits.append((k, v))
                if k[0] == "E":
                    self.milestones[k[1]].add(v)
            if waits:
                self.ins[e].append((None, None, waits, None))
        self.res = {}

    def emit(self):
        nc = self.nc
        sems = {}
        for e in COMPUTE:
            sems[("E", e)] = nc.alloc_semaphore("sem_" + e)
        for k in self.dma_sem_keys:
            sems[k] = nc.alloc_semaphore("semd%d" % k[1])
        rank = {}
        for e in COMPUTE:
            ms = sorted(self.milestones[e])
            rank[e] = {v: i + 1 for i, v in enumerate(ms)}

        def run(engine, lst, ename):
            my = rank.get(ename, {})
            mysem = sems.get(("E", ename))
            for idx, (name, kw, waits, dinc) in enumerate(lst):
                for k, v in waits:
                    if k[0] == "E":
                        engine.wait_ge(sems[k], rank[k[1]][v])
                    else:
                        engine.wait_ge(sems[k], v)
                if name is None:
                    continue
                r = getattr(engine, name)(**kw)
                if dinc is not None:
                    r.then_inc(sems[dinc[0]], dinc[1])
                elif (idx + 1) in my:
                    r.then_inc(mysem, 1)

        with nc.Block() as block:
            @block.tensor
            def _(e):
                run(e, self.ins["pe"], "pe")

            @block.scalar
            def _(e):
                run(e, self.ins["act"], "act")

            @block.vector
            def _(e):
                run(e, self.ins["dve"], "dve")

            @block.gpsimd
            def _(e):
                run(e, self.ins["pool"], "pool")

            @block.sync
            def _(e):
                run(e, self.ins["sp"], "sp")


def rope_tables_fm(T):
    inv = 1.0 / (10000.0 ** (np.arange(0, 64, 2, dtype=np.float32) / 64.0))
    ang = np.arange(T, dtype=np.float32)[:, None] * inv[None, :]
    c = np.cos(ang).astype(np.float32).T
    s = np.sin(ang).astype(np.float32).T
    C = np.concatenate([c, c, c, c], axis=0)
    S = np.concatenate([-s, s, -s, s], axis=0)
    return np.ascontiguousarray(C), np.ascontiguousarray(S)


def na_classes(T):
    rows = T // GRID_W
    NT = T // 128
    kk = np.arange(128)
    qq = np.arange(128)
    cls_of_tile = []
    classes = []
    keys = {}
    for t in range(NT):
        kt0 = min(max(t - 2, 0), NT - 5)
        r = 2 * t + qq // 64
        cq = qq % 64
        r_start = np.clip(r - 4, 0, rows - 8)
        c_start = np.clip(cq - 8, 0, GRID_W - 16)
        valid = np.zeros((128, 5, 128), bool)
        ri = np.zeros((128, 5, 128), np.int64)
        ci = np.zeros((128, 5, 128), np.int64)
        for j in range(5):
            kt = kt0 + j
            rk = 2 * kt + kk // 64
            ck = kk % 64
            v = ((rk[:, None] >= r_start[None, :]) & (rk[:, None] < r_start[None, :] + 8)
                 & (ck[:, None] >= c_start[None, :]) & (ck[:, None] < c_start[None, :] + 16))
            valid[:, j, :] = v
            ri[:, j, :] = np.clip(rk[:, None] - r[None, :] + 7, 0, 14)
            ci[:, j, :] = np.clip(ck[:, None] - cq[None, :] + 15, 0, 30)
        key = (valid.tobytes(), ri.tobytes(), ci.tobytes())
        if key not in keys:
            keys[key] = len(classes)
            classes.append((valid, ri, ci))
        cls_of_tile.append(keys[key])
    return cls_of_tile, classes


def na_bias_tables(rpb, classes):
    Lx = rpb.shape[0]
    out = np.empty((Lx, len(classes), 128, 5, 8, 128), np.float32)
    for c, (valid, ri, ci) in enumerate(classes):
        g = rpb[:, :, ri, ci]
        g = np.where(valid[None, None], g, np.float32(NEG))
        out[:, c] = g.transpose(0, 2, 3, 1, 4)
    return out


def gqa_masks():
    kk = np.arange(128)[:, None]
    qq = np.arange(128)[None, :]
    m_prev = np.where(qq <= kk, 0.0, -1.0e6).astype(np.float32)
    m_next = np.where(kk <= qq, 0.0, -1.0e6).astype(np.float32)
    m = np.stack([np.tile(m_prev, (1, 4)), np.tile(m_next, (1, 4))], axis=0)
    return np.ascontiguousarray(m)


def pcol(v, nchunk):
    return np.ascontiguousarray(v.reshape(nchunk, 128).T)


def build(T, L, ncls, cls_of_tile, debug=False, stages=None):
    NB = T // 512
    NT = T // 128
    nc = bass.Bass("TRN2", target_bir_lowering=False)
    S = Sched(nc)

    def din(name, shape, dt=F32):
        return nc.dram_tensor(name, list(shape), dt, kind="ExternalInput").ap()

    def dscr(name, shape, dt):
        return nc.dram_tensor(name, list(shape), dt, kind=("ExternalOutput" if debug else "Internal")).ap()

    xT = din("xT", [D, T])
    w_in = din("w_in", [L, D, IN_W])
    w_uq = din("mla_w_uq", [L, 384, 768])
    w_ukv = din("mla_w_ukv", [L, 256, 1024])
    w_br = [din("w_br_na", [L, 512, D]), din("w_br_mla", [L, 512, D]), din("w_br_gqa", [L, 512, D])]
    w_out = din("w_out", [L, D, D])
    w_up = din("w_up", [L, D, 2 * D_FF])
    w_down = din("w_down", [L, D_FF, D])
    g1 = din("g1", [L, 128, 8])
    g2 = din("g2", [L, 128, 8])
    gf = din("gf", [128, 8])
    bg = din("bg", [L, 128, 24])
    gqa_ = din("gqa", [L, 128, 3])
    gkva = din("gkva", [L, 128, 2])
    sinkr = din("sinkr", [L, 64, 8])
    cw = din("cw", [L, 128, 44 * 3])
    cb = din("cb", [L, 128, 44])
    ropeC = din("ropeC", [128, T])
    ropeS = din("ropeS", [128, T])
    nab = din("nab", [L, ncls, 128, 5 * 8 * 128])
    gmask = din("gmask", [2, 128, 512])

    yT = nc.dram_tensor("yT", [D, T], F32, kind="ExternalOutput").ap()

    XA = dscr("XA", [D, T], F32)
    XB = dscr("XB", [D, T], F32)
    naq = dscr("naq", [8, 64, T], BF16)
    nak = dscr("nak", [8, 64, T], BF16)
    nav = dscr("nav", [T, 512], BF16)
    mqn = dscr("mqn", [4, 128, T], BF16)
    mqp = dscr("mqp", [4, 64, T], BF16)
    mkn = dscr("mkn", [4, 128, T], BF16)
    mkp = dscr("mkp", [64, T], BF16)
    mv = dscr("mv", [4, 128, NT, 128], BF16)
    gq = dscr("gq", [8, 64, T], BF16)
    gk = dscr("gk", [2, 64, T], BF16)
    gv = dscr("gv", [128, NT, 128], BF16)
    yna = dscr("yna", [512, T], BF16)
    ymla = dscr("ymla", [512, T], BF16)
    ygqa = dscr("ygqa", [512, T], BF16)

    def fm(ap):
        return ap.rearrange("(kc p) t -> p kc t", p=128)

    def mm(out, lhsT, rhs, start, stop, reads, writes, skip=False):
        kw = dict(out=out, lhsT=lhsT, rhs=rhs, start=start, stop=stop)
        if skip:
            kw["skip_group_check"] = True
        S.op("pe", "matmul", reads=reads, writes=writes, **kw)

    def act(out, in_, func, reads, writes, **kw):
        S.op("act", "activation", reads=reads, writes=writes, out=out, in_=in_, func=func, **kw)

    def tt(eng, out, in0, in1, op, reads, writes):
        S.op(eng, "tensor_tensor", reads=reads, writes=writes, out=out, in0=in0, in1=in1, op=op)

    def ts(eng, out, in0, s1, s2, op0, op1, reads, writes):
        kw = dict(out=out, in0=in0, scalar1=s1, scalar2=s2, op0=op0)
        if op1 is not None:
            kw["op1"] = op1
        S.op(eng, "tensor_scalar", reads=reads, writes=writes, **kw)

    def stt(out, in0, scalar, in1, op0, op1, reads, writes):
        S.op("dve", "scalar_tensor_tensor", reads=reads, writes=writes, out=out, in0=in0, scalar=scalar,
             in1=in1, op0=op0, op1=op1)

    def cp(eng, out, in_, reads, writes):
        if eng == "act":
            act(out, in_, AF.Copy, reads, writes)
        else:
            S.op(eng, "tensor_copy", reads=reads, writes=writes, out=out, in_=in_)

    def recip(out, in_, reads, writes):
        S.op("dve", "reciprocal", reads=reads, writes=writes, out=out, in_=in_)

    evac_rr = [0]
    SUF = [""]

    def evac(out, in_, reads, writes):
        evac_rr[0] ^= 1
        cp("act" if evac_rr[0] else "dve", out, in_, reads, writes)

    ones_bf = nc.alloc_sbuf_tensor("ones_bf", [128, 128], BF16)
    ones_f = nc.alloc_sbuf_tensor("ones_f", [128, 128], F32)
    S.op("pool", "memset", writes=["ones_bf"], ap=ones_bf[:], constant=1.0)
    S.op("pool", "memset", writes=["ones_f"], ap=ones_f[:], constant=1.0)

    def fm_norm(xs, xres, nch, n, gcol, gres, sq, sqres, ps, psres, tmp, tmpres, rstd, rstdres, xn, xnres, inv_sqrt_dim):
        act(sq[:, 0:nch, 0:n], xs[:, 0:nch, 0:n], AF.Square, [xres], [sqres], scale=inv_sqrt_dim)
        for c in range(nch):
            mm(ps[:, 0:n], ones_bf[:], sq[:, c, 0:n], c == 0, c == nch - 1, ["ones_bf", sqres], [psres])
        act(tmp[:, 0:n], ps[:, 0:n], AF.Ln, [psres], [tmpres], bias=EPS, scale=1.0)
        act(rstd[:, 0:n], tmp[:, 0:n], AF.Exp, [tmpres], [rstdres], scale=-0.5)
        for c in range(nch):
            stt(xn[:, c, 0:n], xs[:, c, 0:n], gcol[:, c:c + 1], rstd[:, 0:n], ALU.mult, ALU.mult,
                [xres, gres, rstdres], [xnres])

    def run_pipeline(steps, depth, defer):
        n = len(steps)
        ring = {}
        pend = []
        for idx in range(n + depth):
            if idx < n:
                ring[idx] = steps[idx][0]()
            if idx >= depth:
                fl = steps[idx - depth][1](ring.pop(idx - depth))
                if fl is not None:
                    for d, f in fl:
                        pend.append((idx + d, f))
                    pend.sort(key=lambda x: x[0])
            while pend and pend[0][0] <= idx:
                pend.pop(0)[1]()
        for _, f in pend:
            f()

    def stage1(l, xsrc):
        with ExitStack() as es:
            def sb(name, shape, dt):
                return es.enter_context(nc.sbuf_tensor(name + SUF[0], list(shape), dt))

            def pst(name, shape):
                return es.enter_context(nc.psum_tensor(name + SUF[0], list(shape), F32))
            Wa = sb("s1_Wa", [128, 8, 3008], BF16)
            Wsw = sb("s1_Wsw", [128, 8, 704], BF16)
            Wq = sb("s1_Wq", [128, 3, 768], BF16)
            Wqp = sb("s1_Wqp", [128, 3, 256], BF16)
            Wqps = sb("s1_Wqps", [128, 3, 256], BF16)
            Wkv = sb("s1_Wkv", [128, 2, 1024], BF16)
            Wv = sb("s1_Wv", [128, 2, 512], BF16)
            g1s = sb("s1_g1", [128, 8], F32)
            gqs = sb("s1_gq", [128, 3], F32)
            gks = sb("s1_gk", [128, 2], F32)
            xs = [sb("s1_x%d" % i, [128, 8, 512], F32) for i in range(2)]
            xn = [sb("s1_xn%d" % i, [128, 8, 512], BF16) for i in range(2)]
            sq = sb("s1_sq", [128, 8, 512], BF16)
            tmp = sb("s1_tmp", [128, 512], F32)
            rstd = sb("s1_rstd", [128, 512], F32)
            Cs = [sb("s1_C%d" % i, [128, 512], F32) for i in range(2)]
            Ss = [sb("s1_S%d" % i, [128, 512], F32) for i in range(2)]
            cq = sb("s1_cq", [128, 3, 512], F32)
            ckv = sb("s1_ckv", [128, 2, 512], F32)
            cqn = sb("s1_cqn", [128, 3, 512], BF16)
            ckvn = sb("s1_ckvn", [128, 2, 512], BF16)
            NST = 6
            stg = [sb("s1_stg%d" % i, [128, 512], BF16) for i in range(NST)]
            t1 = [sb("s1_t1_%d" % i, [128, 512], F32) for i in range(2)]
            t2 = [sb("s1_t2_%d" % i, [128, 512], F32) for i in range(2)]
            NPS = 5
            pss = [pst("s1_ps%d" % i, [128, 512]) for i in range(NPS)]
            ps_ss = pst("s1_pss", [128, 512])
            cnt = {"ps": 0, "stg": 0, "t": 0}

            def nps():
                i = cnt["ps"] % NPS
                cnt["ps"] += 1
                return pss[i], "s1_ps%d" % i

            def nstg():
                i = cnt["stg"] % NST
                cnt["stg"] += 1
                return stg[i], "s1_stg%d" % i

            win_l = w_in[l].rearrange("(kc p) n -> p kc n", p=128)
            WA_GROUPS = [(1536, 2176), (0, 512), (512, 1024), (2176, 2880), (1024, 1536), (2880, 3008)]

            def wa_res(c0):
                for gi, (a0, a1) in enumerate(WA_GROUPS):
                    if a0 <= c0 < a1:
                        return ("s1_Wa", gi)
                raise ValueError(c0)

            def load_wa(gi):
                a0, a1 = WA_GROUPS[gi]
                S.dma("pool", Wa[:, :, a0:a1], win_l[:, :, a0:a1], writes=[("s1_Wa", gi)], stream="w")
            load_wa(0)
            load_wa(1)
            load_wa(2)
            load_wa(3)
            load_wa(4)
            load_wa(5)
            wuq_l = w_uq[l].rearrange("(kc p) n -> p kc n", p=128)
            S.dma("pool", Wq[:], wuq_l, writes=["s1_Wq"], stream="w")
            wukv_l = w_ukv[l].rearrange("(kc p) n -> p kc n", p=128)
            S.dma("pool", Wkv[:], wukv_l, writes=["s1_Wkv"], stream="w")
            for (src0, nh, dst0) in ((2176, 1, 0), (2240, 8, 64), (2752, 2, 576)):
                srcv = Wa[:, :, src0:src0 + nh * 64].rearrange("p k (h two r) -> p k h two r", two=2, r=32)
                dstv = Wsw[:, :, dst0:dst0 + nh * 64].rearrange("p k (h two r) -> p k h two r", two=2, r=32)
                S.op("pool", "tensor_copy", reads=[("s1_Wa", 3)], writes=["s1_Wsw"], out=dstv[:, :, :, 0, :], in_=srcv[:, :, :, 1, :])
                S.op("pool", "tensor_copy", reads=[("s1_Wa", 3)], writes=["s1_Wsw"], out=dstv[:, :, :, 1, :], in_=srcv[:, :, :, 0, :])
            wq4 = Wq[:].rearrange("p k (h c) -> p k h c", c=192)
            S.op("pool", "tensor_copy", reads=["s1_Wq"], writes=["s1_Wqp"],
                 out=Wqp[:].rearrange("p k (h c) -> p k h c", c=64), in_=wq4[:, :, :, 128:192])
            wqps4 = Wqps[:].rearrange("p k (h c) -> p k h c", c=64)
            S.op("pool", "tensor_copy", reads=["s1_Wq"], writes=["s1_Wqps"], out=wqps4[:, :, :, 0:32], in_=wq4[:, :, :, 160:192])
            S.op("pool", "tensor_copy", reads=["s1_Wq"], writes=["s1_Wqps"], out=wqps4[:, :, :, 32:64], in_=wq4[:, :, :, 128:160])
            S.op("pool", "tensor_copy", reads=["s1_Wkv"], writes=["s1_Wv"],
                 out=Wv[:].rearrange("p k (h c) -> p k h c", c=128),
                 in_=Wkv[:].rearrange("p k (h c) -> p k h c", c=256)[:, :, :, 128:256])
            S.dma("sp", g1s[:], g1[l], writes=["s1_g1"], stream="ld")
            S.dma("sp", gqs[:], gqa_[l], writes=["s1_gq"], stream="ld")
            S.dma("sp", gks[:], gkva[l], writes=["s1_gk"], stream="ld")

            xsrc_f = fm(xsrc)
            naq_f = naq.rearrange("h d t -> (h d) t")
            nak_f = nak.rearrange("h d t -> (h d) t")
            mqp_f = mqp.rearrange("h d t -> (h d) t")
            gq_f = gq.rearrange("h d t -> (h d) t")
            gk_f = gk.rearrange("h d t -> (h d) t")

            def load(b):
                sl = b % 2
                t0 = b * 512
                S.dma("sp", xs[sl][:], xsrc_f[:, :, t0:t0 + 512], reads=[("X", id(xsrc), b)], writes=["s1_x%d" % sl], stream="ld")
                S.dma("sp", Cs[sl][:], ropeC[:, t0:t0 + 512], writes=["s1_C%d" % sl], stream="ld")
                S.dma("sp", Ss[sl][:], ropeS[:, t0:t0 + 512], writes=["s1_S%d" % sl], stream="ld")

            def store(dst, src, srcres, dstres):
                S.dma("sp", dst, src, reads=[srcres], writes=[dstres], stream="st")

            def normA(xin, nch, sqb, sqres, xres, scl):
                act(sqb[:, 0:nch, :], xin[:, 0:nch, :], AF.Square, [xres], [sqres], scale=scl)

            def normB(sqb, nch, ps, psres, sqres):
                for c in range(nch):
                    mm(ps[:], ones_bf[:], sqb[:, c, :], c == 0, c == nch - 1, ["ones_bf", sqres], [psres])

            def normC(ps, psres, tm, tmres, rs, rsres, xin, xres, nch, gcol, gres, xo, xores):
                act(tm[:], ps[:], AF.Ln, [psres], [tmres], bias=EPS, scale=1.0)
                act(rs[:], tm[:], AF.Exp, [tmres], [rsres], scale=-0.5)
                for c in range(nch):
                    stt(xo[:, c, :], xin[:, c, :], gcol[:, c:c + 1], rs[:], ALU.mult, ALU.mult,
                        [xres, gres, rsres], [xores])

            sqc = sb("s1_sqc", [128, 5, 512], BF16)
            tmp_c = sb("s1_tmpc", [128, 512], F32)
            rstd_c = sb("s1_rstdc", [128, 512], F32)
            tmp_k = sb("s1_tmpk", [128, 512], F32)
            rstd_k = sb("s1_rstdk", [128, 512], F32)
            ps_ssc = pst("s1_pssc", [128, 512])
            ps_ssk = pst("s1_pssk", [128, 512])

            def xnorm(b):
                sl_ = b % 2
                xr_, xnr_ = "s1_x%d" % sl_, "s1_xn%d" % sl_
                normA(xs[sl_], 8, sq, "s1_sq", xr_, 1.0 / 32.0)
                normB(sq, 8, ps_ss, "s1_pss", "s1_sq")
                normC(ps_ss, "s1_pss", tmp, "s1_tmp", rstd, "s1_rstd", xs[sl_], xr_, 8, g1s, "s1_g1", xn[sl_], xnr_)

            load(0)
            xnorm(0)
            for b in range(NB):
                sl = b % 2
                t0 = b * 512
                if b + 1 < NB:
                    load(b + 1)
                xr, xnr = "s1_x%d" % sl, "s1_xn%d" % sl
                X = xn[sl]

                def proj_fm(W, Wres, c0, M):
                    ps, psr = nps()
                    if Wres == "s1_Wa":
                        Wres = wa_res(c0)
                    for kc in range(8):
                        mm(ps[0:M, :], W[:, kc, c0:c0 + M], X[:, kc, :], kc == 0, kc == 7, [Wres, xnr], [psr])
                    return ps, psr

                def plain_fm(c0, M, dst):
                    ps, psr = proj_fm(Wa, "s1_Wa", c0, M)
                    st_, sr = nstg()
                    evac(st_[0:M, :], ps[0:M, :], [psr], [sr])
                    store(dst, st_[0:M, :], sr, ("scr", id(dst), b))

                def rope_fm(c0, csw, M, dst, dres):
                    pa, par = proj_fm(Wa, "s1_Wa", c0, M)
                    pb, pbr = proj_fm(Wsw, "s1_Wsw", csw, M)
                    i = cnt["t"] % 2
                    cnt["t"] += 1
                    tt("dve", t1[i][0:M, :], pa[0:M, :], Cs[sl][0:M, :], ALU.mult, [par, "s1_C%d" % sl], ["s1_t1_%d" % i])
                    tt("dve", t2[i][0:M, :], pb[0:M, :], Ss[sl][0:M, :], ALU.mult, [pbr, "s1_S%d" % sl], ["s1_t2_%d" % i])
                    st_, sr = nstg()
                    tt("pool", st_[0:M, :], t1[i][0:M, :], t2[i][0:M, :], ALU.add, ["s1_t1_%d" % i, "s1_t2_%d" % i], [sr])
                    store(dst, st_[0:M, :], sr, dres)

                for c in range(3):
                    ps, psr = proj_fm(Wa, "s1_Wa", 1536 + c * 128, 128)
                    cp("dve", cq[:, c, :], ps[:], [psr], ["s1_cq"])
                for c in range(2):
                    ps, psr = proj_fm(Wa, "s1_Wa", 1920 + c * 128, 128)
                    cp("dve", ckv[:, c, :], ps[:], [psr], ["s1_ckv"])
                normA(cq, 3, sqc[:, 0:3, :], "s1_sqc", "s1_cq", 1.0 / np.sqrt(384.0))
                normA(ckv, 2, sqc[:, 3:5, :], "s1_sqk", "s1_ckv", 1.0 / 16.0)
                for c in range(4):
                    plain_fm(c * 128, 128, naq_f[c * 128:(c + 1) * 128, t0:t0 + 512])
                normB(sqc[:, 0:3, :], 3, ps_ssc, "s1_pssc", "s1_sqc")
                normB(sqc[:, 3:5, :], 2, ps_ssk, "s1_pssk", "s1_sqk")
                normC(ps_ssc, "s1_pssc", tmp_c, "s1_tmpc", rstd_c, "s1_rstdc", cq, "s1_cq", 3, gqs, "s1_gq", cqn, "s1_cqn")
                normC(ps_ssk, "s1_pssk", tmp_k, "s1_tmpk", rstd_k, "s1_rstdk", ckv, "s1_ckv", 2, gks, "s1_gk", ckvn, "s1_ckvn")
                for c in range(4):
                    plain_fm(512 + c * 128, 128, nak_f[c * 128:(c + 1) * 128, t0:t0 + 512])
                rope_fm(2176, 0, 64, mkp[:, t0:t0 + 512], ("scr", "mkp", b))
                for c in range(4):
                    rope_fm(2240 + c * 128, 64 + c * 128, 128, gq_f[c * 128:(c + 1) * 128, t0:t0 + 512], ("scr", "gq", b, c))
                rope_fm(2752, 576, 128, gk_f[:, t0:t0 + 512], ("scr", "gk", b))
                for i in range(4):
                    ps, psr = nps()
                    for kc in range(8):
                        mm(ps[:], X[:, kc, i * 128:(i + 1) * 128], Wa[:, kc, 1024:1536], kc == 0, kc == 7, [wa_res(1024), xnr], [psr])
                    st_, sr = nstg()
                    evac(st_[:], ps[:], [psr], [sr])
                    store(nav[t0 + i * 128:t0 + (i + 1) * 128, :], st_[:], sr, ("scr", "nav", b, i))
                ps, psr = nps()
                for i in range(4):
                    for kc in range(8):
                        mm(ps[:, i * 128:(i + 1) * 128], X[:, kc, i * 128:(i + 1) * 128], Wa[:, kc, 2880:3008],
                           kc == 0, kc == 7, [wa_res(2880), xnr], [psr])
                st_, sr = nstg()
                evac(st_[:], ps[:], [psr], [sr])
                store(gv[:, b * 4:(b + 1) * 4, :],
                      st_[:].rearrange("p (i d) -> p i d", d=128), sr, ("scr", "gv", b))
                if b + 1 < NB:
                    xnorm(b + 1)
                for h in range(4):
                    ps, psr = nps()
                    for kc in range(3):
                        mm(ps[:], Wq[:, kc, h * 192:h * 192 + 128], cqn[:, kc, :], kc == 0, kc == 2, ["s1_Wq", "s1_cqn"], [psr])
                    st_, sr = nstg()
                    evac(st_[:], ps[:], [psr], [sr])
                    store(mqn[h, :, t0:t0 + 512], st_[:], sr, ("scr", "mqn", b, h))
                for hp in range(2):
                    pa, par = nps()
                    for kc in range(3):
                        mm(pa[:], Wqp[:, kc, hp * 128:(hp + 1) * 128], cqn[:, kc, :], kc == 0, kc == 2, ["s1_Wqp", "s1_cqn"], [par])
                    pb, pbr = nps()
                    for kc in range(3):
                        mm(pb[:], Wqps[:, kc, hp * 128:(hp + 1) * 128], cqn[:, kc, :], kc == 0, kc == 2, ["s1_Wqps", "s1_cqn"], [pbr])
                    i = cnt["t"] % 2
                    cnt["t"] += 1
                    tt("dve", t1[i][:], pa[:], Cs[sl][:], ALU.mult, [par, "s1_C%d" % sl], ["s1_t1_%d" % i])
                    tt("dve", t2[i][:], pb[:], Ss[sl][:], ALU.mult, [pbr, "s1_S%d" % sl], ["s1_t2_%d" % i])
                    st_, sr = nstg()
                    tt("pool", st_[:], t1[i][:], t2[i][:], ALU.add, ["s1_t1_%d" % i, "s1_t2_%d" % i], [sr])
                    store(mqp_f[hp * 128:(hp + 1) * 128, t0:t0 + 512], st_[:], sr, ("scr", "mqp", b, hp))
                for h in range(4):
                    ps, psr = nps()
                    for kc in range(2):
                        mm(ps[:], Wkv[:, kc, h * 256:h * 256 + 128], ckvn[:, kc, :], kc == 0, kc == 1, ["s1_Wkv", "s1_ckvn"], [psr])
                    st_, sr = nstg()
                    evac(st_[:], ps[:], [psr], [sr])
                    store(mkn[h, :, t0:t0 + 512], st_[:], sr, ("scr", "mkn", b, h))
                for i in range(4):
                    ps, psr = nps()
                    for kc in range(2):
                        mm(ps[:], ckvn[:, kc, i * 128:(i + 1) * 128], Wv[:, kc, :], kc == 0, kc == 1, ["s1_Wv", "s1_ckvn"], [psr])
                    st_, sr = nstg()
                    evac(st_[:], ps[:], [psr], [sr])
                    store(mv[:, :, b * 4 + i, :].rearrange("h p d -> p h d"),
                          st_[:].rearrange("p (h d) -> p h d", d=128), sr, ("scr", "mv", b, i))
        S.barrier()

    def stage_mla(l, prefetch=None):
        scale = 192.0 ** -0.5
        with ExitStack() as es:
            def sb(name, shape, dt):
                return es.enter_context(nc.sbuf_tensor(name + SUF[0], list(shape), dt))

            def pst(name, shape):
                return es.enter_context(nc.psum_tensor(name + SUF[0], list(shape), F32))
            kpe = sb("ml_kpe", [128, T], BF16)
            Ks = [sb("ml_K%d" % i, [128, T], BF16) for i in range(2)]
            Vs = [sb("ml_V%d" % i, [128, NT, 128], BF16) for i in range(2)]
            Qn = [sb("ml_Qn%d" % i, [128, 512], BF16) for i in range(2)]
            Qp = [sb("ml_Qp%d" % i, [128, 512], BF16) for i in range(2)]
            NP = 4
            NPT = 5
            pt = [sb("ml_pt%d" % i, [128, 512], BF16) for i in range(NPT)]
            acc = [sb("ml_acc%d" % i, [128, 512], F32) for i in range(2)]
            tr1 = sb("ml_tr1", [128, 512], BF16)
            tr2 = sb("ml_tr2", [128, 512], BF16)
            tr3 = sb("ml_tr3", [128, 512], BF16)
            lnt = sb("ml_ln", [128, 512], F32)
            rec = sb("ml_rec", [128, 512], F32)
            yst = [sb("ml_y%d" % i, [128, 512], BF16) for i in range(2)]
            ps_s = [pst("ml_pss%d" % i, [128, 512]) for i in range(NP)]
            ps_o = [pst("ml_pso%d" % i, [128, 512]) for i in range(2)]
            ps_sum = pst("ml_psum", [128, 512])

            S.op("pool", "memset", writes=["ml_kpe"], ap=kpe[64:128, :], constant=0.0)
            for i in range(2):
                S.op("pool", "memset", writes=["ml_Qp%d" % i], ap=Qp[i][64:128, :], constant=0.0)
            S.dma("sp", kpe[0:64, :], mkp[:, :], writes=["ml_kpe"], stream="ld")

            def loadKV(h):
                s = h % 2
                S.dma("sp", Ks[s][:], mkn[h, :, :], writes=["ml_K%d" % s], stream="ld")
                S.dma("sp", Vs[s][:], mv[h], writes=["ml_V%d" % s], stream="ld")

            def loadQ(h, qb, it):
                s = it % 2
                S.dma("sp", Qn[s][:], mqn[h, :, qb * 512:(qb + 1) * 512], writes=["ml_Qn%d" % s], stream="ld")
                S.dma("sp", Qp[s][0:64, :], mqp[h, :, qb * 512:(qb + 1) * 512], writes=["ml_Qp%d" % s], stream="ld")

            items = [(h, qb) for h in range(4) for qb in range(NB)]
            loadKV(0)
            loadQ(0, 0, 0)
            gcount = [0]
            steps = []
            flat = []
            for it, (h, qb) in enumerate(items):
                hs = h % 2
                qs = it % 2
                for kb in range(NT):
                    def qk(it=it, h=h, qb=qb, hs=hs, qs=qs, kb=kb):
                        if kb == min(8, NT - 1) and it == 0 and prefetch is not None:
                            prefetch()
                        if kb == 0 and it + 1 < len(items):
                            nh, nqb = items[it + 1]
                            if nh != h:
                                loadKV(nh)
                            loadQ(nh, nqb, it + 1)
                        i = gcount[0] % NP
                        ip = gcount[0] % NPT
                        gcount[0] += 1
                        mm(ps_s[i][:], Ks[hs][:, kb * 128:(kb + 1) * 128], Qn[qs][:], True, False,
                           ["ml_K%d" % hs, "ml_Qn%d" % qs], ["ml_pss%d" % i])
                        mm(ps_s[i][:], kpe[:, kb * 128:(kb + 1) * 128], Qp[qs][:], False, True,
                           ["ml_kpe", "ml_Qp%d" % qs], ["ml_pss%d" % i])
                        act(pt[ip][:], ps_s[i][:], AF.Exp, ["ml_pss%d" % i], ["ml_pt%d" % ip], scale=scale)
                        return ip
                    steps.append((kb, qk))
                    flat.append((it, h, qb, kb))
            ring = {}
            pend = []
            iprev = [0]
            depth = 3
            n = len(flat)
            for idx in range(n + depth):
                if idx < n:
                    it, h, qb, kb = flat[idx]
                    ring[idx] = steps[idx][1]()
                if idx >= depth:
                    j = idx - depth
                    it, h, qb, kb = flat[j]
                    i = ring.pop(j)
                    hs = h % 2
                    a = it % 2
                    mm(ps_o[a][:], Vs[hs][:, kb, :], pt[i][:], kb == 0, kb == NT - 1,
                       ["ml_V%d" % hs, "ml_pt%d" % i], ["ml_pso%d" % a])
                    if kb % 4 == 1:
                        tt("dve", tr1[:], pt[iprev[0]][:], pt[i][:], ALU.add, ["ml_pt%d" % iprev[0], "ml_pt%d" % i], ["ml_tr1"])
                    elif kb % 4 == 3:
                        tt("dve", tr2[:], pt[iprev[0]][:], pt[i][:], ALU.add, ["ml_pt%d" % iprev[0], "ml_pt%d" % i], ["ml_tr2"])
                        if kb == 3:
                            tt("dve", acc[a][:], tr1[:], tr2[:], ALU.add, ["ml_tr1", "ml_tr2"], ["ml_acc%d" % a])
                        else:
                            tt("dve", tr3[:], tr1[:], tr2[:], ALU.add, ["ml_tr1", "ml_tr2"], ["ml_tr3"])
                            tt("dve", acc[a][:], acc[a][:], tr3[:], ALU.add, ["ml_acc%d" % a, "ml_tr3"], ["ml_acc%d" % a])
                    iprev[0] = i
                    if kb == NT - 1:
                        def fin(a=a, h=h, qb=qb):
                            mm(ps_sum[:], ones_f[:], acc[a][:], True, True, ["ones_f", "ml_acc%d" % a], ["ml_psum"])
                            act(lnt[:], ps_sum[:], AF.Ln, ["ml_psum"], ["ml_ln"])
                            act(rec[:], lnt[:], AF.Exp, ["ml_ln"], ["ml_rec"], scale=-1.0)
                            tt("dve", yst[a][:], ps_o[a][:], rec[:], ALU.mult, ["ml_pso%d" % a, "ml_rec"], ["ml_y%d" % a])
                            S.dma("sp", ymla[h * 128:(h + 1) * 128, qb * 512:(qb + 1) * 512], yst[a][:],
                                  reads=["ml_y%d" % a], writes=[("scr", "ymla", h, qb)], stream="st")
                        pend.append((idx + 4, fin))
                while pend and pend[0][0] <= idx:
                    pend.pop(0)[1]()
            for _, f in pend:
                f()
        S.barrier()

    def stage_na(l):
        with ExitStack() as es:
            def sb(name, shape, dt):
                return es.enter_context(nc.sbuf_tensor(name + SUF[0], list(shape), dt))

            def pst(name, shape):
                return es.enter_context(nc.psum_tensor(name + SUF[0], list(shape), F32))
            RING = 8
            kT = [sb("na_k%d" % i, [128, 4, 128], BF16) for i in range(RING)]
            vr = [sb("na_v%d" % i, [128, 512], BF16) for i in range(RING)]
            qT = [sb("na_q%d" % i, [128, 4, 2, 128], BF16) for i in range(2)]
            bias_int = sb("na_bint", [128, 5, 8, 128], F32)
            bias_sp = sb("na_bsp", [128, 5, 8, 128], F32)
            NP = 3
            sbf = [sb("na_sb%d" % i, [128, 4, 128], F32) for i in range(NP)]
            pt = [sb("na_pt%d" % i, [128, 4, 128], BF16) for i in range(NP)]
            rec = [sb("na_rec%d" % i, [128, 4, 128], F32) for i in range(2)]
            lnt = sb("na_ln", [128, 512], F32)
            yst = [sb("na_y%d" % i, [128, 2, 128], BF16) for i in range(2)]
            ps_s = [pst("na_pss%d" % i, [128, 4, 128]) for i in range(NP)]
            ps_o = [pst("na_pso%d" % i, [128, 4, 128]) for i in range(2)]
            ps_sum = [pst("na_psum%d" % i, [128, 512]) for i in range(2)]

            counts = {}
            for c in cls_of_tile:
                counts[c] = counts.get(c, 0) + 1
            c_int = max(counts, key=lambda c: counts[c])
            S.dma("sp", bias_int[:].rearrange("p j h q -> p (j h q)"), nab[l, c_int], writes=["na_bint"], stream="ld")
            nak_v = nak.rearrange("h d t -> (h d) t").rearrange("(hp p) t -> p hp t", p=128)
            naq_v = naq.rearrange("(hp two) d t -> d two hp t", two=2)
            for i in range(2):
                S.op("pool", "memset", writes=["na_q%d" % i], ap=qT[i][:].rearrange("p a b q -> p (a b q)"), constant=0.0)
            yna_v = yna.rearrange("(pr p) t -> p pr t", p=128)
            loaded = [-1]

            def ensure(kt_hi):
                while loaded[0] < kt_hi:
                    kt = loaded[0] + 1
                    s = kt % RING
                    S.dma("sp", kT[s][:], nak_v[:, :, kt * 128:(kt + 1) * 128], writes=["na_k%d" % s], stream="ld")
                    S.dma("sp", vr[s][:], nav[kt * 128:(kt + 1) * 128, :], writes=["na_v%d" % s], stream="ld")
                    loaded[0] = kt

            def loadq(t):
                s = t % 2
                S.dma("sp", qT[s][0:64, :, 0, :], naq_v[:, 0, :, t * 128:(t + 1) * 128], writes=["na_q%d" % s], stream="ld")
                S.dma("sp", qT[s][64:128, :, 1, :], naq_v[:, 1, :, t * 128:(t + 1) * 128], writes=["na_q%d" % s], stream="ld")

            kt0s = [min(max(t - 2, 0), NT - 5) for t in range(NT)]
            ensure(kt0s[0] + 4)
            loadq(0)
            gcount = [0]
            grp = [0]
            steps = []
            for t in range(NT):
                kt0 = kt0s[t]
                cls = cls_of_tile[t]
                for g in range(2):
                    a = (t * 2 + g) % 2
                    for j in range(5):
                        def front(t=t, g=g, j=j, kt0=kt0, cls=cls, holder=None):
                            if g == 0 and j == 0:
                                if t + 1 < NT:
                                    ensure(kt0s[t + 1] + 4)
                                    loadq(t + 1)
                                if cls != c_int:
                                    S.dma("sp", bias_sp[:].rearrange("p j h q -> p (j h q)"), nab[l, cls],
                                          writes=["na_bsp"], stream="ld")
                            bias, bres = (bias_int, "na_bint") if cls == c_int else (bias_sp, "na_bsp")
                            kt = kt0 + j
                            s = kt % RING
                            qs = t % 2
                            i = gcount[0] % NP
                            gcount[0] += 1
                            for h2 in range(2):
                                hp = g * 2 + h2
                                mm(ps_s[i][:, h2 * 2:(h2 + 1) * 2, :], kT[s][:, hp, :], qT[qs][:, hp, :, :], True, True,
                                   ["na_k%d" % s, "na_q%d" % qs], ["na_pss%d" % i])
                            stt(sbf[i][:], ps_s[i][:], 0.125, bias[:, j, g * 4:(g + 1) * 4, :], ALU.mult, ALU.add,
                                ["na_pss%d" % i, bres], ["na_sb%d" % i])
                            act(pt[i][:], sbf[i][:], AF.Exp, ["na_sb%d" % i], ["na_pt%d" % i])
                            return i

                        def back(i, t=t, g=g, j=j, kt0=kt0, a=a):
                            kt = kt0 + j
                            s = kt % RING
                            for hh in range(4):
                                pr = g * 2 + hh // 2
                                mm(ps_o[a][:, hh, :], vr[s][:, pr * 128:(pr + 1) * 128], pt[i][:, hh, :], j == 0 and hh == 0, j == 4,
                                   ["na_v%d" % s, "na_pt%d" % i], ["na_pso%d" % a], skip=True)
                            mm(ps_sum[a][:], ones_bf[:], pt[i][:].rearrange("p h q -> p (h q)"), j == 0, j == 4,
                               ["ones_bf", "na_pt%d" % i], ["na_psum%d" % a])
                            if j == 4:
                                def fin_a():
                                    act(lnt[:], ps_sum[a][:], AF.Ln, ["na_psum%d" % a], ["na_ln"])
                                    act(rec[a][:].rearrange("p h q -> p (h q)"), lnt[:], AF.Exp, ["na_ln"], ["na_rec%d" % a], scale=-1.0)

                                def fin_b():
                                    po4 = ps_o[a][:].rearrange("p (pr two) q -> p pr two q", two=2)
                                    rc4 = rec[a][:].rearrange("p (pr two) q -> p pr two q", two=2)
                                    for par in range(2):
                                        lo, hi = par * 64, par * 64 + 64
                                        tt("dve", yst[a][lo:hi, :, :], po4[lo:hi, :, par, :], rc4[lo:hi, :, par, :], ALU.mult,
                                           ["na_pso%d" % a, "na_rec%d" % a], ["na_y%d" % a])
                                    S.dma("sp", yna_v[:, g * 2:(g + 1) * 2, t * 128:(t + 1) * 128], yst[a][:],
                                          reads=["na_y%d" % a], writes=[("scr", "yna", t, g)], stream="st")
                                return [(1, fin_a), (4, fin_b)]
                            return None
                        steps.append((front, back))
            run_pipeline(steps, 2, 0)
        S.barrier()

    def stage_gqa(l):
        with ExitStack() as es:
            def sb(name, shape, dt):
                return es.enter_context(nc.sbuf_tensor(name + SUF[0], list(shape), dt))

            def pst(name, shape):
                return es.enter_context(nc.psum_tensor(name + SUF[0], list(shape), F32))
            Kall = sb("gq_K", [128, T], BF16)
            Vall = sb("gq_V", [128, NT, 128], BF16)
            qT = [sb("gq_q%d" % i, [128, 2, 4, 128], BF16) for i in range(2)]
            msk = sb("gq_msk", [128, 2, 512], F32)
            sk = sb("gq_sink", [128, 8], F32)
            esk = sb("gq_esink", [128, 8], F32)
            NP = 3
            sbf = [sb("gq_sb%d" % i, [128, 512], F32) for i in range(NP)]
            pt = [sb("gq_pt%d" % i, [128, 512], BF16) for i in range(NP)]
            rec = [sb("gq_rec%d" % i, [128, 512], F32) for i in range(3)]
            lnt = sb("gq_ln", [128, 512], F32)
            den = sb("gq_den", [128, 512], F32)
            yst = [sb("gq_y%d" % i, [128, 4, 128], BF16) for i in range(3)]
            ps_s = [pst("gq_pss%d" % i, [128, 512]) for i in range(NP)]
            ps_o = [pst("gq_pso%d" % i, [128, 512]) for i in range(3)]
            ps_sum = [pst("gq_psum%d" % i, [128, 512]) for i in range(2)]

            S.dma("sp", Kall[:], gk.rearrange("h d t -> (h d) t"), writes=["gq_K"], stream="ld")
            for i in range(2):
                S.op("pool", "memset", writes=["gq_q%d" % i], ap=qT[i][:].rearrange("p a b q -> p (a b q)"), constant=0.0)
            S.dma("sp", Vall[:], gv, writes=["gq_V"], stream="ld")
            S.dma("sp", msk[:], gmask.rearrange("m p n -> p m n"), writes=["gq_msk"], stream="ld")
            S.dma("sp", sk[0:64, :], sinkr[l], writes=["gq_sink"], stream="ld")
            S.dma("sp", sk[64:128, :], sinkr[l], writes=["gq_sink"], stream="ld")
            act(esk[:], sk[:], AF.Exp, ["gq_sink"], ["gq_esink"])
            gq_v = gq.rearrange("h d t -> d h t")
            ygqa_v = ygqa.rearrange("(h d) t -> d h t", d=64)

            def loadq(t):
                s = t % 2
                S.dma("sp", qT[s][0:64, 0, :, :], gq_v[:, 0:4, t * 128:(t + 1) * 128], writes=["gq_q%d" % s], stream="ld")
                S.dma("sp", qT[s][64:128, 1, :, :], gq_v[:, 4:8, t * 128:(t + 1) * 128], writes=["gq_q%d" % s], stream="ld")

            loadq(0)
            gcount = [0]
            steps = []
            for t in range(NT):
                for kvh in range(2):
                    a = (t * 2 + kvh) % 3
                    a2 = (t * 2 + kvh) % 2
                    kts = [kt for kt in (t - 1, t, t + 1) if 0 <= kt < NT]
                    for jj, kt in enumerate(kts):
                        first = jj == 0
                        last = jj == len(kts) - 1

                        def front(t=t, kvh=kvh, kt=kt, first=first):
                            if kvh == 0 and first and t + 1 < NT:
                                loadq(t + 1)
                            qs = t % 2
                            i = gcount[0] % NP
                            gcount[0] += 1
                            mm(ps_s[i][:], Kall[:, kt * 128:(kt + 1) * 128],
                               qT[qs][:, kvh, :, :], True, True,
                               ["gq_K", "gq_q%d" % qs], ["gq_pss%d" % i])
                            if kt != t:
                                m = 0 if kt < t else 1
                                tt("dve", sbf[i][:], ps_s[i][:], msk[:, m, :], ALU.add, ["gq_pss%d" % i, "gq_msk"], ["gq_sb%d" % i])
                                act(pt[i][:], sbf[i][:], AF.Exp, ["gq_sb%d" % i], ["gq_pt%d" % i], scale=0.125)
                            else:
                                act(pt[i][:], ps_s[i][:], AF.Exp, ["gq_pss%d" % i], ["gq_pt%d" % i], scale=0.125)
                            return i

                        def back(i, t=t, kvh=kvh, kt=kt, first=first, last=last, a=a, a2=a2):
                            mm(ps_o[a][:], Vall[:, kt, :], pt[i][:], first, last,
                               ["gq_V", "gq_pt%d" % i], ["gq_pso%d" % a])
                            mm(ps_sum[a2][:], ones_bf[:], pt[i][:], first, last,
                               ["ones_bf", "gq_pt%d" % i], ["gq_psum%d" % a2])
                            if last:
                                lo, hi = kvh * 64, kvh * 64 + 64

                                def fin_a():
                                    for g in range(4):
                                        h = kvh * 4 + g
                                        ts("dve", den[lo:hi, g * 128:(g + 1) * 128], ps_sum[a2][lo:hi, g * 128:(g + 1) * 128],
                                           esk[lo:hi, h:h + 1], None, ALU.add, None, ["gq_psum%d" % a2, "gq_esink"], ["gq_den"])
                                    act(lnt[lo:hi, :], den[lo:hi, :], AF.Ln, ["gq_den"], ["gq_ln"])
                                    act(rec[a][lo:hi, :], lnt[lo:hi, :], AF.Exp, ["gq_ln"], ["gq_rec%d" % a], scale=-1.0)

                                def fin_b():
                                    tt("dve", yst[a][lo:hi, :, :].rearrange("p h q -> p (h q)"), ps_o[a][lo:hi, :], rec[a][lo:hi, :], ALU.mult,
                                       ["gq_pso%d" % a, "gq_rec%d" % a], ["gq_y%d" % a])
                                    S.dma("sp", ygqa_v[:, kvh * 4:(kvh + 1) * 4, t * 128:(t + 1) * 128], yst[a][lo:hi, :, :],
                                          reads=["gq_y%d" % a], writes=[("scr", "ygqa", t, kvh)], stream="st")
                                return [(1, fin_a), (4, fin_b)]
                            return None
                        steps.append((front, back))
            run_pipeline(steps, 2, 0)
        S.barrier()

    def stage3(l, xsrc, xdst, W3):
        with ExitStack() as es:
            def sb(name, shape, dt):
                return es.enter_context(nc.sbuf_tensor(name + SUF[0], list(shape), dt))

            def pst(name, shape):
                return es.enter_context(nc.psum_tensor(name + SUF[0], list(shape), F32))
            Wg, Wb, Wo = W3
            g1s = sb("s3_g1", [128, 8], F32)
            bgs = sb("s3_bg", [128, 24], F32)
            xs = [sb("s3_x%d" % i, [128, 8, 512], F32) for i in range(2)]
            xn2 = [sb("s3_xn%d" % i, [128, 8, 512], BF16) for i in range(2)]
            tmp = sb("s3_tmp", [128, 512], F32)
            rstd = sb("s3_rstd", [128, 512], F32)
            ys = [[sb("s3_y%d_%d" % (i, s), [128, 4, 512], BF16) for s in range(2)] for i in range(3)]
            mg = sb("s3_mg", [128, 8, 512], BF16)
            gt = [sb("s3_gt%d" % i, [128, 512], F32) for i in range(3)]
            pr = [sb("s3_pr%d" % i, [128, 512], F32) for i in range(3)]
            m01 = sb("s3_m01", [128, 512], F32)
            xo = [sb("s3_xo%d" % i, [128, 512], F32) for i in range(3)]
            NPS = 6
            pss = [pst("s3_ps%d" % i, [128, 512]) for i in range(NPS)]
            ps_ss = pst("s3_pss", [128, 512])
            cnt = {"ps": 0, "xo": 0}

            def nps():
                i = cnt["ps"] % NPS
                cnt["ps"] += 1
                return pss[i], "s3_ps%d" % i

            S.dma("sp", g1s[:], g1[l], writes=["s3_g1"], stream="ld")
            S.dma("sp", bgs[:], bg[l], writes=["s3_bg"], stream="ld")
            xsrc_f = fm(xsrc)
            xdst_f = fm(xdst)
            ysrc = [fm(yna), fm(ymla), fm(ygqa)]

            def load(b):
                sl = b % 2
                t0 = b * 512
                S.dma("sp", xs[sl][:], xsrc_f[:, :, t0:t0 + 512], writes=["s3_x%d" % sl], stream="ld")
                for i in range(3):
                    S.dma("sp", ys[i][sl][:], ysrc[i][:, :, t0:t0 + 512], writes=["s3_y%d_%d" % (i, sl)], stream="ld")

            def xnorm3(b):
                sl_ = b % 2
                fm_norm(xs[sl_], "s3_x%d" % sl_, 8, 512, g1s, "s3_g1", xn2[sl_], "s3_xn%d" % sl_, ps_ss, "s3_pss", tmp, "s3_tmp",
                        rstd, "s3_rstd", xn2[sl_], "s3_xn%d" % sl_, 1.0 / 32.0)

            load(0)
            xnorm3(0)
            for b in range(NB):
                sl = b % 2
                t0 = b * 512
                if b + 1 < NB:
                    load(b + 1)
                xr = "s3_x%d" % sl
                xn = xn2[sl]
                xnres = "s3_xn%d" % sl
                for m in range(8):
                    if m == 4 and b + 1 < NB:
                        xnorm3(b + 1)
                    for i in range(3):
                        pg, pgr = nps()
                        for kc in range(8):
                            mm(pg[:], Wg[:, kc, i * 1024 + m * 128:i * 1024 + (m + 1) * 128], xn[:, kc, :],
                               kc == 0, kc == 7, [xnres], [pgr])
                        py, pyr = nps()
                        yr = "s3_y%d_%d" % (i, sl)
                        for kc in range(4):
                            mm(py[:], Wb[:, i, kc, m * 128:(m + 1) * 128], ys[i][sl][:, kc, :], kc == 0, kc == 3,
                               [yr], [pyr])
                        act(gt[i][:], pg[:], AF.Sigmoid, [pgr, "s3_bg"], ["s3_gt%d" % i],
                            bias=bgs[:, i * 8 + m:i * 8 + m + 1], scale=1.0)
                        tt("dve", pr[i][:], gt[i][:], py[:], ALU.mult, ["s3_gt%d" % i, pyr], ["s3_pr%d" % i])
                    tt("pool", m01[:], pr[0][:], pr[1][:], ALU.add, ["s3_pr0", "s3_pr1"], ["s3_m01"])
                    tt("pool", mg[:, m, :], m01[:], pr[2][:], ALU.add, ["s3_m01", "s3_pr2"], [("s3_mg", m)])
                for m in range(8):
                    po, por = nps()
                    for kc in range(8):
                        mm(po[:], Wo[:, kc, m * 128:(m + 1) * 128], mg[:, kc, :], kc == 0, kc == 7,
                           [("s3_mg", kc)], [por])
                    i = cnt["xo"] % 3
                    cnt["xo"] += 1
                    tt("dve", xo[i][:], po[:], xs[sl][:, m, :], ALU.add, [por, xr], ["s3_xo%d" % i])
                    S.dma("sp", xdst_f[:, m, t0:t0 + 512], xo[i][:], reads=["s3_xo%d" % i],
                          writes=[("scr", "x1", b, m)], stream="st")
        S.barrier()

    def stage_ffn(l, xsrc, xdst, final=False):
        NOUT = 382
        FW = 384
        wins = []
        o0 = 0
        while o0 < T:
            o1 = min(o0 + NOUT, T)
            u0 = max(o0 - 1, 0)
            u1 = min(o1 + 1, T)
            wins.append((o0, o1, u0, u1))
            o0 = o1
        with ExitStack() as es:
            def sb(name, shape, dt):
                return es.enter_context(nc.sbuf_tensor(name + SUF[0], list(shape), dt))

            def pst(name, shape):
                return es.enter_context(nc.psum_tensor(name + SUF[0], list(shape), F32))
            Wu = sb("ff_Wu", [128, 8, 2 * D_FF], BF16)
            Wd = sb("ff_Wd", [128, 22, 1024], BF16)
            g2s = sb("ff_g2", [128, 8], F32)
            cws = sb("ff_cw", [128, 44, 3], F32)
            cbs = sb("ff_cb", [128, 44], F32)
            xs = sb("ff_x", [128, 8, FW], F32)
            xn = sb("ff_xn", [128, 8, FW], BF16)
            tmp = sb("ff_tmp", [128, FW], F32)
            rstd = sb("ff_rstd", [128, FW], F32)
            hm = sb("ff_hm", [128, 22, FW], BF16)
            av = [sb("ff_a%d" % i, [128, FW], F32) for i in range(3)]
            gvv = [sb("ff_gv%d" % i, [128, FW], F32) for i in range(3)]
            gg = [sb("ff_gg%d" % i, [128, FW], F32) for i in range(3)]
            xres = [sb("ff_xr%d" % i, [128, FW], F32) for i in range(2)]
            xo = [sb("ff_xo%d" % i, [128, FW], F32) for i in range(2)]
            NPS = 6
            pss = [pst("ff_ps%d" % i, [128, 512]) for i in range(NPS)]
            ps_ss = pst("ff_pss", [128, 512])
            cnt = {"ps": 0, "c": 0, "xo": 0}

            def nps():
                i = cnt["ps"] % NPS
                cnt["ps"] += 1
                return pss[i], "ff_ps%d" % i

            wu_l = w_up[l].rearrange("(kc p) n -> p kc n", p=128)
            for c in range(22):
                for cc in (c, 22 + c):
                    S.dma("pool", Wu[:, :, cc * 128:(cc + 1) * 128], wu_l[:, :, cc * 128:(cc + 1) * 128],
                          writes=[("ff_Wu", cc)], stream="w")
            wd_l = w_down[l].rearrange("(kc p) n -> p kc n", p=128)
            for kc in range(22):
                S.dma("pool", Wd[:, kc, :], wd_l[:, kc, :], writes=[("ff_Wd", kc)], stream="w")
            S.dma("sp", g2s[:], g2[l], writes=["ff_g2"], stream="ld")
            S.dma("sp", cws[:].rearrange("p c j -> p (c j)"), cw[l], writes=["ff_cw"], stream="ld")
            S.dma("sp", cbs[:], cb[l], writes=["ff_cb"], stream="ld")
            xsrc_f = fm(xsrc)
            xdst_f = fm(xdst)
            if final:
                x2b = sb("ff_x2b", [128, 8, FW], F32)
                tmpf = sb("ff_tmpf", [128, FW], F32)
                rstdf = sb("ff_rstdf", [128, FW], F32)
                gfs = sb("ff_gf", [128, 8], F32)
                ps_fn = pst("ff_psfn", [128, 512])
                S.dma("sp", gfs[:], gf, writes=["ff_gf"], stream="ld")
                y_f = fm(yT)

            def load(w):
                o0, o1, u0, u1 = wins[w]
                S.dma("sp", xs[:, :, 0:u1 - u0], xsrc_f[:, :, u0:u1], writes=["ff_x"], stream="ld")

            load(0)
            for w, (o0, o1, u0, u1) in enumerate(wins):
                nu = u1 - u0
                no = o1 - o0
                uoff = o0 - u0
                if w == 0:
                    fm_norm(xs, "ff_x", 8, nu, g2s, "ff_g2", xn, "ff_xn", ps_ss, "ff_pss",
                            tmp, "ff_tmp", rstd, "ff_rstd", xn, "ff_xn", 1.0 / 32.0)
                if w + 1 < len(wins):
                    load(w + 1)
                defer = []
                for c in range(22):
                    k = cnt["c"] % 3
                    cnt["c"] += 1
                    outs = []
                    for (cc, dst, dres) in ((22 + c, gvv[k], "ff_gv%d" % k), (c, av[k], "ff_a%d" % k)):
                        ps, psr = nps()
                        for kc in range(8):
                            mm(ps[:, 0:nu], Wu[:, kc, cc * 128:(cc + 1) * 128], xn[:, kc, 0:nu], kc == 0, kc == 7,
                               [("ff_Wu", cc), "ff_xn"], [psr])
                        act(dst[:, 0:no], ps[:, uoff:uoff + no], AF.Identity, [psr, "ff_cw", "ff_cb"], [dres],
                            bias=cbs[:, cc:cc + 1], scale=cws[:, cc, 1:2])
                        lo = 0 if uoff >= 1 else 1
                        stt(dst[:, lo:no], ps[:, lo + uoff - 1:no + uoff - 1], cws[:, cc, 0:1], dst[:, lo:no],
                            ALU.mult, ALU.add, [psr, "ff_cw", dres], [dres])
                        hi = min(no, nu - uoff - 1)
                        stt(dst[:, 0:hi], ps[:, uoff + 1:uoff + 1 + hi], cws[:, cc, 2:3], dst[:, 0:hi],
                            ALU.mult, ALU.add, [psr, "ff_cw", dres], [dres])
                    def gl(c=c, k=k, no=no):
                        act(gg[k][:, 0:no], gvv[k][:, 0:no], AF.Gelu_apprx_tanh, ["ff_gv%d" % k], ["ff_gg%d" % k])
                        tt("dve" if c >= 20 else "pool", hm[:, c, 0:no], gg[k][:, 0:no], av[k][:, 0:no], ALU.mult,
                           ["ff_gg%d" % k, "ff_a%d" % k], [("ff_hm", c)])
                    while defer:
                        defer.pop(0)()
                    defer.append(gl)
                while defer:
                    defer.pop(0)()
                if w + 1 < len(wins):
                    n_o0, n_o1, n_u0, n_u1 = wins[w + 1]
                    fm_norm(xs, "ff_x", 8, n_u1 - n_u0, g2s, "ff_g2", xn, "ff_xn", ps_ss, "ff_pss",
                            tmp, "ff_tmp", rstd, "ff_rstd", xn, "ff_xn", 1.0 / 32.0)
                for m in range(8):
                    i = cnt["xo"] % 2
                    cnt["xo"] += 1
                    S.dma("sp", xres[i][:, 0:no], xsrc_f[:, m, o0:o1], writes=["ff_xr%d" % i], stream="ld2")
                    pd, pdr = nps()
                    for c in range(22):
                        mm(pd[:, 0:no], Wd[:, c, m * 128:(m + 1) * 128], hm[:, c, 0:no], c == 0, c == 21,
                           [("ff_Wd", c), ("ff_hm", c)], [pdr])
                    if not final:
                        tt("dve", xo[i][:, 0:no], pd[:, 0:no], xres[i][:, 0:no], ALU.add, [pdr, "ff_xr%d" % i], ["ff_xo%d" % i])
                        S.dma("sp", xdst_f[:, m, o0:o1], xo[i][:, 0:no], reads=["ff_xo%d" % i],
                              writes=[("scr", "x2", w, m)], stream="st")
                    else:
                        tt("dve", x2b[:, m, 0:no], pd[:, 0:no], xres[i][:, 0:no], ALU.add, [pdr, "ff_xr%d" % i], ["ff_x2b"])
                if final:
                    hres = [("ff_hm", c) for c in range(8)]
                    act(hm[:, 0:8, 0:no], x2b[:, :, 0:no], AF.Square, ["ff_x2b"], hres, scale=1.0 / 32.0)
                    for c in range(8):
                        mm(ps_fn[:, 0:no], ones_bf[:], hm[:, c, 0:no], c == 0, c == 7, ["ones_bf", ("ff_hm", c)], ["ff_psfn"])
                    act(tmpf[:, 0:no], ps_fn[:, 0:no], AF.Ln, ["ff_psfn"], ["ff_tmpf"], bias=EPS, scale=1.0)
                    act(rstdf[:, 0:no], tmpf[:, 0:no], AF.Exp, ["ff_tmpf"], ["ff_rstdf"], scale=-0.5)
                    for m in range(8):
                        i = cnt["xo"] % 2
                        cnt["xo"] += 1
                        stt(xo[i][:, 0:no], x2b[:, m, 0:no], gfs[:, m:m + 1], rstdf[:, 0:no], ALU.mult, ALU.mult,
                            ["ff_x2b", "ff_gf", "ff_rstdf"], ["ff_xo%d" % i])
                        S.dma("sp", y_f[:, m, o0:o1], xo[i][:, 0:no], reads=["ff_xo%d" % i],
                              writes=[("out", w, m)], stream="st")
        S.barrier()

    def stage_final(xsrc):
        with ExitStack() as es:
            def sb(name, shape, dt):
                return es.enter_context(nc.sbuf_tensor(name + SUF[0], list(shape), dt))
            gfs = sb("fn_g", [128, 8], F32)
            xs = [sb("fn_x%d" % i, [128, 8, 512], F32) for i in range(2)]
            sq = sb("fn_sq", [128, 8, 512], BF16)
            tmp = sb("fn_tmp", [128, 512], F32)
            rstd = sb("fn_rstd", [128, 512], F32)
            xo = [sb("fn_xo%d" % i, [128, 8, 512], F32) for i in range(2)]
            ps_ss = es.enter_context(nc.psum_tensor("fn_pss" + SUF[0], [128, 512], F32))
            S.dma("sp", gfs[:], gf, writes=["fn_g"], stream="ld")
            xsrc_f = fm(xsrc)
            y_f = fm(yT)
            S.dma("sp", xs[0][:], xsrc_f[:, :, 0:512], writes=["fn_x0"], stream="ld")
            for b in range(NB):
                sl = b % 2
                t0 = b * 512
                if b + 1 < NB:
                    S.dma("sp", xs[1 - sl][:], xsrc_f[:, :, t0 + 512:t0 + 1024], writes=["fn_x%d" % (1 - sl)], stream="ld")
                fm_norm(xs[sl], "fn_x%d" % sl, 8, 512, gfs, "fn_g", sq, "fn_sq", ps_ss, "fn_pss", tmp, "fn_tmp",
                        rstd, "fn_rstd", xo[sl], "fn_xo%d" % sl, 1.0 / 32.0)
                S.dma("sp", y_f[:, :, t0:t0 + 512], xo[sl][:], reads=["fn_xo%d" % sl], writes=[("out", b)], stream="st")
        S.barrier()

    xcur = xT
    for l in range(L):
        SUF[0] = "_L%d" % l
        if stages is None or "s1" in stages:
            stage1(l, xcur)
        with ExitStack() as es3:
            Wg = es3.enter_context(nc.sbuf_tensor("s3_Wg" + SUF[0], [128, 8, 3072], BF16))
            Wb = es3.enter_context(nc.sbuf_tensor("s3_Wb" + SUF[0], [128, 3, 4, 1024], BF16))
            Wo = es3.enter_context(nc.sbuf_tensor("s3_Wo" + SUF[0], [128, 8, 1024], BF16))
            def pre3(l=l, Wg=Wg, Wb=Wb, Wo=Wo):
                win_l = w_in[l].rearrange("(kc p) n -> p kc n", p=128)
                for kc in range(8):
                    S.dma("pool", Wg[:, kc, :], win_l[:, kc, 3008:6080], writes=["s3_Wg"], stream="w")
                for i in range(3):
                    wv = w_br[i][l].rearrange("(kc p) n -> p kc n", p=128)
                    for kc in range(4):
                        S.dma("pool", Wb[:, i, kc, :], wv[:, kc, :], writes=["s3_Wb"], stream="w")
                wo_l = w_out[l].rearrange("(kc p) n -> p kc n", p=128)
                for kc in range(8):
                    S.dma("pool", Wo[:, kc, :], wo_l[:, kc, :], writes=["s3_Wo"], stream="w")
            if stages is None or "mla" in stages:
                stage_mla(l, pre3 if (stages is None or "s3" in stages) else None)
            elif stages is not None and "s3" in stages:
                pre3()
            if stages is None or "na" in stages:
                stage_na(l)
            if stages is None or "gqa" in stages:
                stage_gqa(l)
            if stages is None or "s3" in stages:
                stage3(l, xcur, XA, (Wg, Wb, Wo))
        if stages is None or "ffn" in stages:
            stage_ffn(l, XA, XB, final=(l == L - 1))
        xcur = XB
    S.barrier()
    S.emit()
    return nc


def prep_shared(inputs, T, L):
    f = lambda a: np.ascontiguousarray(np.asarray(a, dtype=np.float32))
    cls_of_tile, classes = na_classes(T)
    C, Sn = rope_tables_fm(T)
    sh = {
        "w_in": f(inputs["w_in"]), "mla_w_uq": f(inputs["mla_w_uq"]), "mla_w_ukv": f(inputs["mla_w_ukv"]),
        "w_br_na": f(inputs["w_br_na"]), "w_br_mla": f(inputs["w_br_mla"]), "w_br_gqa": f(inputs["w_br_gqa"]),
        "w_out": f(inputs["w_out"]), "w_up": f(inputs["w_up"]), "w_down": f(inputs["w_down"]),
        "g1": np.stack([pcol(f(inputs["norm1_g"])[l], 8) for l in range(L)]),
        "g2": np.stack([pcol(f(inputs["norm2_g"])[l], 8) for l in range(L)]),
        "gf": pcol(f(inputs["final_g"]), 8),
        "bg": np.stack([pcol(f(inputs["b_gate"])[l], 24) for l in range(L)]),
        "gqa": np.stack([pcol(f(inputs["mla_qa_g"])[l], 3) for l in range(L)]),
        "gkva": np.stack([pcol(f(inputs["mla_kva_g"])[l], 2) for l in range(L)]),
        "sinkr": np.ascontiguousarray(np.broadcast_to(f(inputs["gqa_sink"])[:, None, :], (L, 64, 8))),
        "cw": np.stack([np.ascontiguousarray(
            f(inputs["conv_w"])[l].reshape(3, 44, 128).transpose(2, 1, 0).reshape(128, 132)) for l in range(L)]),
        "cb": np.stack([pcol(f(inputs["conv_b"])[l], 44) for l in range(L)]),
        "ropeC": C, "ropeS": Sn,
        "nab": na_bias_tables(f(inputs["na_rpb"]), classes).reshape(L, len(classes), 128, 5 * 8 * 128),
        "gmask": gqa_masks(),
    }
    return sh, cls_of_tile, len(classes)


_CACHE = {}


def kernel(**inputs):
    x = np.asarray(inputs["x"], dtype=np.float32)
    B, T, _ = x.shape
    L = np.asarray(inputs["w_in"]).shape[0]
    sh, cls_of_tile, ncls = prep_shared(inputs, T, L)
    key = (T, L, ncls)
    if key not in _CACHE:
        _CACHE[key] = build(T, L, ncls, cls_of_tile)
    nc = _CACHE[key]
    in_maps = []
    for b in range(B):
        m = dict(sh)
        m["xT"] = np.ascontiguousarray(x[b].T)
        in_maps.append(m)
    res = run_bass_kernel_spmd(nc, in_maps, core_ids=list(range(B)))
    out = np.empty((B, T, D), np.float32)
    for b in range(B):
        out[b] = np.asarray(res.results[b]["yT"]).T
    return out
```
